# Optimizing a Trainium2 kernel written in Bass

```python
import math
import jax
import jax.numpy as jnp
from jax import lax
import numpy as np

D_MODEL = 1024
BATCH = 8
SEQ = 4096
DEPTH = 2

NORM_EPS = 1e-6
D_FF = 2816
CONV_CH = 1024
CONV_WIDTH = 31
SSM_D_INNER = 1024
SSM_HEAD_DIM = 64
SSM_HEADS = SSM_D_INNER // SSM_HEAD_DIM
SSM_GROUPS = 2
SSM_STATE = 128
SSM_CONV_WIDTH = 4
SSM_CHUNK = 128
SSM_XBC = SSM_D_INNER + 2 * SSM_GROUPS * SSM_STATE
HYB_IN_COLS = 2 * CONV_CH + SSM_D_INNER + SSM_XBC + SSM_HEADS
HYB_MIX_WIDTH = CONV_CH + SSM_D_INNER
ATTN_HEAD_DIM = 64
ATTN_HEADS = 8
DILATED_PATTERNS = ((128, 1), (512, 4), (2048, 16))
ATTN_GROUPS = len(DILATED_PATTERNS)
ATTN_BLOCK = 128
ATTN_QKV_COLS = 3 * ATTN_GROUPS * ATTN_HEADS * ATTN_HEAD_DIM
ATTN_OUT_WIDTH = ATTN_HEADS * ATTN_HEAD_DIM
N_EVEN = (DEPTH + 1) // 2
N_ODD = DEPTH // 2

kernel_name = 'hybrid_conv_ssd_dilated_attn_macaron'


def rms_norm(x, g):
    xf = x.astype(jnp.float32)
    y = xf * lax.rsqrt(jnp.mean(xf * xf, axis=-1, keepdims=True) + NORM_EPS)
    return (y * g.astype(jnp.float32)).astype(x.dtype)


def layer_norm(x, g, b):
    xf = x.astype(jnp.float32)
    xc = xf - jnp.mean(xf, axis=-1, keepdims=True)
    y = xc * lax.rsqrt(jnp.mean(xc * xc, axis=-1, keepdims=True) + NORM_EPS)
    return (y * g.astype(jnp.float32) + b.astype(jnp.float32)).astype(x.dtype)


def swiglu(u, w1, w2):
    gate, up = jnp.split(u @ w1, 2, axis=-1)
    return (jax.nn.silu(gate) * up) @ w2


def causal_depthwise_conv(x, w, b):
    k = w.shape[0]
    y = lax.conv_general_dilated(x, w[:, None, :].astype(x.dtype), window_strides=(1,), padding=[(k - 1, 0)], dimension_numbers=('NWC', 'WIO', 'NWC'), feature_group_count=x.shape[-1])
    return y + b.astype(x.dtype)


def conformer_conv(u, w_dw, b_dw, ln_g, ln_b):
    val, gate = jnp.split(u, 2, axis=-1)
    hid = causal_depthwise_conv(val * jax.nn.sigmoid(gate), w_dw, b_dw)
    return jax.nn.silu(layer_norm(hid, ln_g, ln_b))


def segsum(a):
    t = a.shape[-1]
    strict = jnp.tril(jnp.ones((t, t), dtype=bool), -1)
    incl = jnp.tril(jnp.ones((t, t), dtype=bool), 0)
    ss = jnp.cumsum(jnp.where(strict, a[..., :, None], 0.0), axis=-2)
    return jnp.where(incl, ss, -jnp.inf)


def ssd_chunked(x, a, bm, cm):
    bsz, seq, heads, hd = x.shape
    groups, nst = bm.shape[2], bm.shape[3]
    rep = heads // groups
    nc = seq // SSM_CHUNK
    x = x.reshape(bsz, nc, SSM_CHUNK, groups, rep, hd)
    a = a.reshape(bsz, nc, SSM_CHUNK, groups, rep).transpose(0, 3, 4, 1, 2)
    bm = bm.reshape(bsz, nc, SSM_CHUNK, groups, nst)
    cm = cm.reshape(bsz, nc, SSM_CHUNK, groups, nst)
    a_cum = jnp.cumsum(a, axis=-1)
    decay_in = jnp.exp(segsum(a))
    cb = jnp.einsum('bclgn,bcsgn->bgcls', cm, bm)
    y_diag = jnp.einsum('bgrcls,bcsgrp->bclgrp', cb[:, :, None] * decay_in, x)
    decay_states = jnp.exp(a_cum[..., -1:] - a_cum)
    states = jnp.einsum('bcsgn,bgrcs,bcsgrp->bcgrpn', bm, decay_states, x)
    states = jnp.concatenate([jnp.zeros_like(states[:, :1]), states], axis=1)
    chunk_tot = jnp.pad(a_cum[..., -1], ((0, 0), (0, 0), (0, 0), (1, 0)))
    decay_chunk = jnp.exp(segsum(chunk_tot))
    states = jnp.einsum('bgrzc,bcgrpn->bzgrpn', decay_chunk, states)[:, :-1]
    y_off = jnp.einsum('bclgn,bcgrpn,bgrcl->bclgrp', cm, states, jnp.exp(a_cum))
    return (y_diag + y_off).reshape(bsz, seq, heads, hd)


def mamba2_ssd(z, xbc, dt_raw, conv_w, conv_b, dt_bias, a_log, d_skip, norm_g):
    bsz, seq, _ = z.shape
    xbc = jax.nn.silu(causal_depthwise_conv(xbc, conv_w, conv_b)).astype(jnp.float32)
    xs, bm, cm = jnp.split(xbc, [SSM_D_INNER, SSM_D_INNER + SSM_GROUPS * SSM_STATE], axis=-1)
    xs = xs.reshape(bsz, seq, SSM_HEADS, SSM_HEAD_DIM)
    bm = bm.reshape(bsz, seq, SSM_GROUPS, SSM_STATE)
    cm = cm.reshape(bsz, seq, SSM_GROUPS, SSM_STATE)
    dt = jax.nn.softplus(dt_raw.astype(jnp.float32) + dt_bias.astype(jnp.float32))
    a = -jnp.exp(a_log.astype(jnp.float32))
    y = ssd_chunked(xs * dt[..., None], dt * a, bm, cm) + d_skip.astype(jnp.float32)[:, None] * xs
    y = y.reshape(bsz, seq, SSM_D_INNER) * jax.nn.silu(z.astype(jnp.float32))
    return rms_norm(y, norm_g).astype(z.dtype)


def conv_ssd_mixer(u, w_in, conv_dw_w, conv_dw_b, conv_ln_g, conv_ln_b, ssm_conv_w, ssm_conv_b, ssm_dt_bias, ssm_a_log, ssm_d, ssm_norm_g, w_out):
    proj = u @ w_in
    s0 = 2 * CONV_CH
    s1 = s0 + SSM_D_INNER
    s2 = s1 + SSM_XBC
    conv_in, z, xbc, dt_raw = jnp.split(proj, [s0, s1, s2], axis=-1)
    ya = conformer_conv(conv_in, conv_dw_w, conv_dw_b, conv_ln_g, conv_ln_b)
    yb = mamba2_ssd(z, xbc, dt_raw, ssm_conv_w, ssm_conv_b, ssm_dt_bias, ssm_a_log, ssm_d, ssm_norm_g)
    return jnp.concatenate([ya.astype(u.dtype), yb.astype(u.dtype)], axis=-1) @ w_out


def dilated_window_attention(q, k, v, slopes, window, dilation):
    bsz, seq, heads, dh = q.shape
    n = seq // dilation
    steps = window // dilation
    nb = -(-n // ATTN_BLOCK)
    pad = nb * ATTN_BLOCK - n

    def strided(t):
        t = t.reshape(bsz, n, dilation, heads, dh).transpose(0, 2, 1, 3, 4)
        t = t.reshape(bsz * dilation, n, heads, dh)
        t = jnp.pad(t, ((0, 0), (0, pad), (0, 0), (0, 0)))
        return t.reshape(bsz * dilation, nb, ATTN_BLOCK, heads, dh)

    def with_previous_block(t):
        prev = jnp.pad(t[:, :-1], ((0, 0), (1, 0), (0, 0), (0, 0), (0, 0)))
        return jnp.concatenate([prev, t], axis=2)

    qb = strided(q)
    kb = with_previous_block(strided(k))
    vb = with_previous_block(strided(v))
    scores = jnp.einsum('znqhd,znkhd->znhqk', qb.astype(jnp.float32), kb.astype(jnp.float32)) * (dh ** -0.5)
    qi = jnp.arange(ATTN_BLOCK)[:, None]
    kj = jnp.arange(2 * ATTN_BLOCK)[None, :] - ATTN_BLOCK
    rel = qi - kj
    blk0 = (jnp.arange(nb) * ATTN_BLOCK)[:, None, None]
    valid = (rel >= 0) & (rel <= steps) & (blk0 + kj >= 0)
    bias = -slopes[:, None, None] * (rel * dilation).astype(jnp.float32)
    scores = jnp.where(valid[None, :, None], scores + bias[None, None], -jnp.inf)
    lse = jax.nn.logsumexp(scores, axis=-1)
    probs = jnp.exp(scores - lse[..., None]).astype(v.dtype)
    out = jnp.einsum('znhqk,znkhd->znqhd', probs, vb)
    out = out.reshape(bsz, dilation, nb * ATTN_BLOCK, heads, dh)[:, :, :n]
    out = out.transpose(0, 2, 1, 3, 4).reshape(bsz, seq, heads, dh)
    lse = lse.transpose(0, 1, 3, 2).reshape(bsz, dilation, nb * ATTN_BLOCK, heads)[:, :, :n]
    lse = lse.transpose(0, 2, 1, 3).reshape(bsz, seq, heads)
    return out, lse


def dilated_attention_mixer(u, w_qkv, w_o):
    bsz, seq, _ = u.shape
    qkv = (u @ w_qkv).reshape(bsz, seq, 3, ATTN_GROUPS, ATTN_HEADS, ATTN_HEAD_DIM)
    slopes = jnp.exp2(-8.0 * jnp.arange(1, ATTN_HEADS + 1, dtype=jnp.float32) / ATTN_HEADS)
    outs = []
    lses = []
    for g, (window, dilation) in enumerate(DILATED_PATTERNS):
        o, l = dilated_window_attention(qkv[:, :, 0, g], qkv[:, :, 1, g], qkv[:, :, 2, g], slopes, window, dilation)
        outs.append(o)
        lses.append(l)
    weights = jax.nn.softmax(jnp.stack(lses, axis=0), axis=0)
    merged = jnp.einsum('gblh,gblhd->blhd', weights, jnp.stack(outs, axis=0).astype(jnp.float32))
    return merged.reshape(bsz, seq, ATTN_OUT_WIDTH).astype(u.dtype) @ w_o


def setup_inputs(seed: int = 0) -> dict:
    key = jax.random.key(seed)
    ks = jax.random.split(key, 20)

    def normal(k, shape, scale):
        return jax.random.normal(k, shape, jnp.float32) * scale

    x = normal(ks[0], (BATCH, SEQ, D_MODEL), 1.0)
    norm_g = 1.0 + normal(ks[1], (DEPTH, 6, D_MODEL), 0.05)
    ffn_w1 = normal(ks[2], (DEPTH, 2, D_MODEL, 2 * D_FF), D_MODEL ** -0.5)
    ffn_w2 = normal(ks[3], (DEPTH, 2, D_FF, D_MODEL), D_FF ** -0.5)
    hyb_w_in = normal(ks[4], (N_EVEN, D_MODEL, HYB_IN_COLS), D_MODEL ** -0.5)
    conv_dw_w = normal(ks[5], (N_EVEN, CONV_WIDTH, CONV_CH), CONV_WIDTH ** -0.5)
    conv_dw_b = normal(ks[6], (N_EVEN, CONV_CH), 0.02)
    conv_ln_g = 1.0 + normal(ks[7], (N_EVEN, CONV_CH), 0.05)
    conv_ln_b = normal(ks[8], (N_EVEN, CONV_CH), 0.02)
    ssm_conv_w = normal(ks[9], (N_EVEN, SSM_CONV_WIDTH, SSM_XBC), SSM_CONV_WIDTH ** -0.5)
    ssm_conv_b = normal(ks[10], (N_EVEN, SSM_XBC), 0.02)
    dt0 = jnp.exp(jax.random.uniform(ks[11], (N_EVEN, SSM_HEADS), jnp.float32, math.log(1e-3), math.log(1e-1)))
    ssm_dt_bias = dt0 + jnp.log(-jnp.expm1(-dt0))
    ssm_a_log = jnp.log(jax.random.uniform(ks[12], (N_EVEN, SSM_HEADS), jnp.float32, 1.0, 16.0))
    ssm_d = 1.0 + normal(ks[13], (N_EVEN, SSM_HEADS), 0.1)
    ssm_norm_g = 1.0 + normal(ks[14], (N_EVEN, SSM_D_INNER), 0.05)
    hyb_w_out = normal(ks[15], (N_EVEN, HYB_MIX_WIDTH, D_MODEL), HYB_MIX_WIDTH ** -0.5)
    attn_w_qkv = normal(ks[16], (N_ODD, D_MODEL, ATTN_QKV_COLS), D_MODEL ** -0.5)
    attn_w_o = normal(ks[17], (N_ODD, ATTN_OUT_WIDTH, D_MODEL), ATTN_OUT_WIDTH ** -0.5)
    return {'x': x, 'norm_g': norm_g, 'ffn_w1': ffn_w1, 'ffn_w2': ffn_w2, 'hyb_w_in': hyb_w_in, 'conv_dw_w': conv_dw_w, 'conv_dw_b': conv_dw_b, 'conv_ln_g': conv_ln_g, 'conv_ln_b': conv_ln_b, 'ssm_conv_w': ssm_conv_w, 'ssm_conv_b': ssm_conv_b, 'ssm_dt_bias': ssm_dt_bias, 'ssm_a_log': ssm_a_log, 'ssm_d': ssm_d, 'ssm_norm_g': ssm_norm_g, 'hyb_w_out': hyb_w_out, 'attn_w_qkv': attn_w_qkv, 'attn_w_o': attn_w_o}


def reference(x, norm_g, ffn_w1, ffn_w2, hyb_w_in, conv_dw_w, conv_dw_b, conv_ln_g, conv_ln_b, ssm_conv_w, ssm_conv_b, ssm_dt_bias, ssm_a_log, ssm_d, ssm_norm_g, hyb_w_out, attn_w_qkv, attn_w_o):
    h = x
    for i in range(DEPTH):
        g = norm_g[i]
        h = h + 0.5 * rms_norm(swiglu(rms_norm(h, g[0]), ffn_w1[i, 0], ffn_w2[i, 0]), g[1])
        u = rms_norm(h, g[2])
        j = i // 2
        if i % 2 == 0:
            m = conv_ssd_mixer(u, hyb_w_in[j], conv_dw_w[j], conv_dw_b[j], conv_ln_g[j], conv_ln_b[j], ssm_conv_w[j], ssm_conv_b[j], ssm_dt_bias[j], ssm_a_log[j], ssm_d[j], ssm_norm_g[j], hyb_w_out[j])
        else:
            m = dilated_attention_mixer(u, attn_w_qkv[j], attn_w_o[j])
        h = h + rms_norm(m, g[3])
        h = h + 0.5 * rms_norm(swiglu(rms_norm(h, g[4]), ffn_w1[i, 1], ffn_w2[i, 1]), g[5])
    return h
```

```python
import numpy as np
import concourse.bass as bass
import concourse.mybir as mybir
from concourse.bass_utils import run_bass_kernel_spmd
from contextlib import ExitStack

F32 = mybir.dt.float32
BF16 = mybir.dt.bfloat16
ALU = mybir.AluOpType
AF = mybir.ActivationFunctionType
AX = mybir.AxisListType

L = 4096
D = 1024
DFF = 2816
NT = L // 128
EPS = 1e-6
SEM_CH = 4000
N_DMA_SEM = 12


class Buf:
    __slots__ = ("name", "w", "r")

    def __init__(self, name):
        self.name = name
        self.w = []
        self.r = []


class Op:
    __slots__ = ("eng", "fn", "deps", "key", "sigval", "need_sig", "dma", "prev_dma", "barrier", "cost", "seq", "pos",
                 "succ", "prio", "fin", "ndep", "line")

    def __init__(self, eng, fn):
        self.eng = eng
        self.fn = fn
        self.deps = []
        self.key = eng
        self.sigval = 0
        self.need_sig = False
        self.dma = False
        self.prev_dma = None
        self.barrier = False
        self.cost = getattr(fn, "cost", 0.3) if fn is not None else 0.0
        self.seq = 0
        self.pos = 0


import heapq

SCHEDULE = True
SYNC_LAT = 0.25
DMA_ISSUE = {"sp": 0.15, "act": 0.15, "pool": 1.0, "pe": 0.15, "dve": 0.15}


class Prog:
    ENGS = ["pe", "act", "dve", "pool", "sp"]

    def __init__(self, nc):
        self.nc = nc
        self.ops = {e: [] for e in self.ENGS}
        self.all_bufs = []
        self.nseq = 0

    def buf(self, name):
        b = Buf(name)
        self.all_bufs.append(b)
        return b

    def bufs(self, name, n):
        return [self.buf("%s%d" % (name, i)) for i in range(n)]

    def _track(self, o, reads, writes):
        deps = {}
        for b in reads:
            for d in b.w:
                deps[id(d)] = d
        for b in writes:
            for d in b.w:
                deps[id(d)] = d
            for d in b.r:
                deps[id(d)] = d
        deps.pop(id(o), None)
        o.deps = list(deps.values())
        for b in reads:
            b.r.append(o)
        for b in writes:
            b.w = [o]
            b.r = []
        o.seq = self.nseq
        self.nseq += 1
        if getattr(self, "verbose", False):
            import sys as _s
            f = _s._getframe(2)
            o.line = f.f_lineno

    def op(self, eng, fn, reads=(), writes=()):
        o = Op(eng, fn)
        if eng == "pool":
            o.cost *= 5.0
        self._track(o, reads, writes)
        self.ops[eng].append(o)
        return o

    def dma(self, eng, fn, reads=(), writes=()):
        o = Op(eng, fn)
        o.dma = True
        self._track(o, reads, writes)
        self.ops[eng].append(o)
        return o

    def barrier(self):
        for e in self.ENGS:
            o = Op(e, None)
            o.barrier = True
            o.seq = self.nseq
            self.ops[e].append(o)
        self.nseq += 1
        for b in self.all_bufs:
            b.w = []
            b.r = []

    def _schedule(self, region):
        allops = sorted((o for e in self.ENGS for o in region[e]), key=lambda o: o.seq)
        inreg = set(id(o) for o in allops)
        for o in allops:
            o.succ = []
            o.fin = None
        for o in allops:
            o.ndep = 0
            for d in o.deps:
                if id(d) in inreg:
                    d.succ.append(o)
                    o.ndep += 1
        for o in reversed(allops):
            lat = o.cost if not o.dma else (2.0 + o.cost)
            o.prio = lat + max([s.prio for s in o.succ], default=0.0)
        for o in allops:
            if o.eng in ("sp", "pool"):
                o.prio = 1e9 - o.seq
        pipe = [0.0]
        free = {e: 0.0 for e in self.ENGS}
        future = {e: [] for e in self.ENGS}
        avail = {e: [] for e in self.ENGS}
        out = {e: [] for e in self.ENGS}

        def push(o):
            rt = 0.0
            for d in o.deps:
                if id(d) in inreg:
                    t = d.fin + (SYNC_LAT if (d.eng != o.eng or d.dma) else 0.05)
                    if t > rt:
                        rt = t
            heapq.heappush(future[o.eng], (rt, o.seq, o))

        for o in allops:
            if o.ndep == 0:
                push(o)
        n = len(allops)
        done = 0
        while done < n:
            best = None
            for e in self.ENGS:
                fu, av = future[e], avail[e]
                while fu and fu[0][0] <= free[e]:
                    rt, sq, o = heapq.heappop(fu)
                    heapq.heappush(av, (-o.prio, sq, o))
                if av:
                    cand = (free[e], 0, e)
                elif fu:
                    cand = (fu[0][0], 1, e)
                else:
                    continue
                if best is None or cand < best:
                    best = cand
            start, kind, e = best
            if kind == 0:
                _, _, o = heapq.heappop(avail[e])
            else:
                _, _, o = heapq.heappop(future[e])
            if o.dma:
                free[e] = start + DMA_ISSUE[e]
                t0 = max(start + 1.0, pipe[0])
                pipe[0] = t0 + o.cost
                o.fin = pipe[0] + 1.0
            else:
                free[e] = start + o.cost
                o.fin = free[e]
            out[e].append(o)
            done += 1
            for s in o.succ:
                s.ndep -= 1
                if s.ndep == 0:
                    push(s)
        self.sim_time = getattr(self, "sim_time", 0.0) + max(free.values())
        if getattr(self, "verbose", False) and n > 700 and hasattr(allops[0], "line"):
            cur = max(allops, key=lambda o: o.prio if o.eng not in ("sp", "pool") else -1)
            agg = {}
            while cur is not None:
                k = (cur.line, cur.eng)
                agg[k] = agg.get(k, 0.0) + (cur.cost if not cur.dma else 2.0 + cur.cost)
                nxt = [s for s in cur.succ if s.eng not in ("sp", "pool")] or cur.succ
                cur = max(nxt, key=lambda s: s.prio) if nxt else None
            print("  critical path by line:", sorted(((int(v), k) for k, v in agg.items()), reverse=True)[:14])
        if getattr(self, "verbose", False) and n > 200:
            cp = max((o.prio for o in allops if o.eng not in ("sp", "pool")), default=0.0)
            busy = {e: sum(o.cost for o in region[e] if not o.dma) for e in self.ENGS}
            print("region n=%d makespan=%.0f critpath=%.0f busy=%s" % (n, max(free.values()), cp, {k: int(v) for k, v in busy.items()}))
        return out

    def emit(self, stack):
        nc = self.nc
        nbar = sum(1 for o in self.ops["pe"] if o.barrier)
        regions = []
        cur = {e: [] for e in self.ENGS}
        idx = {e: 0 for e in self.ENGS}
        final = {e: [] for e in self.ENGS}
        for r in range(nbar + 1):
            reg = {e: [] for e in self.ENGS}
            bar = {}
            for e in self.ENGS:
                lst = self.ops[e]
                i = idx[e]
                while i < len(lst) and not lst[i].barrier:
                    reg[e].append(lst[i])
                    i += 1
                if i < len(lst):
                    bar[e] = lst[i]
                    i += 1
                idx[e] = i
            if SCHEDULE and sum(len(v) for v in reg.values()) > 1:
                reg = self._schedule(reg)
            for e in self.ENGS:
                final[e].extend(reg[e])
                if e in bar:
                    final[e].append(bar[e])
        self.ops = final
        dma_last = {}
        for e in self.ENGS:
            rr = 0
            for p, o in enumerate(self.ops[e]):
                o.pos = p
                if o.dma:
                    o.key = ("dma", e, rr)
                    rr = (rr + 1) % N_DMA_SEM
                    o.prev_dma = dma_last.get(o.key)
                    o.sigval = (o.prev_dma.sigval if o.prev_dma is not None else 0) + 16
                    dma_last[o.key] = o
        def compute_needs(o):
            need = {}
            for d in o.deps:
                if d.dma:
                    cur = need.get(d.key)
                    if cur is None or d.sigval > cur.sigval:
                        need[d.key] = d
                else:
                    if d.eng == "pe" and o.eng == "pe" and not o.dma:
                        continue
                    cur = need.get(d.eng)
                    if cur is None or d.pos > cur.pos:
                        need[d.eng] = d
            return need

        last_compute = {e: None for e in self.ENGS}
        bar_deps = {}
        ptr = {e: 0 for e in self.ENGS}
        dma_seen = {}
        for r in range(nbar):
            for e in self.ENGS:
                lst = self.ops[e]
                i = ptr[e]
                while not lst[i].barrier:
                    o = lst[i]
                    if o.dma:
                        dma_seen[o.key] = o
                    elif o.fn is not None:
                        last_compute[e] = o
                    i += 1
                bar_deps.setdefault(r, {})[e] = lst[i]
                ptr[e] = i + 1
            deps = [o for o in last_compute.values() if o is not None] + list(dma_seen.values())
            for e in self.ENGS:
                bar_deps[r][e].deps = list(deps)
        for e in self.ENGS:
            for o in self.ops[e]:
                if o.barrier:
                    for d in o.deps:
                        if not d.dma:
                            d.need_sig = True
                else:
                    for d in compute_needs(o).values():
                        if not d.dma:
                            d.need_sig = True
        cnt = {}
        for e in self.ENGS:
            c = 0
            for o in self.ops[e]:
                if o.dma or o.barrier:
                    continue
                if o.need_sig:
                    c += 1
                    o.sigval = c
            cnt[e] = c
        esems = {}
        for e in self.ENGS:
            n = max(1, (cnt[e] + SEM_CH - 1) // SEM_CH)
            esems[e] = [stack.enter_context(nc.semaphore("s_%s_%d" % (e, i))) for i in range(n)]
        dsems = {}
        for key in dma_last:
            dsems[key] = stack.enter_context(nc.semaphore("d_%s_%d" % (key[1], key[2])))
        self.n_sems = sum(len(v) for v in esems.values()) + len(dsems)
        self.n_inst = {e: len(self.ops[e]) for e in self.ENGS}
        self.n_sig = cnt

        def run_engine(ename, eng):
            seen = {}

            def wait(key, val):
                if val <= 0 or seen.get(key, 0) >= val:
                    return
                seen[key] = val
                if isinstance(key, tuple):
                    eng.wait_ge(dsems[key], val)
                else:
                    ch = (val - 1) // SEM_CH
                    eng.wait_ge(esems[key][ch], (val - 1) % SEM_CH + 1)

            for o in self.ops[ename]:
                if o.barrier:
                    for d in o.deps:
                        wait(d.key if d.dma else d.eng, d.sigval)
                    continue
                for k, d in compute_needs(o).items():
                    wait(k, d.sigval)
                if o.dma and o.prev_dma is not None:
                    wait(o.key, o.prev_dma.sigval)
                ins = o.fn(eng)
                if o.dma:
                    ins.then_inc(dsems[o.key], 16)
                elif o.need_sig:
                    ch = (o.sigval - 1) // SEM_CH
                    ins.then_inc(esems[ename][ch], 1)
            for key, o in dma_last.items():
                if key[1] == ename:
                    wait(key, o.sigval)

        block = stack.enter_context(nc.Block())

        @block.tensor
        def _(eng):
            run_engine("pe", eng)

        @block.scalar
        def _(eng):
            run_engine("act", eng)

        @block.vector
        def _(eng):
            run_engine("dve", eng)

        @block.gpsimd
        def _(eng):
            run_engine("pool", eng)

        @block.sync
        def _(eng):
            run_engine("sp", eng)


class Ctx:
    pass


def sb(stack, nc, name, shape, dt):
    return stack.enter_context(nc.sbuf_tensor(name, shape, dt))


def ps(stack, nc, name, shape, dt):
    return stack.enter_context(nc.psum_tensor(name, shape, dt))


def _fsz(ap):
    n = 1
    for s in ap.shape[1:]:
        n *= s
    return n


def _wc(f, cost):
    f.cost = cost
    return f


def I_mm(out, lhsT, rhs, start, stop):
    n = _fsz(rhs)
    c = 0.004 + max(n, 64) / 2400.0
    if rhs.dtype == F32:
        c *= 4
    return _wc(lambda e: e.matmul(out, lhsT=lhsT, rhs=rhs, start=start, stop=stop), c)


def I_tr(out, in_, ident):
    return _wc(lambda e: e.transpose(out=out, in_=in_, identity=ident), 0.1)


def I_act(out, in_, func, **kw):
    c = 0.25 + _fsz(out) / 1200.0 + (0.1 if "accum_out" in kw else 0.0)
    return _wc(lambda e: e.activation(out=out, in_=in_, func=func, **kw), c)


def I_tt(out, in0, in1, op):
    return _wc(lambda e: e.tensor_tensor(out=out, in0=in0, in1=in1, op=op), 0.12 + _fsz(out) / 960.0)


def I_ts(out, in0, s1, s2, op0, op1=None):
    c = 0.12 + _fsz(out) / 960.0
    if op1 is None:
        return _wc(lambda e: e.tensor_scalar(out=out, in0=in0, scalar1=s1, scalar2=None, op0=op0), c)
    return _wc(lambda e: e.tensor_scalar(out=out, in0=in0, scalar1=s1, scalar2=s2, op0=op0, op1=op1), c)


def I_stt(out, in0, scalar, in1, op0, op1):
    return _wc(lambda e: e.scalar_tensor_tensor(out=out, in0=in0, scalar=scalar, in1=in1, op0=op0, op1=op1),
               0.12 + _fsz(out) / 960.0)


def I_copy(out, in_):
    return _wc(lambda e: e.tensor_copy(out=out, in_=in_), 0.12 + _fsz(out) / 960.0)


def I_acopy(out, in_):
    return _wc(lambda e: e.copy(out=out, in_=in_), 0.25 + _fsz(out) / 1200.0)


def I_dma(out, in_):
    nbytes = 128 * _fsz(out) * (4 if (in_.dtype == F32 or out.dtype == F32) else 2)
    return _wc(lambda e: e.dma_start(out=out, in_=in_), nbytes / 200e3)


def I_memset(ap, v):
    return _wc(lambda e: e.memset(ap, v), 0.2 + _fsz(ap) / 1000.0)


def emit_consts(P, C, stack):
    nc = P.nc
    C.ident = sb(stack, nc, "ident", [128, 128], BF16)
    C.identf = sb(stack, nc, "identf", [128, 128], F32)
    C.negh = sb(stack, nc, "negh", [128, 1], F32)
    C.b_ident = P.buf("ident")
    P.op("pool", I_memset(C.identf[:], 1.0), writes=[C.b_ident])
    P.op("pool", lambda e: e.affine_select(out=C.identf[:], in_=C.identf[:], pattern=[[-1, 128]],
                                           compare_op=ALU.is_equal, fill=0.0, base=0, channel_multiplier=1),
         reads=[C.b_ident], writes=[C.b_ident])
    P.op("pool", I_copy(C.ident[:], C.identf[:]), reads=[C.b_ident], writes=[C.b_ident])
    P.op("pool", I_memset(C.negh[:], -0.5), writes=[C.b_ident])


def rstd_pool(P, C, out_ap, in_ap, scale, rbufs, wbufs, post=1.0):
    k = 1.0 / (post * post)
    P.op("pool", I_ts(out_ap, in_ap, scale * k, EPS * k, ALU.mult, ALU.add), reads=rbufs, writes=wbufs)
    P.op("pool", I_tt(out_ap, out_ap, C.negh[:, 0:1], ALU.pow), reads=list(wbufs) + [C.b_ident], writes=wbufs)


def ffn_phase(P, C, h_src, h_dst, w1, w2, g_pre, g_post, tag, nblk=4, dbg=None):
    nc = P.nc
    TB = 1024
    TPB = TB // 128
    NJ = DFF // 128
    NJ2 = NJ // 2
    with ExitStack() as st:
        gpre = sb(st, nc, tag + "gpre", [128, D], F32)
        gpost = sb(st, nc, tag + "gpost", [128, D], F32)
        hbuf = [sb(st, nc, tag + "hb%d" % i, [128, D], F32) for i in range(2)]
        hres = [sb(st, nc, tag + "hr%d" % i, [128, D], F32) for i in range(2)]
        ubf = [sb(st, nc, tag + "ub%d" % i, [128, D], BF16) for i in range(2)]
        uT = [sb(st, nc, tag + "uT%d" % i, [128, 8, TB], BF16) for i in range(2)]
        w1s = [sb(st, nc, tag + "w1s%d" % i, [128, 8, 2, 256], BF16) for i in range(3)]
        w2s = sb(st, nc, tag + "w2s", [128, NJ, D], BF16)
        actT = sb(st, nc, tag + "actT", [128, NJ, TB], BF16)
        sg = [sb(st, nc, tag + "sg%d" % i, [128, 512], F32) for i in range(2)]
        mb = [sb(st, nc, tag + "mb%d" % i, [128, D], F32) for i in range(2)]
        junk = sb(st, nc, tag + "junk", [128, D], BF16)
        stat = sb(st, nc, tag + "stat", [128, 24], F32)
        tp = ps(st, nc, tag + "tp", [128, 8, 128], BF16)
        gps = [ps(st, nc, tag + "gps%d" % i, [128, 512], F32) for i in range(2)]
        ups = [ps(st, nc, tag + "ups%d" % i, [128, 512], F32) for i in range(2)]
        ops_ = [ps(st, nc, tag + "ops%d" % i, [128, 512], F32) for i in range(2)]

        b_g = P.buf("g")
        b_hbuf = P.bufs("hbuf", 2)
        b_hres = P.bufs("hres", 2)
        b_ubf = P.bufs("ubf", 2)
        b_uT = P.bufs("uT", 4)
        b_w1 = P.bufs("w1s", 3)
        b_w2 = P.bufs("w2s", NJ)
        b_act = P.bufs("actT", NJ * 2)
        b_sg = P.bufs("sg", 2)
        b_mb = P.bufs("mb", 2)
        b_junk = P.buf("junk")
        b_st1 = P.bufs("st1", 2)
        b_st2 = P.bufs("st2", 2)
        b_tp = P.buf("tp")
        b_gps = P.bufs("gps", 2)
        b_ups = P.bufs("ups", 2)
        b_ops = P.bufs("ops", 2)

        P.dma("sp", I_dma(gpre[:], g_pre.partition_broadcast(128)), writes=[b_g])
        P.dma("sp", I_dma(gpost[:], g_post.partition_broadcast(128)), writes=[b_g])

        cnt = {"prep": 0, "w1": 0, "gu": 0, "ep": 0}
        w1v = w1.rearrange("(kc p) (two f) -> p kc two f", p=128, two=2)

        def prep_load(b, t):
            i = cnt["prep"] % 2
            cnt["prep"] += 1
            tok0 = b * TB + t * 128
            P.dma("sp", I_dma(hbuf[i][:], h_src[tok0:tok0 + 128, :]), writes=[b_hbuf[i]])
            return i

        def prep_norm(b, t, i=None):
            if i is None:
                i = prep_load(b, t)
            P.op("act", I_act(junk[:], hbuf[i][:], AF.Square, accum_out=stat[:, i:i + 1]),
                 reads=[b_hbuf[i]], writes=[b_junk, b_st1[i]])
            rstd_pool(P, C, stat[:, 2 + i:3 + i], stat[:, i:i + 1], 1.0 / D, [b_st1[i]], [b_st1[i]])
            P.op("dve", I_stt(ubf[i][:], hbuf[i][:], stat[:, 2 + i:3 + i], gpre[:], ALU.mult, ALU.mult),
                 reads=[b_hbuf[i], b_st1[i], b_g], writes=[b_ubf[i]])
            return i

        def prep_tr(b, t, i):
            s = b % 2
            for kc in range(8):
                P.op("pe", I_tr(tp[:, kc, :], ubf[i][:, kc * 128:(kc + 1) * 128], C.ident[:]),
                     reads=[b_ubf[i], C.b_ident], writes=[b_tp])
            P.op("act", I_acopy(uT[s][:, :, t * 128:(t + 1) * 128], tp[:, :, :]),
                 reads=[b_tp], writes=[b_uT[s * 2 + t // 4]])

        def load_w1(jp):
            i = cnt["w1"] % 3
            cnt["w1"] += 1
            for two in range(2):
                P.dma("pool", I_dma(w1s[i][:, :, two, :], w1v[:, :, two, jp * 256:(jp + 1) * 256]), writes=[b_w1[i]])
            return i

        b_pace = P.bufs("pace", NJ)

        def load_w2(j):
            P.dma("pool", I_dma(w2s[:, j, :], w2[j * 128:(j + 1) * 128, :]), reads=([b_pace[j - 3]] if j >= 3 else []),
                  writes=[b_w2[j]])

        def first_j(b, j, wi):
            s = b % 2
            jo = (j % 2) * 128
            for half in range(2):
                q = cnt["gu"] % 2
                cnt["gu"] += 1
                for kc in range(8):
                    P.op("pe", I_mm(gps[q][:], w1s[wi][:, kc, 0, jo:jo + 128],
                                    uT[s][:, kc, half * 512:(half + 1) * 512], kc == 0, kc == 7),
                         reads=[b_w1[wi], b_uT[s * 2 + half]], writes=[b_gps[q]])
                for kc in range(8):
                    P.op("pe", I_mm(ups[q][:], w1s[wi][:, kc, 1, jo:jo + 128],
                                    uT[s][:, kc, half * 512:(half + 1) * 512], kc == 0, kc == 7),
                         reads=[b_w1[wi], b_uT[s * 2 + half]], writes=[b_ups[q]] + ([b_pace[j]] if (b == 0 and half == 1 and kc == 7) else []))
                P.op("act", I_act(sg[q][:], gps[q][:], AF.Silu), reads=[b_gps[q]], writes=[b_sg[q]])
                P.op("dve", I_tt(actT[:, j, half * 512:(half + 1) * 512], ups[q][:], sg[q][:], ALU.mult),
                     reads=[b_ups[q], b_sg[q]], writes=[b_act[j * 2 + half]])

        def second_tile(b, t):
            i = cnt["ep"] % 2
            cnt["ep"] += 1
            tok0 = b * TB + t * 128
            half = t // 4
            P.dma("sp", I_dma(hres[i][:], h_src[tok0:tok0 + 128, :]), writes=[b_hres[i]])
            for dh in range(2):
                for j in range(NJ):
                    P.op("pe", I_mm(ops_[dh][:], actT[:, j, t * 128:(t + 1) * 128],
                                    w2s[:, j, dh * 512:(dh + 1) * 512], j == 0, j == NJ - 1),
                         reads=[b_act[j * 2 + half], b_w2[j]], writes=[b_ops[dh]])
                c0 = 4 + 2 * i + dh
                P.op("dve", I_copy(mb[i][:, dh * 512:(dh + 1) * 512], ops_[dh][:]),
                     reads=[b_ops[dh]], writes=[b_mb[i]])
                P.op("act", I_act(junk[:, 0:512], mb[i][:, dh * 512:(dh + 1) * 512], AF.Square,
                                  accum_out=stat[:, c0:c0 + 1]),
                     reads=[b_mb[i]], writes=[b_junk, b_st2[i]])
            if dbg == "mm2":
                return
            P.op("pool", I_tt(stat[:, 16 + i:17 + i], stat[:, 4 + 2 * i:5 + 2 * i], stat[:, 5 + 2 * i:6 + 2 * i], ALU.add),
                 reads=[b_st2[i]], writes=[b_st2[i]])
            rstd_pool(P, C, stat[:, 16 + i:17 + i], stat[:, 16 + i:17 + i], 1.0 / D, [b_st2[i]], [b_st2[i]], post=0.5)
            if dbg == "ep1":
                return
            P.op("dve", I_stt(mb[i][:], mb[i][:], stat[:, 16 + i:17 + i], gpost[:], ALU.mult, ALU.mult),
                 reads=[b_mb[i], b_st2[i], b_g], writes=[b_mb[i]])
            if dbg == "ep2":
                return
            P.op("dve", I_tt(mb[i][:], mb[i][:], hres[i][:], ALU.add),
                 reads=[b_mb[i], b_hres[i]], writes=[b_mb[i]])
            if dbg == "ep3":
                return
            P.dma("sp", I_dma(h_dst[tok0:tok0 + 128, :], mb[i][:]), reads=[b_mb[i]])

        slot0 = {}
        sw_pipeline(TPB, [lambda t: slot0.__setitem__(t, prep_load(0, t)),
                          lambda t: prep_norm(0, t, slot0[t]),
                          lambda t: prep_tr(0, t, slot0[t])])
        if dbg == "prep":
            P.barrier()
            return
        w2_loaded = False
        for b in range(nblk):
            pend = []
            wq = [load_w1(0)]
            for j in range(NJ):
                if j % 2 == 0 and j // 2 + 1 < NJ2:
                    wq.append(load_w1(j // 2 + 1))
                if not w2_loaded:
                    load_w2(j)
                if dbg == "w":
                    continue
                first_j(b, j, wq[j // 2])
                if b + 1 < nblk:
                    if j % 2 == 0 and j // 2 < TPB:
                        pend.append((j // 2, prep_norm(b + 1, j // 2)))
                    if j % 2 == 1 and pend:
                        t, i = pend.pop(0)
                        prep_tr(b + 1, t, i)
            w2_loaded = True
            if dbg in ("w", "first"):
                P.barrier()
                return
            for t in range(TPB):
                second_tile(b, t)
        P.barrier()


def ffn_chain(P, C, jobs, tag, nblk=4):
    nc = P.nc
    TB = 1024
    TPB = TB // 128
    NJ = DFF // 128
    NJ2 = NJ // 2
    nj = len(jobs)
    with ExitStack() as st:
        gpre = [sb(st, nc, tag + "gpre%d" % k, [128, D], F32) for k in range(nj)]
        gpost = [sb(st, nc, tag + "gpost%d" % k, [128, D], F32) for k in range(nj)]
        hbuf = [sb(st, nc, tag + "hb%d" % i, [128, D], F32) for i in range(2)]
        hres = [sb(st, nc, tag + "hr%d" % i, [128, D], F32) for i in range(2)]
        ubf = [sb(st, nc, tag + "ub%d" % i, [128, D], BF16) for i in range(2)]
        uT = [sb(st, nc, tag + "uT%d" % i, [128, 8, TB], BF16) for i in range(2)]
        w1s = [sb(st, nc, tag + "w1s%d" % i, [128, 8, 2, 256], BF16) for i in range(4)]
        w2s = sb(st, nc, tag + "w2s", [128, NJ, D], BF16)
        actT = sb(st, nc, tag + "actT", [128, NJ, TB], BF16)
        sg = [sb(st, nc, tag + "sg%d" % i, [128, 512], F32) for i in range(2)]
        mb = [sb(st, nc, tag + "mb%d" % i, [128, D], F32) for i in range(2)]
        junk = sb(st, nc, tag + "junk", [128, D], BF16)
        stat = sb(st, nc, tag + "stat", [128, 24], F32)
        tp = ps(st, nc, tag + "tp", [128, 8, 128], BF16)
        gps = [ps(st, nc, tag + "gps%d" % i, [128, 512], F32) for i in range(2)]
        ups = [ps(st, nc, tag + "ups%d" % i, [128, 512], F32) for i in range(2)]
        ops_ = [ps(st, nc, tag + "ops%d" % i, [128, 512], F32) for i in range(2)]

        b_g = P.bufs("g", nj)
        b_hd = P.bufs("hdram", NT)
        b_hbuf = P.bufs("hbuf", 2)
        b_hres = P.bufs("hres", 2)
        b_ubf = P.bufs("ubf", 2)
        b_uT = P.bufs("uT", 4)
        b_w1 = P.bufs("w1s", 4)
        b_w2 = P.bufs("w2s", NJ)
        b_act = P.bufs("actT", NJ * 2)
        b_sg = P.bufs("sg", 2)
        b_mb = P.bufs("mb", 2)
        b_junk = P.buf("junk")
        b_st1 = P.bufs("st1", 2)
        b_st2 = P.bufs("st2", 2)
        b_tp = P.buf("tp")
        b_gps = P.bufs("gps", 2)
        b_ups = P.bufs("ups", 2)
        b_ops = P.bufs("ops", 2)
        b_pace = [P.bufs("pace%d" % k, NJ) for k in range(nj)]

        for k, jb in enumerate(jobs):
            P.dma("sp", I_dma(gpre[k][:], jb["g_pre"].partition_broadcast(128)), writes=[b_g[k]])
            P.dma("sp", I_dma(gpost[k][:], jb["g_post"].partition_broadcast(128)), writes=[b_g[k]])

        cnt = {"prep": 0, "w1": 0, "gu": 0, "ep": 0}
        w1v = [jb["w1"].rearrange("(kc p) (two f) -> p kc two f", p=128, two=2) for jb in jobs]

        def prep_load(gb, t):
            k, b = gb // nblk, gb % nblk
            i = cnt["prep"] % 2
            cnt["prep"] += 1
            tok0 = b * TB + t * 128
            P.dma("sp", I_dma(hbuf[i][:], jobs[k]["h_src"][tok0:tok0 + 128, :]), reads=[b_hd[b * TPB + t]], writes=[b_hbuf[i]])
            return i

        def prep_norm(gb, t, i=None):
            k = gb // nblk
            if i is None:
                i = prep_load(gb, t)
            P.op("act", I_act(junk[:], hbuf[i][:], AF.Square, accum_out=stat[:, i:i + 1]),
                 reads=[b_hbuf[i]], writes=[b_junk, b_st1[i]])
            rstd_pool(P, C, stat[:, 2 + i:3 + i], stat[:, i:i + 1], 1.0 / D, [b_st1[i]], [b_st1[i]])
            P.op("dve", I_stt(ubf[i][:], hbuf[i][:], stat[:, 2 + i:3 + i], gpre[k][:], ALU.mult, ALU.mult),
                 reads=[b_hbuf[i], b_st1[i], b_g[k]], writes=[b_ubf[i]])
            return i

        def prep_tr(gb, t, i):
            s = gb % 2
            for kc in range(8):
                P.op("pe", I_tr(tp[:, kc, :], ubf[i][:, kc * 128:(kc + 1) * 128], C.ident[:]),
                     reads=[b_ubf[i], C.b_ident], writes=[b_tp])
            P.op("act", I_acopy(uT[s][:, :, t * 128:(t + 1) * 128], tp[:, :, :]),
                 reads=[b_tp], writes=[b_uT[s * 2 + t // 4]])

        def load_w1(k, jp):
            i = cnt["w1"] % 4
            cnt["w1"] += 1
            for two in range(2):
                P.dma("pool", I_dma(w1s[i][:, :, two, :], w1v[k][:, :, two, jp * 256:(jp + 1) * 256]), writes=[b_w1[i]])
            return i

        def load_w2(k, j):
            P.dma("pool", I_dma(w2s[:, j, :], jobs[k]["w2"][j * 128:(j + 1) * 128, :]),
                  reads=([b_pace[k][j - 3]] if j >= 3 else []), writes=[b_w2[j]])

        def first_j(gb, j, wi):
            k, b = gb // nblk, gb % nblk
            s = gb % 2
            jo = (j % 2) * 128
            for half in range(2):
                q = cnt["gu"] % 2
                cnt["gu"] += 1
                for kc in range(8):
                    P.op("pe", I_mm(gps[q][:], w1s[wi][:, kc, 0, jo:jo + 128],
                                    uT[s][:, kc, half * 512:(half + 1) * 512], kc == 0, kc == 7),
                         reads=[b_w1[wi], b_uT[s * 2 + half]], writes=[b_gps[q]])
                for kc in range(8):
                    P.op("pe", I_mm(ups[q][:], w1s[wi][:, kc, 1, jo:jo + 128],
                                    uT[s][:, kc, half * 512:(half + 1) * 512], kc == 0, kc == 7),
                         reads=[b_w1[wi], b_uT[s * 2 + half]],
                         writes=[b_ups[q]] + ([b_pace[k][j]] if (b == 0 and half == 1 and kc == 7) else []))
                P.op("act", I_act(sg[q][:], gps[q][:], AF.Silu), reads=[b_gps[q]], writes=[b_sg[q]])
                P.op("dve", I_tt(actT[:, j, half * 512:(half + 1) * 512], ups[q][:], sg[q][:], ALU.mult),
                     reads=[b_ups[q], b_sg[q]], writes=[b_act[j * 2 + half]])

        def second_tile(gb, t):
            k, b = gb // nblk, gb % nblk
            i = cnt["ep"] % 2
            cnt["ep"] += 1
            tok0 = b * TB + t * 128
            tile = b * TPB + t
            half = t // 4
            P.dma("sp", I_dma(hres[i][:], jobs[k]["h_src"][tok0:tok0 + 128, :]), reads=[b_hd[tile]], writes=[b_hres[i]])
            for dh in range(2):
                for j in range(NJ):
                    P.op("pe", I_mm(ops_[dh][:], actT[:, j, t * 128:(t + 1) * 128],
                                    w2s[:, j, dh * 512:(dh + 1) * 512], j == 0, j == NJ - 1),
                         reads=[b_act[j * 2 + half], b_w2[j]], writes=[b_ops[dh]])
                c0 = 4 + 2 * i + dh
                P.op("dve", I_copy(mb[i][:, dh * 512:(dh + 1) * 512], ops_[dh][:]),
                     reads=[b_ops[dh]], writes=[b_mb[i]])
                P.op("act", I_act(junk[:, 0:512], mb[i][:, dh * 512:(dh + 1) * 512], AF.Square,
                                  accum_out=stat[:, c0:c0 + 1]),
                     reads=[b_mb[i]], writes=[b_junk, b_st2[i]])
            P.op("pool", I_tt(stat[:, 16 + i:17 + i], stat[:, 4 + 2 * i:5 + 2 * i], stat[:, 5 + 2 * i:6 + 2 * i], ALU.add),
                 reads=[b_st2[i]], writes=[b_st2[i]])
            rstd_pool(P, C, stat[:, 16 + i:17 + i], stat[:, 16 + i:17 + i], 1.0 / D, [b_st2[i]], [b_st2[i]], post=0.5)
            P.op("dve", I_stt(mb[i][:], mb[i][:], stat[:, 16 + i:17 + i], gpost[k][:], ALU.mult, ALU.mult),
                 reads=[b_mb[i], b_st2[i], b_g[k]], writes=[b_mb[i]])
            P.op("dve", I_tt(mb[i][:], mb[i][:], hres[i][:], ALU.add),
                 reads=[b_mb[i], b_hres[i]], writes=[b_mb[i]])
            P.dma("sp", I_dma(jobs[k]["h_dst"][tok0:tok0 + 128, :], mb[i][:]), reads=[b_mb[i]], writes=[b_hd[tile]])

        slot0 = {}
        sw_pipeline(TPB, [lambda t: slot0.__setitem__(t, prep_load(0, t)),
                          lambda t: prep_norm(0, t, slot0[t]),
                          lambda t: prep_tr(0, t, slot0[t])])
        ngb = nj * nblk
        for gb in range(ngb):
            k, b = gb // nblk, gb % nblk
            pend = []
            wq = [load_w1(k, 0)]
            for j in range(NJ):
                if j % 2 == 0 and j // 2 + 1 < NJ2:
                    wq.append(load_w1(k, j // 2 + 1))
                if b == 0:
                    load_w2(k, j)
                first_j(gb, j, wq[j // 2])
                if gb + 1 < ngb:
                    if j % 2 == 0 and j // 2 < TPB:
                        pend.append((j // 2, prep_norm(gb + 1, j // 2)))
                    if j % 2 == 1 and pend:
                        t, i = pend.pop(0)
                        prep_tr(gb + 1, t, i)
            for t in range(TPB):
                second_tile(gb, t)
        P.barrier()


def sw_pipeline(n, stages):
    ns = len(stages)
    for step in range(n + ns - 1):
        for k in reversed(range(ns)):
            t = step - k
            if 0 <= t < n:
                stages[k](t)


def norm_to_uT(P, C, st, nc, tag, h_src, g_dram, uT_all, b_uT, ntiles=NT, rows=None, tp_ext=None):
    with ExitStack() as s2:
        gbc = sb(s2, nc, tag + "gbc", [128, D], F32)
        hb = [sb(s2, nc, tag + "nhb%d" % i, [128, D], F32) for i in range(2)]
        ub = [sb(s2, nc, tag + "nub%d" % i, [128, D], BF16) for i in range(2)]
        stat = sb(s2, nc, tag + "nstat", [128, 8], F32)
        if tp_ext is None:
            tp = [ps(s2, nc, tag + "ntp%d" % i, [128, 8, 128], BF16) for i in range(2)]
            b_tp = P.bufs("ntp", 2)
        else:
            tp, b_tp = tp_ext
        b_g = P.buf("g")
        b_hb = P.bufs("nhb", 2)
        b_ub = P.bufs("nub", 2)
        b_st = P.bufs("nst", 2)
        P.dma("sp", I_dma(gbc[:], g_dram.partition_broadcast(128)), writes=[b_g])

        def s_load(t):
            i = t % 2
            src_rows = h_src[t * 128:(t + 1) * 128, :] if rows is None else h_src[rows(t), :]
            P.dma("sp", I_dma(hb[i][:], src_rows), writes=[b_hb[i]])

        def s_stat(t):
            i, j = t % 2, t % 2
            P.op("act", I_act(ub[j][:], hb[i][:], AF.Square, accum_out=stat[:, j:j + 1]),
                 reads=[b_hb[i]], writes=[b_ub[j], b_st[j]])
            rstd_pool(P, C, stat[:, 2 + j:3 + j], stat[:, j:j + 1], 1.0 / D, [b_st[j]], [b_st[j]])

        def s_scale(t):
            i, j = t % 2, t % 2
            P.op("dve", I_stt(ub[j][:], hb[i][:], stat[:, 2 + j:3 + j], gbc[:], ALU.mult, ALU.mult),
                 reads=[b_hb[i], b_st[j], b_g], writes=[b_ub[j]])

        def s_tr(t):
            j = t % 2
            for kc in range(8):
                P.op("pe", I_tr(tp[j][:, kc, :], ub[j][:, kc * 128:(kc + 1) * 128], C.ident[:]),
                     reads=[b_ub[j], C.b_ident], writes=[b_tp[j]])

        def s_copy(t):
            j = t % 2
            P.op("act", I_acopy(uT_all[:, :, t * 128:(t + 1) * 128], tp[j][:, :, :]),
                 reads=[b_tp[j]], writes=[b_uT])

        sw_pipeline(ntiles, [s_load, s_stat, s_scale, s_tr, s_copy])
        P.barrier()


def I_dma_acc(out, in_):
    nbytes = 128 * _fsz(in_) * 4
    return _wc(lambda e: e.dma_start(out=out, in_=in_, accum_op=ALU.add), 2 * nbytes / 200e3)


def post_part1(P, ops_, b_ops, mb, b_mb, junk, b_junk, stat, b_st, i, copy_eng="dve"):
    for dh in range(2):
        c0 = 4 + 2 * i + dh
        if copy_eng == "dve":
            P.op("dve", I_copy(mb[i][:, dh * 512:(dh + 1) * 512], ops_[dh][:]), reads=[b_ops[dh]], writes=[b_mb[i]])
        else:
            P.op("act", I_acopy(mb[i][:, dh * 512:(dh + 1) * 512], ops_[dh][:]), reads=[b_ops[dh]], writes=[b_mb[i]])
        P.op("act", I_act(junk[:, 0:512], mb[i][:, dh * 512:(dh + 1) * 512], AF.Square, accum_out=stat[:, c0:c0 + 1]),
             reads=[b_mb[i]], writes=[b_junk, b_st[i]])


def post_part2(P, C, mb, b_mb, hres, b_hres, stat, b_st, gpost, b_g, h_dst, tok0, i, post, add_eng="dve"):
    P.op("pool", I_tt(stat[:, 16 + i:17 + i], stat[:, 4 + 2 * i:5 + 2 * i], stat[:, 5 + 2 * i:6 + 2 * i], ALU.add),
         reads=[b_st[i]], writes=[b_st[i]])
    rstd_pool(P, C, stat[:, 16 + i:17 + i], stat[:, 16 + i:17 + i], 1.0 / D, [b_st[i]], [b_st[i]], post=post)
    P.op("dve", I_stt(mb[i][:], mb[i][:], stat[:, 16 + i:17 + i], gpost[:], ALU.mult, ALU.mult),
         reads=[b_mb[i], b_st[i], b_g], writes=[b_mb[i]])
    if add_eng == "dma":
        P.dma("pool", I_dma_acc(h_dst[tok0:tok0 + 128, :], mb[i][:]), reads=[b_mb[i]])
        return
    P.op(add_eng, I_tt(mb[i][:], mb[i][:], hres[i][:], ALU.add), reads=[b_mb[i], b_hres[i]], writes=[b_mb[i]])
    P.dma("sp", I_dma(h_dst[tok0:tok0 + 128, :], mb[i][:]), reads=[b_mb[i]])


def post_residual(P, C, tag_bufs, ops_, b_ops, mb, b_mb, hres, b_hres, junk, b_junk, stat, b_st, gpost, b_g,
                  h_src, h_dst, tok0, i, post, add_eng="dve"):
    P.dma("sp", I_dma(hres[i][:], h_src[tok0:tok0 + 128, :]), writes=[b_hres[i]])
    post_part1(P, ops_, b_ops, mb, b_mb, junk, b_junk, stat, b_st, i)
    post_part2(P, C, mb, b_mb, hres, b_hres, stat, b_st, gpost, b_g, h_dst, tok0, i, post, add_eng)


ATT_PATTERNS = ((128, 1), (512, 4), (2048, 16))
NEG = -30000.0


def attn_phase(P, C, h, w_qkv, w_o, g_pre, g_post, acc, ubs, tag="at", groups=(0, 1, 2), do_merge=True):
    nc = P.nc
    with ExitStack() as st:
        uT = sb(st, nc, tag + "uT", [128, 8, L], BF16)
        b_uT = P.bufs("uTr", 8)
        b_acc = P.buf("acc_dram")
        with ExitStack() as s1:
            gbc = sb(s1, nc, tag + "gbc", [128, D], F32)
            hb = [sb(s1, nc, tag + "nhb%d" % i, [128, D], F32) for i in range(4)]
            ub = [sb(s1, nc, tag + "nub%d" % i, [128, D], BF16) for i in range(3)]
            stat = sb(s1, nc, tag + "nstat", [128, 8], F32)
            b_g = P.buf("g")
            b_hb = P.bufs("nhb", 4)
            b_ub = P.bufs("nub", 3)
            b_st = P.bufs("nst", 2)
            P.dma("sp", I_dma(gbc[:], g_pre.partition_broadcast(128)), writes=[b_g])

            def n0(t):
                P.dma("sp", I_dma(hb[t % 4][:], h[t * 128:(t + 1) * 128, :]), writes=[b_hb[t % 4]])

            def n1(t):
                i, j, k = t % 4, t % 2, t % 3
                P.op("act", I_act(ub[k][:], hb[i][:], AF.Square, accum_out=stat[:, j:j + 1]),
                     reads=[b_hb[i]], writes=[b_ub[k], b_st[j]])
                rstd_pool(P, C, stat[:, 2 + j:3 + j], stat[:, j:j + 1], 1.0 / D, [b_st[j]], [b_st[j]])

            def n2(t):
                i, j, k = t % 4, t % 2, t % 3
                P.op("dve", I_stt(ub[k][:], hb[i][:], stat[:, 2 + j:3 + j], gbc[:], ALU.mult, ALU.mult),
                     reads=[b_hb[i], b_st[j], b_g], writes=[b_ub[k]])
                P.dma("sp", I_dma(ubs[t * 128:(t + 1) * 128, :], ub[k][:]), reads=[b_ub[k]])

            sw_pipeline(NT, [n0, n1, n2])
            P.barrier()
        with ExitStack() as s2:
            KT = sb(s2, nc, tag + "KT", [128, 4, L], BF16)
            QT = [sb(s2, nc, tag + "QT%d" % i, [128, 4, 512], BF16) for i in range(2)]
            VA = sb(s2, nc, tag + "VA", [128, NT, 8, 65], BF16)
            Wq = sb(s2, nc, tag + "Wq", [128, 8, 512], BF16)
            Wk = sb(s2, nc, tag + "Wk", [128, 8, 512], BF16)
            Wv = sb(s2, nc, tag + "Wv", [128, 8, 512], BF16)
            rel = sb(s2, nc, tag + "rel", [128, 128], F32)
            maskc = sb(s2, nc, tag + "maskc", [128, 128], F32)
            maskp = sb(s2, nc, tag + "maskp", [128, 128], F32)
            bias_c = sb(s2, nc, tag + "bc", [128, 2, 4, 128], F32)
            bias_p = sb(s2, nc, tag + "bp", [128, 2, 4, 128], F32)
            Ec = sb(s2, nc, tag + "Ec", [128, 2, 4, 128], F32)
            Ep = sb(s2, nc, tag + "Ep", [128, 2, 4, 128], F32)
            Pc = sb(s2, nc, tag + "Pc", [128, 2, 4, 128], BF16)
            Pp = sb(s2, nc, tag + "Pp", [128, 2, 4, 128], BF16)
            osb = [sb(s2, nc, tag + "osb%d" % i, [128, 8, 65], F32) for i in range(2)]
            ubl = [sb(s2, nc, tag + "ubl%d" % i, [128, D], BF16) for i in range(4)]
            b_ubl = P.bufs("ubl", 4)
            pj = [ps(s2, nc, tag + "pj%d" % i, [128, 512], F32) for i in range(2)]
            sc = ps(s2, nc, tag + "sc", [128, 2, 4, 128], F32)
            sp_ = ps(s2, nc, tag + "sp", [128, 2, 4, 128], F32)
            po = ps(s2, nc, tag + "po", [128, 2, 512], F32)
            b_KT = P.bufs("KT", 8)
            b_QT = P.bufs("QT", 2)
            b_VA = P.bufs("VA", NT)
            b_W = P.bufs("Wqkv", 3)
            b_bias = P.buf("bias")
            b_Ec, b_Ep, b_Pc, b_Pp = P.buf("Ec"), P.buf("Ep"), P.buf("Pc"), P.buf("Pp")
            b_osb = P.bufs("osb", 2)
            b_pj = P.bufs("pj", 2)
            b_sc, b_sp, b_po = P.buf("sc"), P.buf("sp"), P.buf("po")
            cnt = {"pj": 0, "ev": 0, "o": 0}

            P.op("pool", lambda e: e.iota(rel[:], pattern=[[1, 128]], base=0, channel_multiplier=-1,
                                          allow_small_or_imprecise_dtypes=True), writes=[b_bias])
            P.op("pool", I_memset(VA[:, :, :, 64:65], 1.0), writes=b_VA)
            P.op("pool", I_memset(maskc[:], 0.0), writes=[b_bias])
            P.op("pool", lambda e: e.affine_select(out=maskc[:], in_=maskc[:], pattern=[[1, 128]], compare_op=ALU.is_ge,
                                                   fill=NEG, base=0, channel_multiplier=-1), reads=[b_bias], writes=[b_bias])
            P.op("pool", I_memset(maskp[:], 0.0), writes=[b_bias])
            P.op("pool", lambda e: e.affine_select(out=maskp[:], in_=maskp[:], pattern=[[-1, 128]], compare_op=ALU.is_ge,
                                                   fill=NEG, base=0, channel_multiplier=1), reads=[b_bias], writes=[b_bias])

            for g in groups:
                window, d = ATT_PATTERNS[g]
                nd = L // d
                bpr = nd // 128

                def segs(pi0, count):
                    out = []
                    p = pi0
                    while p < pi0 + count:
                        r, j = p // nd, p % nd
                        n = min(nd - j, pi0 + count - p)
                        out.append((p - pi0, n, r + d * j, d))
                        p += n
                    return out

                def tok_slice(t0, n, step):
                    return slice(t0, t0 + step * (n - 1) + 1, step)

                def pi_rows(tau):
                    (o0, n, t0, step), = segs(tau * 128, 128)
                    return tok_slice(t0, n, step)

                tpv = [pj[i].bitcast(BF16)[:, :].rearrange("p (k e) -> p k e", k=8) for i in range(2)]
                for tau in range(NT):
                    j = tau % 2
                    P.dma("sp", I_dma(ubl[tau % 4][:], ubs[pi_rows(tau), :]), writes=[b_ubl[tau % 4]])
                    for kc in range(8):
                        P.op("pe", I_tr(tpv[j][:, kc, :], ubl[tau % 4][:, kc * 128:(kc + 1) * 128], C.ident[:]),
                             reads=[b_ubl[tau % 4], C.b_ident], writes=[b_pj[j]])
                    if tau % 2 == 0:
                        P.op("act", I_acopy(uT[:, :, tau * 128:(tau + 1) * 128], tpv[j][:, :, :]), reads=[b_pj[j]], writes=[b_uT[tau // 4]])
                    else:
                        P.op("dve", I_copy(uT[:, :, tau * 128:(tau + 1) * 128], tpv[j][:, :, :]), reads=[b_pj[j]], writes=[b_uT[tau // 4]])

                for s_, Wt in ((0, Wq), (1, Wk), (2, Wv)):
                    c0 = s_ * 1536 + g * 512
                    src = w_qkv.rearrange("(kc p) c -> p kc c", p=128)[:, :, c0:c0 + 512]
                    P.dma("pool", I_dma(Wt[:, :, :], src), writes=[b_W[s_]])
                for hh in range(8):
                    par, pi = hh % 2, hh // 2
                    sl = -(2.0 ** -(hh + 1)) * d
                    P.op("pool", I_ts(bias_c[:, par, pi, :], rel[:], sl, None, ALU.mult), reads=[b_bias], writes=[b_bias])
                    P.op("pool", I_tt(bias_c[:, par, pi, :], bias_c[:, par, pi, :], maskc[:], ALU.add),
                         reads=[b_bias], writes=[b_bias])
                    P.op("pool", I_ts(bias_p[:, par, pi, :], rel[:], 128.0, sl, ALU.add, ALU.mult), reads=[b_bias], writes=[b_bias])
                    P.op("pool", I_tt(bias_p[:, par, pi, :], bias_p[:, par, pi, :], maskp[:], ALU.add),
                         reads=[b_bias], writes=[b_bias])

                def proj_T(dst, b_dst, Wt, b_Wt, pi0, scale, dcol0):
                    for c in range(4):
                        q = cnt["pj"] % 2
                        cnt["pj"] += 1
                        for kc in range(8):
                            P.op("pe", I_mm(pj[q][:], Wt[:, kc, c * 128:(c + 1) * 128],
                                            uT[:, kc, pi0:pi0 + 512], kc == 0, kc == 7),
                                 reads=[b_Wt, b_uT[pi0 // 512]], writes=[b_pj[q]])
                        if cnt["ev"] % 2 == 0:
                            P.op("act", I_act(dst[:, c, dcol0:dcol0 + 512], pj[q][:], AF.Copy, scale=scale),
                                 reads=[b_pj[q]], writes=[b_dst])
                        else:
                            P.op("dve", I_ts(dst[:, c, dcol0:dcol0 + 512], pj[q][:], scale, None, ALU.mult),
                                 reads=[b_pj[q]], writes=[b_dst])
                        cnt["ev"] += 1

                for rg in range(8):
                    proj_T(KT, b_KT[rg], Wk, b_W[1], rg * 512, 1.0, rg * 512)
                for tau in range(NT):
                    q = cnt["pj"] % 2
                    cnt["pj"] += 1
                    for kc in range(8):
                        P.op("pe", I_mm(pj[q][:], uT[:, kc, tau * 128:(tau + 1) * 128], Wv[:, kc, :], kc == 0, kc == 7),
                             reads=[b_W[2], b_uT[tau // 4]], writes=[b_pj[q]])
                    src = pj[q][:, :].rearrange("p (h e) -> p h e", h=8)
                    if cnt["ev"] % 2 == 0:
                        P.op("act", I_acopy(VA[:, tau, :, 0:64], src), reads=[b_pj[q]], writes=[b_VA[tau]])
                    else:
                        P.op("dve", I_copy(VA[:, tau, :, 0:64], src), reads=[b_pj[q]], writes=[b_VA[tau]])
                    cnt["ev"] += 1
                for rg in range(8):
                    qi = rg % 2
                    proj_T(QT[qi], b_QT[qi], Wq, b_W[0], rg * 512, 0.125, 0)
                    for tb_ in range(4):
                        tau = rg * 4 + tb_
                        has_prev = (tau % bpr) != 0
                        for hh in range(8):
                            par, pi = hh % 2, hh // 2
                            c, p0 = hh // 2, (hh % 2) * 64
                            P.op("pe", I_mm(sc[:, par, pi, :], KT[p0:p0 + 64, c, tau * 128:(tau + 1) * 128],
                                            QT[qi][p0:p0 + 64, c, tb_ * 128:(tb_ + 1) * 128], True, True),
                                 reads=[b_KT[tau // 4], b_QT[qi]], writes=[b_sc])
                        if has_prev:
                            for hh in range(8):
                                par, pi = hh % 2, hh // 2
                                c, p0 = hh // 2, (hh % 2) * 64
                                P.op("pe", I_mm(sp_[:, par, pi, :], KT[p0:p0 + 64, c, (tau - 1) * 128:tau * 128],
                                                QT[qi][p0:p0 + 64, c, tb_ * 128:(tb_ + 1) * 128], True, True),
                                     reads=[b_KT[(tau - 1) // 4], b_QT[qi]], writes=[b_sp])
                        P.op("dve", I_tt(Ec[:], sc[:], bias_c[:], ALU.add), reads=[b_sc, b_bias], writes=[b_Ec])
                        P.op("act", I_act(Pc[:], Ec[:], AF.Exp), reads=[b_Ec], writes=[b_Pc])
                        if has_prev:
                            P.op("dve", I_tt(Ep[:], sp_[:], bias_p[:], ALU.add), reads=[b_sp, b_bias], writes=[b_Ep])
                            P.op("act", I_act(Pp[:], Ep[:], AF.Exp), reads=[b_Ep], writes=[b_Pp])
                        for hh in range(8):
                            par, pi = hh % 2, hh // 2
                            oap = po[:, hh // 4, (hh % 4) * 65:(hh % 4) * 65 + 65]
                            P.op("pe", I_mm(oap, Pc[:, par, pi, :], VA[:, tau, hh, :], True, not has_prev),
                                 reads=[b_Pc, b_VA[tau]], writes=[b_po])
                            if has_prev:
                                P.op("pe", I_mm(oap, Pp[:, par, pi, :], VA[:, tau - 1, hh, :], False, True),
                                     reads=[b_Pp, b_VA[tau - 1]], writes=[b_po])
                        oi = cnt["o"] % 2
                        cnt["o"] += 1
                        P.op("dve", I_copy(osb[oi][:, :, :].rearrange("p (b h) e -> p b (h e)", b=2), po[:, :, 0:260]),
                             reads=[b_po], writes=[b_osb[oi]])
                        (o0, n, t0, step), = segs(tau * 128, 128)
                        dst = acc[0, tok_slice(t0, n, step), :]
                        if g == groups[0]:
                            P.dma("sp", I_dma(dst, osb[oi][:, :, :].rearrange("p h e -> p (h e)")), reads=[b_osb[oi]], writes=[b_acc])
                        else:
                            P.dma("pool", I_dma_acc(dst, osb[oi][:, :, :].rearrange("p h e -> p (h e)")), reads=[b_osb[oi]], writes=[b_acc])
            P.barrier()
        if not do_merge:
            return
        with ExitStack() as s3:
            Wo = sb(s3, nc, tag + "Wo", [128, 4, D], BF16)
            gpost = sb(s3, nc, tag + "gpo", [128, D], F32)
            a_in = [[sb(s3, nc, tag + "ain%d_%d" % (i, g), [128, 8, 65], F32) for g in range(1)] for i in range(3)]
            rz = sb(s3, nc, tag + "rz", [128, 2, 8], F32)
            mg = [sb(s3, nc, tag + "mg%d" % i, [128, 8, 64], BF16) for i in range(2)]
            mT = [sb(s3, nc, tag + "mT%d" % i, [128, 4, 128], BF16) for i in range(2)]
            hres = [sb(s3, nc, tag + "hr%d" % i, [128, D], F32) for i in range(2)]
            mb = [sb(s3, nc, tag + "mb%d" % i, [128, D], F32) for i in range(4)]
            junk = sb(s3, nc, tag + "junk", [128, 512], BF16)
            stat = sb(s3, nc, tag + "stat", [128, 24], F32)
            tp2 = [ps(s3, nc, tag + "tp%d" % i, [128, 4, 128], BF16) for i in range(2)]
            ops4 = [ps(s3, nc, tag + "ops%d" % i, [128, 512], F32) for i in range(4)]
            b_Wo, b_g = P.buf("Wo"), P.buf("gpo")
            b_ain = P.bufs("ain", 3)
            b_rz = P.bufs("rz", 2)
            b_mg = P.bufs("mg", 2)
            b_mT = P.bufs("mT", 2)
            b_hres = P.bufs("hres", 2)
            b_mb = P.bufs("mb", 4)
            b_junk = P.buf("junk")
            b_st = P.bufs("st", 4)
            b_tp2 = P.bufs("tp", 2)
            b_ops4 = P.bufs("ops", 4)
            P.dma("pool", I_dma(Wo[:, :, :], w_o.rearrange("(c p) n -> p c n", p=128)), writes=[b_Wo])
            P.dma("sp", I_dma(gpost[:], g_post.partition_broadcast(128)), writes=[b_g])
            def m0(t):
                P.dma("sp", I_dma(a_in[t % 3][0][:, :, :].rearrange("p h e -> p (h e)"), acc[0, t * 128:(t + 1) * 128, :]),
                      writes=[b_ain[t % 3]])

            def m1(t):
                i = t % 2
                A0 = a_in[t % 3][0]
                P.op("dve", lambda e, o=rz[:, i, :], s=A0[:, :, 64]: e.reciprocal(out=o, in_=s), reads=[b_ain[t % 3]], writes=[b_rz[i]])
                P.op("dve", I_tt(mg[i][:], A0[:, :, 0:64], rz[:, i, :].unsqueeze(2).to_broadcast([128, 8, 64]), ALU.mult),
                     reads=[b_ain[t % 3], b_rz[i]], writes=[b_mg[i]])

            def m2(t):
                i = t % 2
                tp, b_tp = tp2[t % 2], b_tp2[t % 2]
                for c in range(4):
                    P.op("pe", I_tr(tp[:, c, :], mg[i][:, 2 * c:2 * c + 2, :].rearrange("p h e -> p (h e)"), C.ident[:]),
                         reads=[b_mg[i], C.b_ident], writes=[b_tp])
                P.op("act", I_acopy(mT[i][:, :, :], tp[:, :, :]), reads=[b_tp], writes=[b_mT[i]])

            def m3(t):
                i = t % 2
                ops_, b_ops = ops4[(t % 2) * 2:(t % 2) * 2 + 2], b_ops4[(t % 2) * 2:(t % 2) * 2 + 2]
                for dh in range(2):
                    for c in range(4):
                        P.op("pe", I_mm(ops_[dh][:], mT[i][:, c, :], Wo[:, c, dh * 512:(dh + 1) * 512], c == 0, c == 3),
                             reads=[b_mT[i], b_Wo], writes=[b_ops[dh]])

            def m4(t):
                i = t % 4
                ops_, b_ops = ops4[(t % 2) * 2:(t % 2) * 2 + 2], b_ops4[(t % 2) * 2:(t % 2) * 2 + 2]
                post_part1(P, ops_, b_ops, mb, b_mb, junk, b_junk, stat, b_st, i, copy_eng="act")

            def m5(t):
                i = t % 4
                post_part2(P, C, mb, b_mb, hres, b_hres, stat, b_st, gpost, b_g, h, t * 128, i, 1.0, add_eng="dma")

            sw_pipeline(NT, [m0, m1, m2, m3, m4, m5])
            P.barrier()


def bc3(ap2, n, axis):
    a = ap2.shape[1]
    if axis == 2:
        return ap2.unsqueeze(2).to_broadcast([ap2.shape[0], a, n])
    return ap2.unsqueeze(1).to_broadcast([ap2.shape[0], n, a])


def hyb_phase(P, C, h, w_in, conv_w, conv_b, ln_g, ln_b, sconv_w, sconv_b, dt_bias, a_log, d_skip, ng, w_out,
              g_pre, g_post, ysc, tag="hy", nblk=8):
    nc = P.nc
    TBK = 512
    with ExitStack() as st:
        Wdt = sb(st, nc, tag + "Wdt", [128, 8, 16], BF16)
        gpre = sb(st, nc, tag + "gpre", [128, D], F32)
        ngb = sb(st, nc, tag + "ngb", [128, D], F32)
        trif = sb(st, nc, tag + "trif", [128, 128], F32)
        onesf = sb(st, nc, tag + "onesf", [128, 128], F32)
        onesb = sb(st, nc, tag + "onesb", [128, 128], BF16)
        maskrep = sb(st, nc, tag + "maskrep", [128, 4, 128], BF16)
        w31T = sb(st, nc, tag + "w31T", [128, 8, 31], F32)
        w4T = sb(st, nc, tag + "w4T", [128, 12, 4], F32)
        cvec = sb(st, nc, tag + "cvec", [128, 48], F32)
        small = sb(st, nc, tag + "small", [128, 64], F32)
        brow = sb(st, nc, tag + "brow", [1, 1280], BF16)
        onesrow = sb(st, nc, tag + "onesrow", [1, 128], BF16)
        dg4 = sb(st, nc, tag + "dg4", [128, 12, 4, 128], BF16)
        dg31 = [sb(st, nc, tag + "dg31_%d" % i, [128, 31, 128], BF16) for i in range(2)]
        gluT = sb(st, nc, tag + "gluT", [128, 8, 30 + TBK], BF16)
        xbcT = sb(st, nc, tag + "xbcT", [128, 12, 3 + TBK], BF16)
        hidT = sb(st, nc, tag + "hidT", [128, 8, TBK], BF16)
        szT = sb(st, nc, tag + "szT", [128, 8, TBK], BF16)
        XT = sb(st, nc, tag + "XT", [128, 12, TBK], BF16)
        uT = [sb(st, nc, tag + "uT%d" % i, [128, 8, TBK], BF16) for i in range(2)]
        wsl = [sb(st, nc, tag + "wsl%d" % i, [128, 8, 128], BF16) for i in range(6)]
        S32 = sb(st, nc, tag + "S32", [128, D], F32)
        Sbf = sb(st, nc, tag + "Sbf", [128, D], BF16)
        dts = sb(st, nc, tag + "dts", [128, 4, 16], F32)
        lndt = sb(st, nc, tag + "lndt", [128, 4, 16], F32)
        hb = [sb(st, nc, tag + "hb%d" % i, [128, D], F32) for i in range(2)]
        ub = [sb(st, nc, tag + "ub%d" % i, [128, D], BF16) for i in range(2)]
        stat = sb(st, nc, tag + "stat", [128, 24], F32)
        f512 = [sb(st, nc, tag + "f512_%d" % i, [128, 512], F32) for i in range(8)]
        sqb = sb(st, nc, tag + "sqb", [128, 512], BF16)
        xs_t = [sb(st, nc, tag + "xs%d" % i, [128, D], BF16) for i in range(2)]
        yt_t = [sb(st, nc, tag + "yt%d" % i, [128, D], F32) for i in range(2)]
        Btm = [sb(st, nc, tag + "Btm%d" % i, [128, 256], BF16) for i in range(2)]
        sm_t = [sb(st, nc, tag + "sm%d" % i, [128, 160], F32) for i in range(2)]
        Rt_t = [sb(st, nc, tag + "Rt%d" % i, [128, 8, 128], F32) for i in range(2)]
        Lt_t = [sb(st, nc, tag + "Lt%d" % i, [128, 8, 128], F32) for i in range(2)]
        Mt_t = [sb(st, nc, tag + "Mt%d" % i, [128, 8, 128], BF16) for i in range(2)]
        CBs_t = [sb(st, nc, tag + "CBs%d" % i, [128, 256], F32) for i in range(2)]
        xdtd_t = [sb(st, nc, tag + "xdtd%d" % i, [128, D], BF16) for i in range(2)]
        xsD_t = [sb(st, nc, tag + "xsD%d" % i, [128, D], BF16) for i in range(2)]
        yb_t = [sb(st, nc, tag + "yb%d" % i, [128, D], BF16) for i in range(2)]
        ybT_t = [sb(st, nc, tag + "ybT%d" % i, [128, 8, 128], BF16) for i in range(2)]
        pA = ps(st, nc, tag + "pA", [128, 2, 512], F32)
        pB = ps(st, nc, tag + "pB", [128, 2, 512], F32)
        pC = ps(st, nc, tag + "pC", [128, 2, 512], F32)
        p6 = ps(st, nc, tag + "p6", [128, 512], F32)
        p7 = ps(st, nc, tag + "p7", [128, 8, 128], BF16)
        stg = hb[0]

        B = {}
        for n_ in ["Wdt", "g", "const", "dg4", "S32", "Sbf", "dts",
                   "stat", "sqb", "stg", "pA0", "pA1", "pB0", "pB1", "pC0", "pC1", "p6", "p7"]:
            B[n_] = P.buf(n_)
        for n_, k in (("gluT", 8), ("xbcT", 12), ("hidT", 8), ("szT", 8), ("XT", 12), ("uT", 2), ("hb", 2), ("ub", 2), ("dg31", 2), ("wsl", 6), ("f512", 8), ("xs", 2), ("yt", 2), ("Btm", 2),
                      ("sm", 2), ("Rt", 2), ("Lt", 2), ("Mt", 2), ("CBs", 2), ("xdtd", 2), ("xsD", 2), ("yb", 2),
                      ("ybT", 2), ("nst", 2)):
            B[n_] = P.bufs(n_, k)
        cnt = {"w": 0, "dg": 0, "prep": 0, "ch": 0, "g": 0}
        w_in_v = w_in.rearrange("(kc p) c -> p kc c", p=128)

        P.dma("pool", I_dma(Wdt[:, :, :], w_in_v[:, :, 4608:4624]), writes=[B["Wdt"]])
        P.dma("sp", I_dma(gpre[:], g_pre.partition_broadcast(128)), writes=[B["g"]])
        P.dma("sp", I_dma(ngb[:], ng.partition_broadcast(128)), writes=[B["g"]])
        P.dma("sp", I_dma(small[:, 0:16], a_log.partition_broadcast(128)), writes=[B["const"]])
        P.dma("sp", I_dma(small[:, 16:32], dt_bias.partition_broadcast(128)), writes=[B["const"]])
        P.dma("sp", I_dma(small[:, 32:48], d_skip.partition_broadcast(128)), writes=[B["const"]])
        P.op("act", I_act(small[:, 0:16], small[:, 0:16], AF.Exp), reads=[B["const"]], writes=[B["const"]])
        P.op("dve", I_ts(small[:, 0:16], small[:, 0:16], -1.0, None, ALU.mult), reads=[B["const"]], writes=[B["const"]])
        P.op("pool", I_memset(trif[:], 1.0), writes=[B["const"]])
        P.op("pool", lambda e: e.affine_select(out=trif[:], in_=trif[:], pattern=[[1, 128]], compare_op=ALU.is_ge,
                                               fill=0.0, base=0, channel_multiplier=-1), reads=[B["const"]], writes=[B["const"]])
        P.op("pool", I_memset(onesf[:], 1.0), writes=[B["const"]])
        P.op("pool", I_memset(onesb[:], 1.0 / D), writes=[B["const"]])
        P.op("pool", I_memset(onesrow[:], 1.0), writes=[B["const"]])
        P.op("pool", I_memset(maskrep[:], 0.0), writes=[B["const"]])
        P.op("pool", lambda e: e.affine_select(out=maskrep[:], in_=maskrep[:], pattern=[[0, 4], [1, 128]],
                                               compare_op=ALU.is_ge, fill=NEG, base=0, channel_multiplier=-1),
             reads=[B["const"]], writes=[B["const"]])
        P.op("pool", I_memset(gluT[:, :, 0:30], 0.0), writes=B["gluT"])
        P.op("pool", I_memset(xbcT[:, :, 0:3], 0.0), writes=B["xbcT"])
        P.op("pool", I_memset(S32[:], 0.0), writes=[B["S32"]])
        P.op("pool", I_memset(Sbf[:], 0.0), writes=[B["Sbf"]])

        stg_tiles = [hb[0], hb[1], yt_t[0], yt_t[1]]
        stg_slots = [(ti, base) for ti in range(4) for base in (0, 32, 64)]
        b_stg = P.bufs("stgs", len(stg_slots))
        ps_rot = [(pA[:, 0, :], B["pA0"]), (pA[:, 1, :], B["pA1"]), (pB[:, 0, :], B["pB0"]), (pB[:, 1, :], B["pB1"]),
                  (pC[:, 0, :], B["pC0"]), (pC[:, 1, :], B["pC1"]), (p6[:, :], B["p6"])]
        scnt = {"s": 0, "p": 0}

        def load_T(src2d, r, n, dst3, scale=1.0):
            for c0 in range(0, n, 8):
                nn = min(8, n - c0)
                si = scnt["s"] % len(stg_slots)
                scnt["s"] += 1
                ti, base = stg_slots[si]
                sv = stg_tiles[ti]
                P.dma("sp", I_dma(sv[base:base + r, 0:nn * 128], src2d[:, c0 * 128:(c0 + nn) * 128]), writes=[b_stg[si]])
                for c in range(nn):
                    pbank, pbuf = ps_rot[scnt["p"] % len(ps_rot)]
                    scnt["p"] += 1
                    P.op("pe", I_tr(pbank[:, 0:r], sv[base:base + r, c * 128:(c + 1) * 128],
                                    C.identf[base:base + r, base:base + r]),
                         reads=[b_stg[si], C.b_ident], writes=[pbuf])
                    P.op("dve", I_ts(dst3[:, c0 + c, :], pbank[:, 0:r], scale, None, ALU.mult),
                         reads=[pbuf], writes=[B["const"]])

        load_T(conv_w, 31, 8, w31T[:, :, :], 0.5)
        load_T(sconv_w, 4, 12, w4T[:, :, :], 0.5)
        cv3 = cvec[:, :].rearrange("p (a b) -> p a b", b=1)
        load_T(conv_b.rearrange("(o n) -> o n", o=1), 1, 8, cv3[:, 0:8, :], 1.0)
        load_T(ln_g.rearrange("(o n) -> o n", o=1), 1, 8, cv3[:, 8:16, :], 0.5)
        load_T(ln_b.rearrange("(o n) -> o n", o=1), 1, 8, cv3[:, 16:24, :], 0.5)
        load_T(sconv_b.rearrange("(o n) -> o n", o=1), 1, 12, cv3[:, 24:36, :], 0.5)
        for c in range(12):
            P.op("dve", I_tt(dg4[:, c, :, :], bc3(C.identf[:], 4, 1), bc3(w4T[:, c, :], 128, 2), ALU.mult),
                 reads=[B["const"], C.b_ident], writes=[B["dg4"]])
        P.barrier()

        def prep_l(blk, t):
            i = cnt["prep"] % 2
            cnt["prep"] += 1
            tok0 = blk * TBK + t * 128
            P.dma("sp", I_dma(hb[i][:], h[tok0:tok0 + 128, :]), writes=[B["hb"][i]])
            return i

        def prep_a(blk, t, i=None):
            if i is None:
                i = prep_l(blk, t)
            P.op("act", I_act(ub[i][:], hb[i][:], AF.Square, accum_out=stat[:, i:i + 1]),
                 reads=[B["hb"][i]], writes=[B["ub"][i], B["nst"][i]])
            rstd_pool(P, C, stat[:, 2 + i:3 + i], stat[:, i:i + 1], 1.0 / D, [B["nst"][i]], [B["nst"][i]])
            P.op("dve", I_stt(ub[i][:], hb[i][:], stat[:, 2 + i:3 + i], gpre[:], ALU.mult, ALU.mult),
                 reads=[B["hb"][i], B["nst"][i], B["g"]], writes=[B["ub"][i]])
            return i

        def prep_b(blk, t, i):
            s = blk % 2
            for kc in range(8):
                P.op("pe", I_tr(p7[:, kc, :], ub[i][:, kc * 128:(kc + 1) * 128], C.ident[:]),
                     reads=[B["ub"][i], C.b_ident], writes=[B["p7"]])
            P.op("act", I_acopy(uT[s][:, :, t * 128:(t + 1) * 128], p7[:, :, :]), reads=[B["p7"]], writes=[B["uT"][s]])

        def load_w(col0):
            i = cnt["w"] % 6
            cnt["w"] += 1
            P.dma("pool", I_dma(wsl[i][:, :, :], w_in_v[:, :, col0:col0 + 128]), writes=[B["wsl"][i]])
            return i

        def proj_cm(blk, col0, half):
            s = blk % 2
            wi = load_w(col0)
            for kc in range(8):
                P.op("pe", I_mm(pA[:, half, :], wsl[wi][:, kc, :], uT[s][:, kc, :], kc == 0, kc == 7),
                     reads=[B["wsl"][wi], B["uT"][s]], writes=[B["pA%d" % half]])

        for t in range(4):
            prep_b(0, t, prep_a(0, t))

        for blk in range(nblk):
            tokb = blk * TBK
            pend = []
            nxt = [(blk + 1, t) for t in range(4)] if blk + 1 < nblk else []

            def prefetch_load():
                if nxt:
                    b_, t_ = nxt.pop(0)
                    pend.append((b_, t_, prep_l(b_, t_)))

            def prefetch_compute():
                if pend:
                    b_, t_, i_ = pend.pop(0)
                    prep_a(b_, t_, i_)
                    prep_b(b_, t_, i_)

            def c0_(c):
                proj_cm(blk, c * 128, 0)
                proj_cm(blk, 1024 + c * 128, 1)

            def c1_(c):
                di = c % 2
                P.op("dve", I_tt(dg31[di][:, :, :], bc3(C.ident[:], 31, 1), bc3(w31T[:, c, :], 128, 2), ALU.mult),
                     reads=[B["const"], C.b_ident], writes=[B["dg31"][di]])
                P.op("act", I_act(f512[2][:], pA[:, 1, :], AF.Tanh, scale=0.5), reads=[B["pA1"]], writes=[B["f512"][2]])
                P.op("dve", I_stt(gluT[:, c, 30:30 + TBK], f512[2][:], 1.0, pA[:, 0, :], ALU.add, ALU.mult),
                     reads=[B["f512"][2], B["pA0"]], writes=[B["gluT"][c]])

            def c2_(c):
                di = c % 2
                cb = c % 2
                for k in range(31):
                    P.op("pe", I_mm(pC[:, cb, :], dg31[di][:, k, :], gluT[:, c, k:k + TBK], k == 0, k == 30),
                         reads=[B["dg31"][di], B["gluT"][c]], writes=[B["pC%d" % cb]])

            def c3_(c):
                cb = c % 2
                P.op("act", I_act(hidT[:, c, :], pC[:, cb, :], AF.Identity, bias=cvec[:, c:c + 1]),
                     reads=[B["pC%d" % cb], B["const"]], writes=[B["hidT"][c]])
                P.op("act", I_act(sqb[:], hidT[:, c, :], AF.Square), reads=[B["hidT"][c]], writes=[B["sqb"]])

            def c4_(c):
                P.op("pe", I_mm(pB[:, 0, :], onesb[:], hidT[:, c, :], c == 0, c == 7), reads=[B["hidT"][c], B["const"]], writes=[B["pB0"]])
                P.op("pe", I_mm(pB[:, 1, :], onesb[:], sqb[:], c == 0, c == 7), reads=[B["sqb"], B["const"]], writes=[B["pB1"]])

            sw_pipeline(8, [c0_, c1_, c2_, c3_, c4_])
            P.op("pool", I_copy(gluT[:, :, 0:30], gluT[:, :, TBK:TBK + 30]), reads=B["gluT"], writes=B["gluT"])
            P.op("act", I_acopy(f512[0][:], pB[:, 0, :]), reads=[B["pB0"]], writes=[B["f512"][0]])
            P.op("dve", I_copy(f512[1][:], pB[:, 1, :]), reads=[B["pB1"]], writes=[B["f512"][1]])
            P.op("dve", I_tt(f512[2][:], f512[0][:], f512[0][:], ALU.mult), reads=[B["f512"][0]], writes=[B["f512"][2]])
            P.op("dve", I_tt(f512[1][:], f512[1][:], f512[2][:], ALU.subtract), reads=[B["f512"][1], B["f512"][2]], writes=[B["f512"][1]])
            P.op("dve", I_ts(f512[1][:], f512[1][:], EPS, None, ALU.add), reads=[B["f512"][1]], writes=[B["f512"][1]])
            P.op("act", I_act(f512[1][:], f512[1][:], AF.Ln), reads=[B["f512"][1]], writes=[B["f512"][1]])
            P.op("act", I_act(f512[1][:], f512[1][:], AF.Exp, scale=-0.5), reads=[B["f512"][1]], writes=[B["f512"][1]])
            s = blk % 2
            for t in range(4):
                for kc in range(8):
                    P.op("pe", I_mm(p6[:, t * 16:(t + 1) * 16], uT[s][:, kc, t * 128:(t + 1) * 128], Wdt[:, kc, :], kc == 0, kc == 7),
                         reads=[B["uT"][s], B["Wdt"]], writes=[B["p6"]])
            P.op("dve", I_tt(dts[:, :, :], p6[:, 0:64].rearrange("p (t h) -> p t h", t=4), bc3(small[:, 16:32], 4, 1), ALU.add),
                 reads=[B["p6"], B["const"]], writes=[B["dts"]])
            P.op("act", I_act(dts[:, :, :], dts[:, :, :], AF.Exp), reads=[B["dts"]], writes=[B["dts"]])
            P.op("act", I_act(dts[:, :, :], dts[:, :, :], AF.Ln, bias=1.0), reads=[B["dts"]], writes=[B["dts"]])
            P.op("act", I_act(lndt[:, :, :], dts[:, :, :], AF.Ln), reads=[B["dts"]], writes=[B["dts"]])

            def ln_(c):
                a, b_, d_ = 3 + (c % 2), 5 + (c % 2), 7
                P.op("dve", I_tt(f512[a][:], hidT[:, c, :], f512[0][:], ALU.subtract), reads=[B["hidT"][c], B["f512"][0]], writes=[B["f512"][a]])
                P.op("dve", I_tt(f512[a][:], f512[a][:], f512[1][:], ALU.mult), reads=[B["f512"][a], B["f512"][1]], writes=[B["f512"][a]])
                P.op("act", I_act(f512[b_][:], f512[a][:], AF.Identity, scale=cvec[:, 8 + c:9 + c], bias=cvec[:, 16 + c:17 + c]),
                     reads=[B["f512"][a], B["const"]], writes=[B["f512"][b_]])
                P.op("act", I_act(f512[a][:], f512[b_][:], AF.Tanh), reads=[B["f512"][b_]], writes=[B["f512"][a]])
                P.op("dve", I_stt(hidT[:, c, :], f512[a][:], 1.0, f512[b_][:], ALU.add, ALU.mult),
                     reads=[B["f512"][a], B["f512"][b_]], writes=[B["hidT"][c]])

            def z_(c):
                hf = c % 2
                proj_cm(blk, 2048 + c * 128, hf)
                P.op("act", I_act(f512[2][:] if hf == 0 else f512[7][:], pA[:, hf, :], AF.Tanh, scale=0.5),
                     reads=[B["pA%d" % hf]], writes=[B["f512"][2 if hf == 0 else 7]])
                P.op("dve", I_stt(szT[:, c, :], f512[2][:] if hf == 0 else f512[7][:], 1.0, pA[:, hf, :], ALU.add, ALU.mult),
                     reads=[B["f512"][2 if hf == 0 else 7], B["pA%d" % hf]], writes=[B["szT"][c]])

            for c in range(8):
                z_(c)
                ln_(c)
            for t in range(4):
                P.dma("sp", I_dma(ysc[blk * 4 + t, :, 0:8, :], hidT[:, :, t * 128:(t + 1) * 128]), reads=B["hidT"])
            for c in range(12):
                hf = c % 2
                proj_cm(blk, 3072 + c * 128, hf)
                if hf == 0:
                    P.op("act", I_acopy(xbcT[:, c, 3:3 + TBK], pA[:, 0, :]), reads=[B["pA0"]], writes=[B["xbcT"][c]])
                else:
                    P.op("dve", I_copy(xbcT[:, c, 3:3 + TBK], pA[:, 1, :]), reads=[B["pA1"]], writes=[B["xbcT"][c]])
            for c in range(12):
                a = 3 + (c % 2)
                b_ = 5 + (c % 2)
                hf = c % 2
                for k in range(4):
                    P.op("pe", I_mm(pC[:, hf, :], dg4[:, c, k, :], xbcT[:, c, k:k + TBK], k == 0, k == 3),
                         reads=[B["dg4"], B["xbcT"][c]], writes=[B["pC%d" % hf]])
                P.op("act", I_act(f512[b_][:], pC[:, hf, :], AF.Identity, bias=cvec[:, 24 + c:25 + c]),
                     reads=[B["pC%d" % hf], B["const"]], writes=[B["f512"][b_]])
                P.op("act", I_act(f512[a][:], f512[b_][:], AF.Tanh), reads=[B["f512"][b_]], writes=[B["f512"][a]])
                P.op("dve", I_stt(XT[:, c, :], f512[a][:], 1.0, f512[b_][:], ALU.add, ALU.mult),
                     reads=[B["f512"][a], B["f512"][b_]], writes=[B["XT"][c]])

            def ssd_vars(t):
                q = (blk * 4 + t) % 2
                tsl = slice(t * 128, (t + 1) * 128)
                return q, tsl, xs_t[q], sm_t[q], None, xdtd_t[q], xsD_t[q], yt_t[q], B["sm"][q], dts[:, t, :]

            def ssd_A(t):
                q, tsl, xs, sm, xdt, xdtd, xsD, yt, bsm, dtc = ssd_vars(t)
                prefetch_load()
                for c in range(2):
                    P.op("pe", I_tr(p7[:, c, :], XT[:, 8 + c, tsl], C.ident[:]), reads=[B["XT"][8 + c], C.b_ident], writes=[B["p7"]])
                P.op("act", I_acopy(Btm[q][:, :].rearrange("p (c e) -> p c e", c=2), p7[:, 0:2, :]), reads=[B["p7"]], writes=[B["Btm"][q]])
                for c in range(8):
                    P.op("pe", I_tr(p7[:, c, :], XT[:, c, tsl], C.ident[:]), reads=[B["XT"][c], C.b_ident], writes=[B["p7"]])
                P.op("act", I_acopy(xs[:, :].rearrange("p (c e) -> p c e", c=8), p7[:, :, :]), reads=[B["p7"]], writes=[B["xs"][q]])
                P.op("dve", I_tt(sm[:, 0:16], dtc, small[:, 0:16], ALU.mult), reads=[B["dts"], B["const"]], writes=[bsm])
                P.op("pe", I_mm(p6[:, 256:272], trif[:], sm[:, 0:16], True, True), reads=[bsm, B["const"]], writes=[B["p6"]])
                P.op("pe", I_mm(p6[:, 272:288], onesf[:], sm[:, 0:16], True, True), reads=[bsm, B["const"]], writes=[B["p6"]])
                P.op("dve", I_copy(sm[:, 16:48], p6[:, 256:288]), reads=[B["p6"]], writes=[bsm])
                P.op("dve", I_tt(sm[:, 48:64], sm[:, 32:48], sm[:, 16:32], ALU.subtract), reads=[bsm], writes=[bsm])
                P.op("act", I_act(sm[:, 48:64], sm[:, 48:64], AF.Exp), reads=[bsm], writes=[bsm])
                P.op("dve", I_tt(sm[:, 64:80], sm[:, 48:64], dtc, ALU.mult), reads=[bsm, B["dts"]], writes=[bsm])
                P.op("act", I_act(sm[:, 80:112], sm[:, 16:48], AF.Exp), reads=[bsm], writes=[bsm])
                P.op("dve", I_stt(sm[:, 112:128], sm[:, 16:32], -1.0, lndt[:, t, :], ALU.mult, ALU.add),
                     reads=[bsm, B["dts"]], writes=[bsm])
                xs3 = xs[:, :].rearrange("p (h e) -> p h e", h=16)
                P.op("dve", I_tt(xdtd[:, :].rearrange("p (h e) -> p h e", h=16), xs3, bc3(sm[:, 64:80], 64, 2), ALU.mult),
                     reads=[B["xs"][q], bsm], writes=[B["xdtd"][q]])
                P.op("dve", I_tt(xsD[:, :].rearrange("p (h e) -> p h e", h=16), xs3, bc3(small[:, 32:48], 64, 2), ALU.mult),
                     reads=[B["xs"][q], B["const"]], writes=[B["xsD"][q]])
                for g in range(2):
                    P.op("pe", I_mm(pC[:, 1, g * 128:(g + 1) * 128], XT[:, 8 + g, tsl], XT[:, 10 + g, tsl], True, True),
                         reads=[B["XT"][8 + g], B["XT"][10 + g]], writes=[B["pC1"]])
                P.op("act", I_acopy(CBs_t[q][:], pC[:, 1, 0:256]), reads=[B["pC1"]], writes=[B["CBs"][q]])

            def ssd_B(t, part):
                q, tsl, xs, sm, xdt, xdtd, xsD, yt, bsm, dtc = ssd_vars(t)
                pB_flat = pB[:, :, :].rearrange("p a b -> p (a b)")
                if part == 0:
                    for g in range(2):
                        P.op("pe", I_mm(pA[:, g, :], XT[:, 10 + g, tsl], Sbf[:, g * 512:(g + 1) * 512], True, True),
                             reads=[B["XT"][10 + g], B["Sbf"]], writes=[B["pA%d" % g]])

                for g in ([0] if part == 0 else [1]):
                    gq = cnt["g"] % 2
                    cnt["g"] += 1
                    Rt, Lt, Mt = Rt_t[gq], Lt_t[gq], Mt_t[gq]
                    P.op("dve", I_tt(Rt[:, :, :], bc3(trif[:], 8, 1), bc3(sm[:, g * 8:(g + 1) * 8], 128, 2), ALU.mult),
                         reads=[bsm, B["const"]], writes=[B["Rt"][gq]])
                    for hf in range(2):
                        P.op("pe", I_mm(pC[:, hf, :], onesf[:], Rt[:, hf * 4:(hf + 1) * 4, :].rearrange("p h l -> p (h l)"), True, False),
                             reads=[B["Rt"][gq], B["const"]], writes=[B["pC%d" % hf]])
                        P.op("pe", I_mm(pC[:, hf, :], C.ident[:], maskrep[:, :, :].rearrange("p h l -> p (h l)"), False, True),
                             reads=[B["const"], C.b_ident], writes=[B["pC%d" % hf]])
                    for hf in range(2):
                        for hh in range(4):
                            hgl = g * 8 + hf * 4 + hh
                            P.op("act", I_act(Lt[:, hf * 4 + hh, :], pC[:, hf, hh * 128:(hh + 1) * 128], AF.Exp,
                                              bias=sm[:, 112 + hgl:113 + hgl]),
                                 reads=[B["pC%d" % hf], bsm], writes=[B["Lt"][gq]])
                    P.op("dve", I_tt(Mt[:, :, :], Lt[:, :, :], bc3(CBs_t[q][:, g * 128:(g + 1) * 128], 8, 1), ALU.mult),
                         reads=[B["Lt"][gq], B["CBs"][q]], writes=[B["Mt"][gq]])
                    for hh in range(8):
                        hgl = g * 8 + hh
                        P.op("pe", I_mm(pB[:, g, hh * 64:(hh + 1) * 64], Mt[:, hh, :], xs[:, hgl * 64:(hgl + 1) * 64], hh == 0, False),
                             reads=[B["Mt"][gq], B["xs"][q]], writes=[B["pB%d" % g]])
                    P.op("pe", I_mm(pB[:, g, :], C.ident[:], xsD[:, g * 512:(g + 1) * 512], False, True),
                         reads=[B["xsD"][q], C.b_ident], writes=[B["pB%d" % g]])
                if part == 0:
                    return
                yt3 = yt[:, :].rearrange("p (h e) -> p h e", h=16)
                P.op("dve", I_tt(yt3, pA[:, :, :].rearrange("p a (h e) -> p (a h) e", e=64), bc3(sm[:, 80:96], 64, 2), ALU.mult),
                     reads=[B["pA0"], B["pA1"], bsm], writes=[B["yt"][q]])
                P.op("dve", I_tt(yt[:], yt[:], pB_flat, ALU.add), reads=[B["yt"][q], B["pB0"], B["pB1"]], writes=[B["yt"][q]])
                for g in range(2):
                    P.op("pe", I_mm(pC[:, g, :], Btm[q][:, g * 128:(g + 1) * 128], xdtd[:, g * 512:(g + 1) * 512], True, True),
                         reads=[B["Btm"][q], B["xdtd"][q]], writes=[B["pC%d" % g]])
                S3 = S32[:, :].rearrange("p (h e) -> p h e", h=16)
                P.op("dve", I_tt(S3, S3, bc3(sm[:, 96:112], 64, 2), ALU.mult), reads=[B["S32"], bsm], writes=[B["S32"]])
                P.op("dve", I_tt(S32[:], S32[:], pC[:, :, :].rearrange("p a b -> p (a b)"), ALU.add),
                     reads=[B["S32"], B["pC0"], B["pC1"]], writes=[B["S32"]])
                P.op("act", I_acopy(Sbf[:], S32[:]), reads=[B["S32"]], writes=[B["Sbf"]])

            def ssd_C(t):
                q, tsl, xs, sm, xdt, xdtd, xsD, yt, bsm, dtc = ssd_vars(t)
                for c in range(8):
                    P.op("pe", I_tr(p7[:, c, :], szT[:, c, tsl], C.ident[:]), reads=[B["szT"][c], C.b_ident], writes=[B["p7"]])
                P.op("dve", I_tt(yt[:], yt[:], p7[:, :, :].rearrange("p c e -> p (c e)"), ALU.mult),
                     reads=[B["yt"][q], B["p7"]], writes=[B["yt"][q]])
                P.op("act", I_act(yb_t[q][:], yt[:], AF.Square, accum_out=stat[:, 4 + q:5 + q]),
                     reads=[B["yt"][q]], writes=[B["yb"][q], B["stat"]])
                rstd_pool(P, C, stat[:, 6 + q:7 + q], stat[:, 4 + q:5 + q], 0.25 / D, [B["stat"]], [B["stat"]], post=0.5)
                P.op("dve", I_stt(yb_t[q][:], yt[:], stat[:, 6 + q:7 + q], ngb[:], ALU.mult, ALU.mult),
                     reads=[B["yt"][q], B["stat"], B["g"]], writes=[B["yb"][q]])
                for c in range(8):
                    P.op("pe", I_tr(p7[:, c, :], yb_t[q][:, c * 128:(c + 1) * 128], C.ident[:]),
                         reads=[B["yb"][q], C.b_ident], writes=[B["p7"]])
                P.op("act", I_acopy(ybT_t[q][:, :, :], p7[:, :, :]), reads=[B["p7"]], writes=[B["ybT"][q]])
                P.dma("sp", I_dma(ysc[blk * 4 + t, :, 8:16, :], ybT_t[q][:, :, :]), reads=[B["ybT"][q]])
                prefetch_compute()


            ssd_A(0)
            for t in range(4):
                ssd_B(t, 0)
                if t + 1 < 4:
                    ssd_A(t + 1)
                ssd_B(t, 1)
                ssd_C(t)
            P.op("pool", I_copy(xbcT[:, :, 0:3], xbcT[:, :, TBK:TBK + 3]), reads=B["xbcT"], writes=B["xbcT"])
        P.barrier()
    with ExitStack() as st:
        Wout = sb(st, nc, tag + "Wout", [128, 16, D], BF16)
        gpost = sb(st, nc, tag + "gpost", [128, D], F32)
        yin = [sb(st, nc, tag + "yin%d" % i, [128, 16, 128], BF16) for i in range(3)]
        hres = [sb(st, nc, tag + "ohr%d" % i, [128, D], F32) for i in range(2)]
        mb = [sb(st, nc, tag + "omb%d" % i, [128, D], F32) for i in range(4)]
        junk = sb(st, nc, tag + "ojunk", [128, 512], BF16)
        stat = sb(st, nc, tag + "ostat", [128, 24], F32)
        ops_ = [ps(st, nc, tag + "oops%d" % i, [128, 512], F32) for i in range(4)]
        b_Wo, b_g, b_junk = P.buf("Wout"), P.buf("gpo"), P.buf("ojunk")
        b_yin = P.bufs("yin", 3)
        b_hres = P.bufs("ohr", 2)
        b_mb = P.bufs("omb", 4)
        b_st = P.bufs("ost", 4)
        b_ops = P.bufs("oops", 4)
        P.dma("pool", I_dma(Wout[:, :, :], w_out.rearrange("(c p) n -> p c n", p=128)), writes=[b_Wo])
        P.dma("sp", I_dma(gpost[:], g_post.partition_broadcast(128)), writes=[b_g])
        ntile = nblk * 4

        def o0(t):
            P.dma("sp", I_dma(yin[t % 3][:, :, :], ysc[t]), writes=[b_yin[t % 3]])

        def o1(t):
            pp = (t % 2) * 2
            for dh in range(2):
                for c in range(16):
                    P.op("pe", I_mm(ops_[pp + dh][:], yin[t % 3][:, c, :], Wout[:, c, dh * 512:(dh + 1) * 512], c == 0, c == 15),
                         reads=[b_yin[t % 3], b_Wo], writes=[b_ops[pp + dh]])

        def o2(t):
            i = t % 4
            pp = (t % 2) * 2
            post_part1(P, ops_[pp:pp + 2], b_ops[pp:pp + 2], mb, b_mb, junk, b_junk, stat, b_st, i)

        def o3(t):
            i = t % 4
            post_part2(P, C, mb, b_mb, hres, b_hres, stat, b_st, gpost, b_g, h, t * 128, i, 1.0, add_eng="dma")

        sw_pipeline(ntile, [o0, o1, o2, o3])
        P.barrier()


def copy_phase(P, C, x, out):
    nc = P.nc
    with ExitStack() as st:
        t = [sb(st, nc, "cp%d" % i, [128, D], F32) for i in range(2)]
        b = P.bufs("cp", 2)
        for k in range(NT):
            i = k % 2
            P.dma("sp", I_dma(t[i][:], x[k * 128:(k + 1) * 128, :]), writes=[b[i]])
            P.dma("sp", I_dma(out[k * 128:(k + 1) * 128, :], t[i][:]), reads=[b[i]])
        P.barrier()


def build(phases, debug=None):
    nc = bass.Bass("TRN2", target_bir_lowering=False)
    x = nc.dram_tensor("x", [L, D], F32, kind="ExternalInput").ap()
    norm_g = nc.dram_tensor("norm_g", [2, 6, D], F32, kind="ExternalInput").ap()
    ffn_w1 = nc.dram_tensor("ffn_w1", [2, 2, D, 2 * DFF], F32, kind="ExternalInput").ap()
    ffn_w2 = nc.dram_tensor("ffn_w2", [2, 2, DFF, D], F32, kind="ExternalInput").ap()
    hy = {}
    for nm, shp in (("hyb_w_in", [1, D, 4624]), ("conv_dw_w", [1, 31, D]), ("conv_dw_b", [1, D]), ("conv_ln_g", [1, D]),
                    ("conv_ln_b", [1, D]), ("ssm_conv_w", [1, 4, 1536]), ("ssm_conv_b", [1, 1536]), ("ssm_dt_bias", [1, 16]),
                    ("ssm_a_log", [1, 16]), ("ssm_d", [1, 16]), ("ssm_norm_g", [1, D]), ("hyb_w_out", [1, 2048, D])):
        hy[nm] = nc.dram_tensor(nm, shp, F32, kind="ExternalInput").ap()
    attn_w_qkv = nc.dram_tensor("attn_w_qkv", [1, D, 4608], F32, kind="ExternalInput").ap()
    attn_w_o = nc.dram_tensor("attn_w_o", [1, 512, D], F32, kind="ExternalInput").ap()
    out = nc.dram_tensor("out", [L, D], F32, kind="ExternalOutput").ap()
    acc = nc.dram_tensor("attn_acc", [3, L, 520], F32).ap()
    ysc = nc.dram_tensor("hyb_ysc", [NT, 128, 16, 128], BF16).ap()
    ubs = nc.dram_tensor("attn_ubf", [L, D], BF16).ap()
    P = Prog(nc)
    C = Ctx()
    with ExitStack() as stack:
        emit_consts(P, C, stack)
        P.barrier()
        for ph in phases:
            if ph[0] == "ffn":
                _, li, fi, src_is_x, nblk = ph
                ffn_phase(P, C, x if src_is_x else out, out, ffn_w1[li, fi], ffn_w2[li, fi],
                          norm_g[li, 0 if fi == 0 else 4], norm_g[li, 1 if fi == 0 else 5],
                          "f%d%d" % (li, fi), nblk=nblk, dbg=debug)
            elif ph[0] == "ffnc":
                jobs = []
                for (li, fi, src_is_x) in ph[1]:
                    jobs.append(dict(h_src=(x if src_is_x else out), h_dst=out, w1=ffn_w1[li, fi], w2=ffn_w2[li, fi],
                                     g_pre=norm_g[li, 0 if fi == 0 else 4], g_post=norm_g[li, 1 if fi == 0 else 5]))
                ffn_chain(P, C, jobs, "fc%d" % len(jobs) + "".join("%d%d" % (a, b_) for (a, b_, _) in ph[1]))
            elif ph[0] == "copy":
                copy_phase(P, C, x, out)
            elif ph[0] == "hyb":
                hyb_phase(P, C, out, hy["hyb_w_in"][0], hy["conv_dw_w"][0], hy["conv_dw_b"][0], hy["conv_ln_g"][0],
                          hy["conv_ln_b"][0], hy["ssm_conv_w"][0], hy["ssm_conv_b"][0], hy["ssm_dt_bias"][0],
                          hy["ssm_a_log"][0], hy["ssm_d"][0], hy["ssm_norm_g"][0], hy["hyb_w_out"][0],
                          norm_g[0, 2], norm_g[0, 3], ysc, **ph[1])
            elif ph[0] == "attn":
                attn_phase(P, C, out, attn_w_qkv[0], attn_w_o[0], norm_g[1, 2], norm_g[1, 3], acc, ubs, **ph[1])
        P.emit(stack)
    return nc, P


FULL_PHASES = [("ffnc", [(0, 0, True)]), ("hyb", {}), ("ffnc", [(0, 1, False), (1, 0, False)]),
               ("attn", {}), ("ffnc", [(1, 1, False)])]
_W_NAMES = ["norm_g", "ffn_w1", "ffn_w2", "hyb_w_in", "conv_dw_w", "conv_dw_b", "conv_ln_g", "conv_ln_b", "ssm_conv_w",
            "ssm_conv_b", "ssm_dt_bias", "ssm_a_log", "ssm_d", "ssm_norm_g", "hyb_w_out", "attn_w_qkv", "attn_w_o"]


def kernel(**inputs):
    x = np.ascontiguousarray(np.asarray(inputs["x"], dtype=np.float32))
    n = x.shape[0]
    nc, _ = build(FULL_PHASES)
    shared = {k: np.ascontiguousarray(np.asarray(inputs[k], dtype=np.float32)) for k in _W_NAMES}
    in_maps = []
    for i in range(n):
        m = dict(shared)
        m["x"] = x[i]
        in_maps.append(m)
    res = run_bass_kernel_spmd(nc, in_maps, core_ids=list(range(n)))
    return np.stack([np.asarray(r["out"]) for r in res.results], axis=0).astype(np.float32)
```

```python
import numpy as np
import concourse.bass as bass
import concourse.mybir as mybir
from concourse.bass_utils import run_bass_kernel_spmd
from contextlib import ExitStack

F32 = mybir.dt.float32
BF16 = mybir.dt.bfloat16
ALU = mybir.AluOpType
AF = mybir.ActivationFunctionType
AX = mybir.AxisListType

L = 4096
D = 1024
DFF = 2816
NT = L // 128
EPS = 1e-6
SEM_CH = 4000
N_DMA_SEM = 12


class Buf:
    __slots__ = ("name", "w", "r")

    def __init__(self, name):
        self.name = name
        self.w = []
        self.r = []


class Op:
    __slots__ = ("eng", "fn", "deps", "key", "sigval", "need_sig", "dma", "prev_dma", "barrier", "cost", "seq", "pos",
                 "succ", "prio", "fin", "ndep", "line")

    def __init__(self, eng, fn):
        self.eng = eng
        self.fn = fn
        self.deps = []
        self.key = eng
        self.sigval = 0
        self.need_sig = False
        self.dma = False
        self.prev_dma = None
        self.barrier = False
        self.cost = getattr(fn, "cost", 0.3) if fn is not None else 0.0
        self.seq = 0
        self.pos = 0


import heapq

SCHEDULE = True
SYNC_LAT = 0.25
DMA_ISSUE = {"sp": 0.15, "act": 0.15, "pool": 1.0, "pe": 0.15, "dve": 0.15}


class Prog:
    ENGS = ["pe", "act", "dve", "pool", "sp"]

    def __init__(self, nc):
        self.nc = nc
        self.ops = {e: [] for e in self.ENGS}
        self.all_bufs = []
        self.nseq = 0

    def buf(self, name):
        b = Buf(name)
        self.all_bufs.append(b)
        return b

    def bufs(self, name, n):
        return [self.buf("%s%d" % (name, i)) for i in range(n)]

    def _track(self, o, reads, writes):
        deps = {}
        for b in reads:
            for d in b.w:
                deps[id(d)] = d
        for b in writes:
            for d in b.w:
                deps[id(d)] = d
            for d in b.r:
                deps[id(d)] = d
        deps.pop(id(o), None)
        o.deps = list(deps.values())
        for b in reads:
            b.r.append(o)
        for b in writes:
            b.w = [o]
            b.r = []
        o.seq = self.nseq
        self.nseq += 1
        if getattr(self, "verbose", False):
            import sys as _s
            f = _s._getframe(2)
            o.line = f.f_lineno

    def op(self, eng, fn, reads=(), writes=()):
        o = Op(eng, fn)
        if eng == "pool":
            o.cost *= 5.0
        self._track(o, reads, writes)
        self.ops[eng].append(o)
        return o

    def dma(self, eng, fn, reads=(), writes=()):
        o = Op(eng, fn)
        o.dma = True
        self._track(o, reads, writes)
        self.ops[eng].append(o)
        return o

    def barrier(self):
        for e in self.ENGS:
            o = Op(e, None)
            o.barrier = True
            o.seq = self.nseq
            self.ops[e].append(o)
        self.nseq += 1
        for b in self.all_bufs:
            b.w = []
            b.r = []

    def _schedule(self, region):
        allops = sorted((o for e in self.ENGS for o in region[e]), key=lambda o: o.seq)
        inreg = set(id(o) for o in allops)
        for o in allops:
            o.succ = []
            o.fin = None
        for o in allops:
            o.ndep = 0
            for d in o.deps:
                if id(d) in inreg:
                    d.succ.append(o)
                    o.ndep += 1
        for o in reversed(allops):
            lat = o.cost if not o.dma else (2.0 + o.cost)
            o.prio = lat + max([s.prio for s in o.succ], default=0.0)
        for o in allops:
            if o.eng in ("sp", "pool"):
                o.prio = 1e9 - o.seq
        pipe = [0.0]
        free = {e: 0.0 for e in self.ENGS}
        future = {e: [] for e in self.ENGS}
        avail = {e: [] for e in self.ENGS}
        out = {e: [] for e in self.ENGS}

        def push(o):
            rt = 0.0
            for d in o.deps:
                if id(d) in inreg:
                    t = d.fin + (SYNC_LAT if (d.eng != o.eng or d.dma) else 0.05)
                    if t > rt:
                        rt = t
            heapq.heappush(future[o.eng], (rt, o.seq, o))

        for o in allops:
            if o.ndep == 0:
                push(o)
        n = len(allops)
        done = 0
        while done < n:
            best = None
            for e in self.ENGS:
                fu, av = future[e], avail[e]
                while fu and fu[0][0] <= free[e]:
                    rt, sq, o = heapq.heappop(fu)
                    heapq.heappush(av, (-o.prio, sq, o))
                if av:
                    cand = (free[e], 0, e)
                elif fu:
                    cand = (fu[0][0], 1, e)
                else:
                    continue
                if best is None or cand < best:
                    best = cand
            start, kind, e = best
            if kind == 0:
                _, _, o = heapq.heappop(avail[e])
            else:
                _, _, o = heapq.heappop(future[e])
            if o.dma:
                free[e] = start + DMA_ISSUE[e]
                t0 = max(start + 1.0, pipe[0])
                pipe[0] = t0 + o.cost
                o.fin = pipe[0] + 1.0
            else:
                free[e] = start + o.cost
                o.fin = free[e]
            out[e].append(o)
            done += 1
            for s in o.succ:
                s.ndep -= 1
                if s.ndep == 0:
                    push(s)
        self.sim_time = getattr(self, "sim_time", 0.0) + max(free.values())
        if getattr(self, "verbose", False) and n > 700 and hasattr(allops[0], "line"):
            cur = max(allops, key=lambda o: o.prio if o.eng not in ("sp", "pool") else -1)
            agg = {}
            while cur is not None:
                k = (cur.line, cur.eng)
                agg[k] = agg.get(k, 0.0) + (cur.cost if not cur.dma else 2.0 + cur.cost)
                nxt = [s for s in cur.succ if s.eng not in ("sp", "pool")] or cur.succ
                cur = max(nxt, key=lambda s: s.prio) if nxt else None
            print("  critical path by line:", sorted(((int(v), k) for k, v in agg.items()), reverse=True)[:14])
        if getattr(self, "verbose", False) and n > 200:
            cp = max((o.prio for o in allops if o.eng not in ("sp", "pool")), default=0.0)
            busy = {e: sum(o.cost for o in region[e] if not o.dma) for e in self.ENGS}
            print("region n=%d makespan=%.0f critpath=%.0f busy=%s" % (n, max(free.values()), cp, {k: int(v) for k, v in busy.items()}))
        return out

    def emit(self, stack):
        nc = self.nc
        nbar = sum(1 for o in self.ops["pe"] if o.barrier)
        regions = []
        cur = {e: [] for e in self.ENGS}
        idx = {e: 0 for e in self.ENGS}
        final = {e: [] for e in self.ENGS}
        for r in range(nbar + 1):
            reg = {e: [] for e in self.ENGS}
            bar = {}
            for e in self.ENGS:
                lst = self.ops[e]
                i = idx[e]
                while i < len(lst) and not lst[i].barrier:
                    reg[e].append(lst[i])
                    i += 1
                if i < len(lst):
                    bar[e] = lst[i]
                    i += 1
                idx[e] = i
            if SCHEDULE and sum(len(v) for v in reg.values()) > 1:
                reg = self._schedule(reg)
            for e in self.ENGS:
                final[e].extend(reg[e])
                if e in bar:
                    final[e].append(bar[e])
        self.ops = final
        dma_last = {}
        for e in self.ENGS:
            rr = 0
            for p, o in enumerate(self.ops[e]):
                o.pos = p
                if o.dma:
                    o.key = ("dma", e, rr)
                    rr = (rr + 1) % N_DMA_SEM
                    o.prev_dma = dma_last.get(o.key)
                    o.sigval = (o.prev_dma.sigval if o.prev_dma is not None else 0) + 16
                    dma_last[o.key] = o
        def compute_needs(o):
            need = {}
            for d in o.deps:
                if d.dma:
                    cur = need.get(d.key)
                    if cur is None or d.sigval > cur.sigval:
                        need[d.key] = d
                else:
                    if d.eng == "pe" and o.eng == "pe" and not o.dma:
                        continue
                    cur = need.get(d.eng)
                    if cur is None or d.pos > cur.pos:
                        need[d.eng] = d
            return need

        last_compute = {e: None for e in self.ENGS}
        bar_deps = {}
        ptr = {e: 0 for e in self.ENGS}
        dma_seen = {}
        for r in range(nbar):
            for e in self.ENGS:
                lst = self.ops[e]
                i = ptr[e]
                while not lst[i].barrier:
                    o = lst[i]
                    if o.dma:
                        dma_seen[o.key] = o
                    elif o.fn is not None:
                        last_compute[e] = o
                    i += 1
                bar_deps.setdefault(r, {})[e] = lst[i]
                ptr[e] = i + 1
            deps = [o for o in last_compute.values() if o is not None] + list(dma_seen.values())
            for e in self.ENGS:
                bar_deps[r][e].deps = list(deps)
        for e in self.ENGS:
            for o in self.ops[e]:
                if o.barrier:
                    for d in o.deps:
                        if not d.dma:
                            d.need_sig = True
                else:
                    for d in compute_needs(o).values():
                        if not d.dma:
                            d.need_sig = True
        cnt = {}
        for e in self.ENGS:
            c = 0
            for o in self.ops[e]:
                if o.dma or o.barrier:
                    continue
                if o.need_sig:
                    c += 1
                    o.sigval = c
            cnt[e] = c
        esems = {}
        for e in self.ENGS:
            n = max(1, (cnt[e] + SEM_CH - 1) // SEM_CH)
            esems[e] = [stack.enter_context(nc.semaphore("s_%s_%d" % (e, i))) for i in range(n)]
        dsems = {}
        for key in dma_last:
            dsems[key] = stack.enter_context(nc.semaphore("d_%s_%d" % (key[1], key[2])))
        self.n_sems = sum(len(v) for v in esems.values()) + len(dsems)
        self.n_inst = {e: len(self.ops[e]) for e in self.ENGS}
        self.n_sig = cnt

        def run_engine(ename, eng):
            seen = {}

            def wait(key, val):
                if val <= 0 or seen.get(key, 0) >= val:
                    return
                seen[key] = val
                if isinstance(key, tuple):
                    eng.wait_ge(dsems[key], val)
                else:
                    ch = (val - 1) // SEM_CH
                    eng.wait_ge(esems[key][ch], (val - 1) % SEM_CH + 1)

            for o in self.ops[ename]:
                if o.barrier:
                    for d in o.deps:
                        wait(d.key if d.dma else d.eng, d.sigval)
                    continue
                for k, d in compute_needs(o).items():
                    wait(k, d.sigval)
                if o.dma and o.prev_dma is not None:
                    wait(o.key, o.prev_dma.sigval)
                ins = o.fn(eng)
                if o.dma:
                    ins.then_inc(dsems[o.key], 16)
                elif o.need_sig:
                    ch = (o.sigval - 1) // SEM_CH
                    ins.then_inc(esems[ename][ch], 1)
            for key, o in dma_last.items():
                if key[1] == ename:
                    wait(key, o.sigval)

        block = stack.enter_context(nc.Block())

        @block.tensor
        def _(eng):
            run_engine("pe", eng)

        @block.scalar
        def _(eng):
            run_engine("act", eng)

        @block.vector
        def _(eng):
            run_engine("dve", eng)

        @block.gpsimd
        def _(eng):
            run_engine("pool", eng)

        @block.sync
        def _(eng):
            run_engine("sp", eng)


class Ctx:
    pass


def sb(stack, nc, name, shape, dt):
    return stack.enter_context(nc.sbuf_tensor(name, shape, dt))


def ps(stack, nc, name, shape, dt):
    return stack.enter_context(nc.psum_tensor(name, shape, dt))


def _fsz(ap):
    n = 1
    for s in ap.shape[1:]:
        n *= s
    return n


def _wc(f, cost):
    f.cost = cost
    return f


def I_mm(out, lhsT, rhs, start, stop):
    n = _fsz(rhs)
    c = 0.004 + max(n, 64) / 2400.0
    if rhs.dtype == F32:
        c *= 4
    return _wc(lambda e: e.matmul(out, lhsT=lhsT, rhs=rhs, start=start, stop=stop), c)


def I_tr(out, in_, ident):
    return _wc(lambda e: e.transpose(out=out, in_=in_, identity=ident), 0.1)


def I_act(out, in_, func, **kw):
    c = 0.25 + _fsz(out) / 1200.0 + (0.1 if "accum_out" in kw else 0.0)
    return _wc(lambda e: e.activation(out=out, in_=in_, func=func, **kw), c)


def I_tt(out, in0, in1, op):
    return _wc(lambda e: e.tensor_tensor(out=out, in0=in0, in1=in1, op=op), 0.12 + _fsz(out) / 960.0)


def I_ts(out, in0, s1, s2, op0, op1=None):
    c = 0.12 + _fsz(out) / 960.0
    if op1 is None:
        return _wc(lambda e: e.tensor_scalar(out=out, in0=in0, scalar1=s1, scalar2=None, op0=op0), c)
    return _wc(lambda e: e.tensor_scalar(out=out, in0=in0, scalar1=s1, scalar2=s2, op0=op0, op1=op1), c)


def I_stt(out, in0, scalar, in1, op0, op1):
    return _wc(lambda e: e.scalar_tensor_tensor(out=out, in0=in0, scalar=scalar, in1=in1, op0=op0, op1=op1),
               0.12 + _fsz(out) / 960.0)


def I_copy(out, in_):
    return _wc(lambda e: e.tensor_copy(out=out, in_=in_), 0.12 + _fsz(out) / 960.0)


def I_acopy(out, in_):
    return _wc(lambda e: e.copy(out=out, in_=in_), 0.25 + _fsz(out) / 1200.0)


def I_dma(out, in_):
    nbytes = 128 * _fsz(out) * (4 if (in_.dtype == F32 or out.dtype == F32) else 2)
    return _wc(lambda e: e.dma_start(out=out, in_=in_), nbytes / 200e3)


def I_memset(ap, v):
    return _wc(lambda e: e.memset(ap, v), 0.2 + _fsz(ap) / 1000.0)


def emit_consts(P, C, stack):
    nc = P.nc
    C.ident = sb(stack, nc, "ident", [128, 128], BF16)
    C.identf = sb(stack, nc, "identf", [128, 128], F32)
    C.negh = sb(stack, nc, "negh", [128, 1], F32)
    C.b_ident = P.buf("ident")
    P.op("pool", I_memset(C.identf[:], 1.0), writes=[C.b_ident])
    P.op("pool", lambda e: e.affine_select(out=C.identf[:], in_=C.identf[:], pattern=[[-1, 128]],
                                           compare_op=ALU.is_equal, fill=0.0, base=0, channel_multiplier=1),
         reads=[C.b_ident], writes=[C.b_ident])
    P.op("pool", I_copy(C.ident[:], C.identf[:]), reads=[C.b_ident], writes=[C.b_ident])
    P.op("pool", I_memset(C.negh[:], -0.5), writes=[C.b_ident])


def rstd_pool(P, C, out_ap, in_ap, scale, rbufs, wbufs, post=1.0):
    k = 1.0 / (post * post)
    P.op("pool", I_ts(out_ap, in_ap, scale * k, EPS * k, ALU.mult, ALU.add), reads=rbufs, writes=wbufs)
    P.op("pool", I_tt(out_ap, out_ap, C.negh[:, 0:1], ALU.pow), reads=list(wbufs) + [C.b_ident], writes=wbufs)


def ffn_phase(P, C, h_src, h_dst, w1, w2, g_pre, g_post, tag, nblk=4, dbg=None):
    nc = P.nc
    TB = 1024
    TPB = TB // 128
    NJ = DFF // 128
    NJ2 = NJ // 2
    with ExitStack() as st:
        gpre = sb(st, nc, tag + "gpre", [128, D], F32)
        gpost = sb(st, nc, tag + "gpost", [128, D], F32)
        hbuf = [sb(st, nc, tag + "hb%d" % i, [128, D], F32) for i in range(2)]
        hres = [sb(st, nc, tag + "hr%d" % i, [128, D], F32) for i in range(2)]
        ubf = [sb(st, nc, tag + "ub%d" % i, [128, D], BF16) for i in range(2)]
        uT = [sb(st, nc, tag + "uT%d" % i, [128, 8, TB], BF16) for i in range(2)]
        w1s = [sb(st, nc, tag + "w1s%d" % i, [128, 8, 2, 256], BF16) for i in range(3)]
        w2s = sb(st, nc, tag + "w2s", [128, NJ, D], BF16)
        actT = sb(st, nc, tag + "actT", [128, NJ, TB], BF16)
        sg = [sb(st, nc, tag + "sg%d" % i, [128, 512], F32) for i in range(2)]
        mb = [sb(st, nc, tag + "mb%d" % i, [128, D], F32) for i in range(2)]
        junk = sb(st, nc, tag + "junk", [128, D], BF16)
        stat = sb(st, nc, tag + "stat", [128, 24], F32)
        tp = ps(st, nc, tag + "tp", [128, 8, 128], BF16)
        gps = [ps(st, nc, tag + "gps%d" % i, [128, 512], F32) for i in range(2)]
        ups = [ps(st, nc, tag + "ups%d" % i, [128, 512], F32) for i in range(2)]
        ops_ = [ps(st, nc, tag + "ops%d" % i, [128, 512], F32) for i in range(2)]

        b_g = P.buf("g")
        b_hbuf = P.bufs("hbuf", 2)
        b_hres = P.bufs("hres", 2)
        b_ubf = P.bufs("ubf", 2)
        b_uT = P.bufs("uT", 4)
        b_w1 = P.bufs("w1s", 3)
        b_w2 = P.bufs("w2s", NJ)
        b_act = P.bufs("actT", NJ * 2)
        b_sg = P.bufs("sg", 2)
        b_mb = P.bufs("mb", 2)
        b_junk = P.buf("junk")
        b_st1 = P.bufs("st1", 2)
        b_st2 = P.bufs("st2", 2)
        b_tp = P.buf("tp")
        b_gps = P.bufs("gps", 2)
        b_ups = P.bufs("ups", 2)
        b_ops = P.bufs("ops", 2)

        P.dma("sp", I_dma(gpre[:], g_pre.partition_broadcast(128)), writes=[b_g])
        P.dma("sp", I_dma(gpost[:], g_post.partition_broadcast(128)), writes=[b_g])

        cnt = {"prep": 0, "w1": 0, "gu": 0, "ep": 0}
        w1v = w1.rearrange("(kc p) (two f) -> p kc two f", p=128, two=2)

        def prep_load(b, t):
            i = cnt["prep"] % 2
            cnt["prep"] += 1
            tok0 = b * TB + t * 128
            P.dma("sp", I_dma(hbuf[i][:], h_src[tok0:tok0 + 128, :]), writes=[b_hbuf[i]])
            return i

        def prep_norm(b, t, i=None):
            if i is None:
                i = prep_load(b, t)
            P.op("act", I_act(junk[:], hbuf[i][:], AF.Square, accum_out=stat[:, i:i + 1]),
                 reads=[b_hbuf[i]], writes=[b_junk, b_st1[i]])
            rstd_pool(P, C, stat[:, 2 + i:3 + i], stat[:, i:i + 1], 1.0 / D, [b_st1[i]], [b_st1[i]])
            P.op("dve", I_stt(ubf[i][:], hbuf[i][:], stat[:, 2 + i:3 + i], gpre[:], ALU.mult, ALU.mult),
                 reads=[b_hbuf[i], b_st1[i], b_g], writes=[b_ubf[i]])
            return i

        def prep_tr(b, t, i):
            s = b % 2
            for kc in range(8):
                P.op("pe", I_tr(tp[:, kc, :], ubf[i][:, kc * 128:(kc + 1) * 128], C.ident[:]),
                     reads=[b_ubf[i], C.b_ident], writes=[b_tp])
            P.op("act", I_acopy(uT[s][:, :, t * 128:(t + 1) * 128], tp[:, :, :]),
                 reads=[b_tp], writes=[b_uT[s * 2 + t // 4]])

        def load_w1(jp):
            i = cnt["w1"] % 3
            cnt["w1"] += 1
            for two in range(2):
                P.dma("pool", I_dma(w1s[i][:, :, two, :], w1v[:, :, two, jp * 256:(jp + 1) * 256]), writes=[b_w1[i]])
            return i

        b_pace = P.bufs("pace", NJ)

        def load_w2(j):
            P.dma("pool", I_dma(w2s[:, j, :], w2[j * 128:(j + 1) * 128, :]), reads=([b_pace[j - 3]] if j >= 3 else []),
                  writes=[b_w2[j]])

        def first_j(b, j, wi):
            s = b % 2
            jo = (j % 2) * 128
            for half in range(2):
                q = cnt["gu"] % 2
                cnt["gu"] += 1
                for kc in range(8):
                    P.op("pe", I_mm(gps[q][:], w1s[wi][:, kc, 0, jo:jo + 128],
                                    uT[s][:, kc, half * 512:(half + 1) * 512], kc == 0, kc == 7),
                         reads=[b_w1[wi], b_uT[s * 2 + half]], writes=[b_gps[q]])
                for kc in range(8):
                    P.op("pe", I_mm(ups[q][:], w1s[wi][:, kc, 1, jo:jo + 128],
                                    uT[s][:, kc, half * 512:(half + 1) * 512], kc == 0, kc == 7),
                         reads=[b_w1[wi], b_uT[s * 2 + half]], writes=[b_ups[q]] + ([b_pace[j]] if (b == 0 and half == 1 and kc == 7) else []))
                P.op("act", I_act(sg[q][:], gps[q][:], AF.Silu), reads=[b_gps[q]], writes=[b_sg[q]])
                P.op("dve", I_tt(actT[:, j, half * 512:(half + 1) * 512], ups[q][:], sg[q][:], ALU.mult),
                     reads=[b_ups[q], b_sg[q]], writes=[b_act[j * 2 + half]])

        def second_tile(b, t):
            i = cnt["ep"] % 2
            cnt["ep"] += 1
            tok0 = b * TB + t * 128
            half = t // 4
            P.dma("sp", I_dma(hres[i][:], h_src[tok0:tok0 + 128, :]), writes=[b_hres[i]])
            for dh in range(2):
                for j in range(NJ):
                    P.op("pe", I_mm(ops_[dh][:], actT[:, j, t * 128:(t + 1) * 128],
                                    w2s[:, j, dh * 512:(dh + 1) * 512], j == 0, j == NJ - 1),
                         reads=[b_act[j * 2 + half], b_w2[j]], writes=[b_ops[dh]])
                c0 = 4 + 2 * i + dh
                P.op("dve", I_copy(mb[i][:, dh * 512:(dh + 1) * 512], ops_[dh][:]),
                     reads=[b_ops[dh]], writes=[b_mb[i]])
                P.op("act", I_act(junk[:, 0:512], mb[i][:, dh * 512:(dh + 1) * 512], AF.Square,
                                  accum_out=stat[:, c0:c0 + 1]),
                     reads=[b_mb[i]], writes=[b_junk, b_st2[i]])
            if dbg == "mm2":
                return
            P.op("pool", I_tt(stat[:, 16 + i:17 + i], stat[:, 4 + 2 * i:5 + 2 * i], stat[:, 5 + 2 * i:6 + 2 * i], ALU.add),
                 reads=[b_st2[i]], writes=[b_st2[i]])
            rstd_pool(P, C, stat[:, 16 + i:17 + i], stat[:, 16 + i:17 + i], 1.0 / D, [b_st2[i]], [b_st2[i]], post=0.5)
            if dbg == "ep1":
                return
            P.op("dve", I_stt(mb[i][:], mb[i][:], stat[:, 16 + i:17 + i], gpost[:], ALU.mult, ALU.mult),
                 reads=[b_mb[i], b_st2[i], b_g], writes=[b_mb[i]])
            if dbg == "ep2":
                return
            P.op("dve", I_tt(mb[i][:], mb[i][:], hres[i][:], ALU.add),
                 reads=[b_mb[i], b_hres[i]], writes=[b_mb[i]])
            if dbg == "ep3":
                return
            P.dma("sp", I_dma(h_dst[tok0:tok0 + 128, :], mb[i][:]), reads=[b_mb[i]])

        slot0 = {}
        sw_pipeline(TPB, [lambda t: slot0.__setitem__(t, prep_load(0, t)),
                          lambda t: prep_norm(0, t, slot0[t]),
                          lambda t: prep_tr(0, t, slot0[t])])
        if dbg == "prep":
            P.barrier()
            return
        w2_loaded = False
        for b in range(nblk):
            pend = []
            wq = [load_w1(0)]
            for j in range(NJ):
                if j % 2 == 0 and j // 2 + 1 < NJ2:
                    wq.append(load_w1(j // 2 + 1))
                if not w2_loaded:
                    load_w2(j)
                if dbg == "w":
                    continue
                first_j(b, j, wq[j // 2])
                if b + 1 < nblk:
                    if j % 2 == 0 and j // 2 < TPB:
                        pend.append((j // 2, prep_norm(b + 1, j // 2)))
                    if j % 2 == 1 and pend:
                        t, i = pend.pop(0)
                        prep_tr(b + 1, t, i)
            w2_loaded = True
            if dbg in ("w", "first"):
                P.barrier()
                return
            for t in range(TPB):
                second_tile(b, t)
        P.barrier()


def ffn_chain(P, C, jobs, tag, nblk=4):
    nc = P.nc
    TB = 1024
    TPB = TB // 128
    NJ = DFF // 128
    NJ2 = NJ // 2
    nj = len(jobs)
    with ExitStack() as st:
        gpre = [sb(st, nc, tag + "gpre%d" % k, [128, D], F32) for k in range(nj)]
        gpost = [sb(st, nc, tag + "gpost%d" % k, [128, D], F32) for k in range(nj)]
        hbuf = [sb(st, nc, tag + "hb%d" % i, [128, D], F32) for i in range(2)]
        hres = [sb(st, nc, tag + "hr%d" % i, [128, D], F32) for i in range(2)]
        ubf = [sb(st, nc, tag + "ub%d" % i, [128, D], BF16) for i in range(2)]
        uT = [sb(st, nc, tag + "uT%d" % i, [128, 8, TB], BF16) for i in range(2)]
        w1s = [sb(st, nc, tag + "w1s%d" % i, [128, 8, 2, 256], BF16) for i in range(4)]
        w2s = sb(st, nc, tag + "w2s", [128, NJ, D], BF16)
        actT = sb(st, nc, tag + "actT", [128, NJ, TB], BF16)
        sg = [sb(st, nc, tag + "sg%d" % i, [128, 512], F32) for i in range(2)]
        mb = [sb(st, nc, tag + "mb%d" % i, [128, D], F32) for i in range(2)]
        junk = sb(st, nc, tag + "junk", [128, D], BF16)
        stat = sb(st, nc, tag + "stat", [128, 24], F32)
        tp = ps(st, nc, tag + "tp", [128, 8, 128], BF16)
        gps = [ps(st, nc, tag + "gps%d" % i, [128, 512], F32) for i in range(2)]
        ups = [ps(st, nc, tag + "ups%d" % i, [128, 512], F32) for i in range(2)]
        ops_ = [ps(st, nc, tag + "ops%d" % i, [128, 512], F32) for i in range(2)]

        b_g = P.bufs("g", nj)
        b_hd = P.bufs("hdram", NT)
        b_hbuf = P.bufs("hbuf", 2)
        b_hres = P.bufs("hres", 2)
        b_ubf = P.bufs("ubf", 2)
        b_uT = P.bufs("uT", 4)
        b_w1 = P.bufs("w1s", 4)
        b_w2 = P.bufs("w2s", NJ)
        b_act = P.bufs("actT", NJ * 2)
        b_sg = P.bufs("sg", 2)
        b_mb = P.bufs("mb", 2)
        b_junk = P.buf("junk")
        b_st1 = P.bufs("st1", 2)
        b_st2 = P.bufs("st2", 2)
        b_tp = P.buf("tp")
        b_gps = P.bufs("gps", 2)
        b_ups = P.bufs("ups", 2)
        b_ops = P.bufs("ops", 2)
        b_pace = [P.bufs("pace%d" % k, NJ) for k in range(nj)]

        for k, jb in enumerate(jobs):
            P.dma("sp", I_dma(gpre[k][:], jb["g_pre"].partition_broadcast(128)), writes=[b_g[k]])
            P.dma("sp", I_dma(gpost[k][:], jb["g_post"].partition_broadcast(128)), writes=[b_g[k]])

        cnt = {"prep": 0, "w1": 0, "gu": 0, "ep": 0}
        w1v = [jb["w1"].rearrange("(kc p) (two f) -> p kc two f", p=128, two=2) for jb in jobs]

        def prep_load(gb, t):
            k, b = gb // nblk, gb % nblk
            i = cnt["prep"] % 2
            cnt["prep"] += 1
            tok0 = b * TB + t * 128
            P.dma("sp", I_dma(hbuf[i][:], jobs[k]["h_src"][tok0:tok0 + 128, :]), reads=[b_hd[b * TPB + t]], writes=[b_hbuf[i]])
            return i

        def prep_norm(gb, t, i=None):
            k = gb // nblk
            if i is None:
                i = prep_load(gb, t)
            P.op("act", I_act(junk[:], hbuf[i][:], AF.Square, accum_out=stat[:, i:i + 1]),
                 reads=[b_hbuf[i]], writes=[b_junk, b_st1[i]])
            rstd_pool(P, C, stat[:, 2 + i:3 + i], stat[:, i:i + 1], 1.0 / D, [b_st1[i]], [b_st1[i]])
            P.op("dve", I_stt(ubf[i][:], hbuf[i][:], stat[:, 2 + i:3 + i], gpre[k][:], ALU.mult, ALU.mult),
                 reads=[b_hbuf[i], b_st1[i], b_g[k]], writes=[b_ubf[i]])
            return i

        def prep_tr(gb, t, i):
            s = gb % 2
            for kc in range(8):
                P.op("pe", I_tr(tp[:, kc, :], ubf[i][:, kc * 128:(kc + 1) * 128], C.ident[:]),
                     reads=[b_ubf[i], C.b_ident], writes=[b_tp])
            P.op("act", I_acopy(uT[s][:, :, t * 128:(t + 1) * 128], tp[:, :, :]),
                 reads=[b_tp], writes=[b_uT[s * 2 + t // 4]])

        def load_w1(k, jp):
            i = cnt["w1"] % 4
            cnt["w1"] += 1
            for two in range(2):
                P.dma("pool", I_dma(w1s[i][:, :, two, :], w1v[k][:, :, two, jp * 256:(jp + 1) * 256]), writes=[b_w1[i]])
            return i

        def load_w2(k, j):
            P.dma("pool", I_dma(w2s[:, j, :], jobs[k]["w2"][j * 128:(j + 1) * 128, :]),
                  reads=([b_pace[k][j - 3]] if j >= 3 else []), writes=[b_w2[j]])

        def first_j(gb, j, wi):
            k, b = gb // nblk, gb % nblk
            s = gb % 2
            jo = (j % 2) * 128
            for half in range(2):
                q = cnt["gu"] % 2
                cnt["gu"] += 1
                for kc in range(8):
                    P.op("pe", I_mm(gps[q][:], w1s[wi][:, kc, 0, jo:jo + 128],
                                    uT[s][:, kc, half * 512:(half + 1) * 512], kc == 0, kc == 7),
                         reads=[b_w1[wi], b_uT[s * 2 + half]], writes=[b_gps[q]])
                for kc in range(8):
                    P.op("pe", I_mm(ups[q][:], w1s[wi][:, kc, 1, jo:jo + 128],
                                    uT[s][:, kc, half * 512:(half + 1) * 512], kc == 0, kc == 7),
                         reads=[b_w1[wi], b_uT[s * 2 + half]],
                         writes=[b_ups[q]] + ([b_pace[k][j]] if (b == 0 and half == 1 and kc == 7) else []))
                P.op("act", I_act(sg[q][:], gps[q][:], AF.Silu), reads=[b_gps[q]], writes=[b_sg[q]])
                P.op("dve", I_tt(actT[:, j, half * 512:(half + 1) * 512], ups[q][:], sg[q][:], ALU.mult),
                     reads=[b_ups[q], b_sg[q]], writes=[b_act[j * 2 + half]])

        def second_tile(gb, t):
            k, b = gb // nblk, gb % nblk
            i = cnt["ep"] % 2
            cnt["ep"] += 1
            tok0 = b * TB + t * 128
            tile = b * TPB + t
            half = t // 4
            P.dma("sp", I_dma(hres[i][:], jobs[k]["h_src"][tok0:tok0 + 128, :]), reads=[b_hd[tile]], writes=[b_hres[i]])
            for dh in range(2):
                for j in range(NJ):
                    P.op("pe", I_mm(ops_[dh][:], actT[:, j, t * 128:(t + 1) * 128],
                                    w2s[:, j, dh * 512:(dh + 1) * 512], j == 0, j == NJ - 1),
                         reads=[b_act[j * 2 + half], b_w2[j]], writes=[b_ops[dh]])
                c0 = 4 + 2 * i + dh
                P.op("dve", I_copy(mb[i][:, dh * 512:(dh + 1) * 512], ops_[dh][:]),
                     reads=[b_ops[dh]], writes=[b_mb[i]])
                P.op("act", I_act(junk[:, 0:512], mb[i][:, dh * 512:(dh + 1) * 512], AF.Square,
                                  accum_out=stat[:, c0:c0 + 1]),
                     reads=[b_mb[i]], writes=[b_junk, b_st2[i]])
            P.op("pool", I_tt(stat[:, 16 + i:17 + i], stat[:, 4 + 2 * i:5 + 2 * i], stat[:, 5 + 2 * i:6 + 2 * i], ALU.add),
                 reads=[b_st2[i]], writes=[b_st2[i]])
            rstd_pool(P, C, stat[:, 16 + i:17 + i], stat[:, 16 + i:17 + i], 1.0 / D, [b_st2[i]], [b_st2[i]], post=0.5)
            P.op("dve", I_stt(mb[i][:], mb[i][:], stat[:, 16 + i:17 + i], gpost[k][:], ALU.mult, ALU.mult),
                 reads=[b_mb[i], b_st2[i], b_g[k]], writes=[b_mb[i]])
            P.op("dve", I_tt(mb[i][:], mb[i][:], hres[i][:], ALU.add),
                 reads=[b_mb[i], b_hres[i]], writes=[b_mb[i]])
            P.dma("sp", I_dma(jobs[k]["h_dst"][tok0:tok0 + 128, :], mb[i][:]), reads=[b_mb[i]], writes=[b_hd[tile]])

        slot0 = {}
        sw_pipeline(TPB, [lambda t: slot0.__setitem__(t, prep_load(0, t)),
                          lambda t: prep_norm(0, t, slot0[t]),
                          lambda t: prep_tr(0, t, slot0[t])])
        ngb = nj * nblk
        for gb in range(ngb):
            k, b = gb // nblk, gb % nblk
            pend = []
            wq = [load_w1(k, 0)]
            for j in range(NJ):
                if j % 2 == 0 and j // 2 + 1 < NJ2:
                    wq.append(load_w1(k, j // 2 + 1))
                if b == 0:
                    load_w2(k, j)
                first_j(gb, j, wq[j // 2])
                if gb + 1 < ngb:
                    if j % 2 == 0 and j // 2 < TPB:
                        pend.append((j // 2, prep_norm(gb + 1, j // 2)))
                    if j % 2 == 1 and pend:
                        t, i = pend.pop(0)
                        prep_tr(gb + 1, t, i)
            for t in range(TPB):
                second_tile(gb, t)
        P.barrier()


def sw_pipeline(n, stages):
    ns = len(stages)
    for step in range(n + ns - 1):
        for k in reversed(range(ns)):
            t = step - k
            if 0 <= t < n:
                stages[k](t)


def norm_to_uT(P, C, st, nc, tag, h_src, g_dram, uT_all, b_uT, ntiles=NT, rows=None, tp_ext=None):
    with ExitStack() as s2:
        gbc = sb(s2, nc, tag + "gbc", [128, D], F32)
        hb = [sb(s2, nc, tag + "nhb%d" % i, [128, D], F32) for i in range(2)]
        ub = [sb(s2, nc, tag + "nub%d" % i, [128, D], BF16) for i in range(2)]
        stat = sb(s2, nc, tag + "nstat", [128, 8], F32)
        if tp_ext is None:
            tp = [ps(s2, nc, tag + "ntp%d" % i, [128, 8, 128], BF16) for i in range(2)]
            b_tp = P.bufs("ntp", 2)
        else:
            tp, b_tp = tp_ext
        b_g = P.buf("g")
        b_hb = P.bufs("nhb", 2)
        b_ub = P.bufs("nub", 2)
        b_st = P.bufs("nst", 2)
        P.dma("sp", I_dma(gbc[:], g_dram.partition_broadcast(128)), writes=[b_g])

        def s_load(t):
            i = t % 2
            src_rows = h_src[t * 128:(t + 1) * 128, :] if rows is None else h_src[rows(t), :]
            P.dma("sp", I_dma(hb[i][:], src_rows), writes=[b_hb[i]])

        def s_stat(t):
            i, j = t % 2, t % 2
            P.op("act", I_act(ub[j][:], hb[i][:], AF.Square, accum_out=stat[:, j:j + 1]),
                 reads=[b_hb[i]], writes=[b_ub[j], b_st[j]])
            rstd_pool(P, C, stat[:, 2 + j:3 + j], stat[:, j:j + 1], 1.0 / D, [b_st[j]], [b_st[j]])

        def s_scale(t):
            i, j = t % 2, t % 2
            P.op("dve", I_stt(ub[j][:], hb[i][:], stat[:, 2 + j:3 + j], gbc[:], ALU.mult, ALU.mult),
                 reads=[b_hb[i], b_st[j], b_g], writes=[b_ub[j]])

        def s_tr(t):
            j = t % 2
            for kc in range(8):
                P.op("pe", I_tr(tp[j][:, kc, :], ub[j][:, kc * 128:(kc + 1) * 128], C.ident[:]),
                     reads=[b_ub[j], C.b_ident], writes=[b_tp[j]])

        def s_copy(t):
            j = t % 2
            P.op("act", I_acopy(uT_all[:, :, t * 128:(t + 1) * 128], tp[j][:, :, :]),
                 reads=[b_tp[j]], writes=[b_uT])

        sw_pipeline(ntiles, [s_load, s_stat, s_scale, s_tr, s_copy])
        P.barrier()


def I_dma_acc(out, in_):
    nbytes = 128 * _fsz(in_) * 4
    return _wc(lambda e: e.dma_start(out=out, in_=in_, accum_op=ALU.add), 2 * nbytes / 200e3)


def post_part1(P, ops_, b_ops, mb, b_mb, junk, b_junk, stat, b_st, i, copy_eng="dve"):
    for dh in range(2):
        c0 = 4 + 2 * i + dh
        if copy_eng == "dve":
            P.op("dve", I_copy(mb[i][:, dh * 512:(dh + 1) * 512], ops_[dh][:]), reads=[b_ops[dh]], writes=[b_mb[i]])
        else:
            P.op("act", I_acopy(mb[i][:, dh * 512:(dh + 1) * 512], ops_[dh][:]), reads=[b_ops[dh]], writes=[b_mb[i]])
        P.op("act", I_act(junk[:, 0:512], mb[i][:, dh * 512:(dh + 1) * 512], AF.Square, accum_out=stat[:, c0:c0 + 1]),
             reads=[b_mb[i]], writes=[b_junk, b_st[i]])


def post_part2(P, C, mb, b_mb, hres, b_hres, stat, b_st, gpost, b_g, h_dst, tok0, i, post, add_eng="dve"):
    P.op("pool", I_tt(stat[:, 16 + i:17 + i], stat[:, 4 + 2 * i:5 + 2 * i], stat[:, 5 + 2 * i:6 + 2 * i], ALU.add),
         reads=[b_st[i]], writes=[b_st[i]])
    rstd_pool(P, C, stat[:, 16 + i:17 + i], stat[:, 16 + i:17 + i], 1.0 / D, [b_st[i]], [b_st[i]], post=post)
    P.op("dve", I_stt(mb[i][:], mb[i][:], stat[:, 16 + i:17 + i], gpost[:], ALU.mult, ALU.mult),
         reads=[b_mb[i], b_st[i], b_g], writes=[b_mb[i]])
    if add_eng == "dma":
        P.dma("pool", I_dma_acc(h_dst[tok0:tok0 + 128, :], mb[i][:]), reads=[b_mb[i]])
        return
    P.op(add_eng, I_tt(mb[i][:], mb[i][:], hres[i][:], ALU.add), reads=[b_mb[i], b_hres[i]], writes=[b_mb[i]])
    P.dma("sp", I_dma(h_dst[tok0:tok0 + 128, :], mb[i][:]), reads=[b_mb[i]])


def post_residual(P, C, tag_bufs, ops_, b_ops, mb, b_mb, hres, b_hres, junk, b_junk, stat, b_st, gpost, b_g,
                  h_src, h_dst, tok0, i, post, add_eng="dve"):
    P.dma("sp", I_dma(hres[i][:], h_src[tok0:tok0 + 128, :]), writes=[b_hres[i]])
    post_part1(P, ops_, b_ops, mb, b_mb, junk, b_junk, stat, b_st, i)
    post_part2(P, C, mb, b_mb, hres, b_hres, stat, b_st, gpost, b_g, h_dst, tok0, i, post, add_eng)


ATT_PATTERNS = ((128, 1), (512, 4), (2048, 16))
NEG = -30000.0


def attn_phase(P, C, h, w_qkv, w_o, g_pre, g_post, acc, ubs, tag="at", groups=(0, 1, 2), do_merge=True):
    nc = P.nc
    with ExitStack() as st:
        uT = sb(st, nc, tag + "uT", [128, 8, L], BF16)
        b_uT = P.bufs("uTr", 8)
        b_acc = P.buf("acc_dram")
        with ExitStack() as s1:
            gbc = sb(s1, nc, tag + "gbc", [128, D], F32)
            hb = [sb(s1, nc, tag + "nhb%d" % i, [128, D], F32) for i in range(4)]
            ub = [sb(s1, nc, tag + "nub%d" % i, [128, D], BF16) for i in range(3)]
            stat = sb(s1, nc, tag + "nstat", [128, 8], F32)
            b_g = P.buf("g")
            b_hb = P.bufs("nhb", 4)
            b_ub = P.bufs("nub", 3)
            b_st = P.bufs("nst", 2)
            P.dma("sp", I_dma(gbc[:], g_pre.partition_broadcast(128)), writes=[b_g])

            def n0(t):
                P.dma("sp", I_dma(hb[t % 4][:], h[t * 128:(t + 1) * 128, :]), writes=[b_hb[t % 4]])

            def n1(t):
                i, j, k = t % 4, t % 2, t % 3
                P.op("act", I_act(ub[k][:], hb[i][:], AF.Square, accum_out=stat[:, j:j + 1]),
                     reads=[b_hb[i]], writes=[b_ub[k], b_st[j]])
                rstd_pool(P, C, stat[:, 2 + j:3 + j], stat[:, j:j + 1], 1.0 / D, [b_st[j]], [b_st[j]])

            def n2(t):
                i, j, k = t % 4, t % 2, t % 3
                P.op("dve", I_stt(ub[k][:], hb[i][:], stat[:, 2 + j:3 + j], gbc[:], ALU.mult, ALU.mult),
                     reads=[b_hb[i], b_st[j], b_g], writes=[b_ub[k]])
                P.dma("sp", I_dma(ubs[t * 128:(t + 1) * 128, :], ub[k][:]), reads=[b_ub[k]])

            sw_pipeline(NT, [n0, n1, n2])
            P.barrier()
        with ExitStack() as s2:
            KT = sb(s2, nc, tag + "KT", [128, 4, L], BF16)
            QT = [sb(s2, nc, tag + "QT%d" % i, [128, 4, 2, 512], BF16) for i in range(2)]
            VA = sb(s2, nc, tag + "VA", [128, NT, 8, 65], BF16)
            Wq = sb(s2, nc, tag + "Wq", [128, 8, 512], BF16)
            Wk = sb(s2, nc, tag + "Wk", [128, 8, 512], BF16)
            Wv = sb(s2, nc, tag + "Wv", [128, 8, 512], BF16)
            rel = sb(s2, nc, tag + "rel", [128, 128], F32)
            maskc = sb(s2, nc, tag + "maskc", [128, 128], F32)
            maskp = sb(s2, nc, tag + "maskp", [128, 128], F32)
            bias_c = sb(s2, nc, tag + "bc", [128, 4, 2, 128], F32)
            bias_p = sb(s2, nc, tag + "bp", [128, 4, 2, 128], F32)
            Ec = sb(s2, nc, tag + "Ec", [128, 4, 2, 128], F32)
            Ep = sb(s2, nc, tag + "Ep", [128, 4, 2, 128], F32)
            Pc = sb(s2, nc, tag + "Pc", [128, 4, 2, 128], BF16)
            Pp = sb(s2, nc, tag + "Pp", [128, 4, 2, 128], BF16)
            osb = [sb(s2, nc, tag + "osb%d" % i, [128, 8, 65], F32) for i in range(2)]
            ubl = [sb(s2, nc, tag + "ubl%d" % i, [128, D], BF16) for i in range(4)]
            b_ubl = P.bufs("ubl", 4)
            pj = [ps(s2, nc, tag + "pj%d" % i, [128, 512], F32) for i in range(2)]
            sc = ps(s2, nc, tag + "sc", [128, 4, 2, 128], F32)
            sp_ = ps(s2, nc, tag + "sp", [128, 4, 2, 128], F32)
            po = ps(s2, nc, tag + "po", [128, 2, 512], F32)
            b_KT = P.bufs("KT", 8)
            b_QT = P.bufs("QT", 2)
            b_VA = P.bufs("VA", NT)
            b_W = P.bufs("Wqkv", 3)
            b_bias = P.buf("bias")
            b_Ec, b_Ep, b_Pc, b_Pp = P.buf("Ec"), P.buf("Ep"), P.buf("Pc"), P.buf("Pp")
            b_osb = P.bufs("osb", 2)
            b_pj = P.bufs("pj", 2)
            b_sc, b_sp, b_po = P.buf("sc"), P.buf("sp"), P.buf("po")
            cnt = {"pj": 0, "ev": 0, "o": 0}

            P.op("pool", lambda e: e.iota(rel[:], pattern=[[1, 128]], base=0, channel_multiplier=-1,
                                          allow_small_or_imprecise_dtypes=True), writes=[b_bias])
            P.op("pool", I_memset(VA[:, :, :, 64:65], 1.0), writes=b_VA)
            for i_ in range(2):
                P.op("pool", I_memset(QT[i_][:, :, :, :], 0.0), writes=[b_QT[i_]])
            P.op("pool", I_memset(maskc[:], 0.0), writes=[b_bias])
            P.op("pool", lambda e: e.affine_select(out=maskc[:], in_=maskc[:], pattern=[[1, 128]], compare_op=ALU.is_ge,
                                                   fill=NEG, base=0, channel_multiplier=-1), reads=[b_bias], writes=[b_bias])
            P.op("pool", I_memset(maskp[:], 0.0), writes=[b_bias])
            P.op("pool", lambda e: e.affine_select(out=maskp[:], in_=maskp[:], pattern=[[-1, 128]], compare_op=ALU.is_ge,
                                                   fill=NEG, base=0, channel_multiplier=1), reads=[b_bias], writes=[b_bias])

            for g in groups:
                window, d = ATT_PATTERNS[g]
                nd = L // d
                bpr = nd // 128

                def segs(pi0, count):
                    out = []
                    p = pi0
                    while p < pi0 + count:
                        r, j = p // nd, p % nd
                        n = min(nd - j, pi0 + count - p)
                        out.append((p - pi0, n, r + d * j, d))
                        p += n
                    return out

                def tok_slice(t0, n, step):
                    return slice(t0, t0 + step * (n - 1) + 1, step)

                def pi_rows(tau):
                    (o0, n, t0, step), = segs(tau * 128, 128)
                    return tok_slice(t0, n, step)

                tpv = [pj[i].bitcast(BF16)[:, :].rearrange("p (k e) -> p k e", k=8) for i in range(2)]
                for tau in range(NT):
                    j = tau % 2
                    P.dma("sp", I_dma(ubl[tau % 4][:], ubs[pi_rows(tau), :]), writes=[b_ubl[tau % 4]])
                    for kc in range(8):
                        P.op("pe", I_tr(tpv[j][:, kc, :], ubl[tau % 4][:, kc * 128:(kc + 1) * 128], C.ident[:]),
                             reads=[b_ubl[tau % 4], C.b_ident], writes=[b_pj[j]])
                    if tau % 2 == 0:
                        P.op("act", I_acopy(uT[:, :, tau * 128:(tau + 1) * 128], tpv[j][:, :, :]), reads=[b_pj[j]], writes=[b_uT[tau // 4]])
                    else:
                        P.op("dve", I_copy(uT[:, :, tau * 128:(tau + 1) * 128], tpv[j][:, :, :]), reads=[b_pj[j]], writes=[b_uT[tau // 4]])

                for s_, Wt in ((0, Wq), (1, Wk), (2, Wv)):
                    c0 = s_ * 1536 + g * 512
                    src = w_qkv.rearrange("(kc p) c -> p kc c", p=128)[:, :, c0:c0 + 512]
                    P.dma("pool", I_dma(Wt[:, :, :], src), writes=[b_W[s_]])
                for hh in range(8):
                    par, pi = hh % 2, hh // 2
                    sl = -(2.0 ** -(hh + 1)) * d
                    P.op("pool", I_ts(bias_c[:, pi, par, :], rel[:], sl, None, ALU.mult), reads=[b_bias], writes=[b_bias])
                    P.op("pool", I_tt(bias_c[:, pi, par, :], bias_c[:, pi, par, :], maskc[:], ALU.add),
                         reads=[b_bias], writes=[b_bias])
                    P.op("pool", I_ts(bias_p[:, pi, par, :], rel[:], 128.0, sl, ALU.add, ALU.mult), reads=[b_bias], writes=[b_bias])
                    P.op("pool", I_tt(bias_p[:, pi, par, :], bias_p[:, pi, par, :], maskp[:], ALU.add),
                         reads=[b_bias], writes=[b_bias])

                def proj_T(dst, b_dst, Wt, b_Wt, pi0, scale, dcol0, pair=False):
                    for c in range(4):
                        q = cnt["pj"] % 2
                        cnt["pj"] += 1
                        for kc in range(8):
                            P.op("pe", I_mm(pj[q][:], Wt[:, kc, c * 128:(c + 1) * 128],
                                            uT[:, kc, pi0:pi0 + 512], kc == 0, kc == 7),
                                 reads=[b_Wt, b_uT[pi0 // 512]], writes=[b_pj[q]])
                        if pair:
                            P.op("act", I_act(dst[0:64, c, 0, dcol0:dcol0 + 512], pj[q][0:64, :], AF.Copy, scale=scale),
                                 reads=[b_pj[q]], writes=[b_dst])
                            P.op("dve", I_ts(dst[64:128, c, 1, dcol0:dcol0 + 512], pj[q][64:128, :], scale, None, ALU.mult),
                                 reads=[b_pj[q]], writes=[b_dst])
                        elif cnt["ev"] % 2 == 0:
                            P.op("act", I_act(dst[:, c, dcol0:dcol0 + 512], pj[q][:], AF.Copy, scale=scale),
                                 reads=[b_pj[q]], writes=[b_dst])
                        else:
                            P.op("dve", I_ts(dst[:, c, dcol0:dcol0 + 512], pj[q][:], scale, None, ALU.mult),
                                 reads=[b_pj[q]], writes=[b_dst])
                        cnt["ev"] += 1

                for rg in range(8):
                    proj_T(KT, b_KT[rg], Wk, b_W[1], rg * 512, 1.0, rg * 512)
                for tau in range(NT):
                    q = cnt["pj"] % 2
                    cnt["pj"] += 1
                    for kc in range(8):
                        P.op("pe", I_mm(pj[q][:], uT[:, kc, tau * 128:(tau + 1) * 128], Wv[:, kc, :], kc == 0, kc == 7),
                             reads=[b_W[2], b_uT[tau // 4]], writes=[b_pj[q]])
                    src = pj[q][:, :].rearrange("p (h e) -> p h e", h=8)
                    if cnt["ev"] % 2 == 0:
                        P.op("act", I_acopy(VA[:, tau, :, 0:64], src), reads=[b_pj[q]], writes=[b_VA[tau]])
                    else:
                        P.op("dve", I_copy(VA[:, tau, :, 0:64], src), reads=[b_pj[q]], writes=[b_VA[tau]])
                    cnt["ev"] += 1
                for rg in range(8):
                    qi = rg % 2
                    proj_T(QT[qi], b_QT[qi], Wq, b_W[0], rg * 512, 0.125, 0, pair=True)
                    for tb_ in range(4):
                        tau = rg * 4 + tb_
                        has_prev = (tau % bpr) != 0
                        for pi in range(4):
                            P.op("pe", I_mm(sc[:, pi, :, :], KT[:, pi, tau * 128:(tau + 1) * 128],
                                            QT[qi][:, pi, :, tb_ * 128:(tb_ + 1) * 128], True, True),
                                 reads=[b_KT[tau // 4], b_QT[qi]], writes=[b_sc])
                        if has_prev:
                            for pi in range(4):
                                P.op("pe", I_mm(sp_[:, pi, :, :], KT[:, pi, (tau - 1) * 128:tau * 128],
                                                QT[qi][:, pi, :, tb_ * 128:(tb_ + 1) * 128], True, True),
                                     reads=[b_KT[(tau - 1) // 4], b_QT[qi]], writes=[b_sp])
                        P.op("dve", I_tt(Ec[:], sc[:], bias_c[:], ALU.add), reads=[b_sc, b_bias], writes=[b_Ec])
                        P.op("act", I_act(Pc[:], Ec[:], AF.Exp), reads=[b_Ec], writes=[b_Pc])
                        if has_prev:
                            P.op("dve", I_tt(Ep[:], sp_[:], bias_p[:], ALU.add), reads=[b_sp, b_bias], writes=[b_Ep])
                            P.op("act", I_act(Pp[:], Ep[:], AF.Exp), reads=[b_Ep], writes=[b_Pp])
                        for hh in range(8):
                            par, pi = hh % 2, hh // 2
                            oap = po[:, hh // 4, (hh % 4) * 65:(hh % 4) * 65 + 65]
                            P.op("pe", I_mm(oap, Pc[:, pi, par, :], VA[:, tau, hh, :], True, not has_prev),
                                 reads=[b_Pc, b_VA[tau]], writes=[b_po])
                            if has_prev:
                                P.op("pe", I_mm(oap, Pp[:, pi, par, :], VA[:, tau - 1, hh, :], False, True),
                                     reads=[b_Pp, b_VA[tau - 1]], writes=[b_po])
                        oi = cnt["o"] % 2
                        cnt["o"] += 1
                        P.op("dve", I_copy(osb[oi][:, :, :].rearrange("p (b h) e -> p b (h e)", b=2), po[:, :, 0:260]),
                             reads=[b_po], writes=[b_osb[oi]])
                        (o0, n, t0, step), = segs(tau * 128, 128)
                        dst = acc[0, tok_slice(t0, n, step), :]
                        if g == groups[0]:
                            P.dma("sp", I_dma(dst, osb[oi][:, :, :].rearrange("p h e -> p (h e)")), reads=[b_osb[oi]], writes=[b_acc])
                        else:
                            P.dma("pool", I_dma_acc(dst, osb[oi][:, :, :].rearrange("p h e -> p (h e)")), reads=[b_osb[oi]], writes=[b_acc])
            P.barrier()
        if not do_merge:
            return
        with ExitStack() as s3:
            Wo = sb(s3, nc, tag + "Wo", [128, 4, D], BF16)
            gpost = sb(s3, nc, tag + "gpo", [128, D], F32)
            a_in = [[sb(s3, nc, tag + "ain%d_%d" % (i, g), [128, 8, 65], F32) for g in range(1)] for i in range(3)]
            rz = sb(s3, nc, tag + "rz", [128, 2, 8], F32)
            mg = [sb(s3, nc, tag + "mg%d" % i, [128, 8, 64], BF16) for i in range(2)]
            mT = [sb(s3, nc, tag + "mT%d" % i, [128, 4, 128], BF16) for i in range(2)]
            hres = [sb(s3, nc, tag + "hr%d" % i, [128, D], F32) for i in range(2)]
            mb = [sb(s3, nc, tag + "mb%d" % i, [128, D], F32) for i in range(4)]
            junk = sb(s3, nc, tag + "junk", [128, 512], BF16)
            stat = sb(s3, nc, tag + "stat", [128, 24], F32)
            tp2 = [ps(s3, nc, tag + "tp%d" % i, [128, 4, 128], BF16) for i in range(2)]
            ops4 = [ps(s3, nc, tag + "ops%d" % i, [128, 512], F32) for i in range(4)]
            b_Wo, b_g = P.buf("Wo"), P.buf("gpo")
            b_ain = P.bufs("ain", 3)
            b_rz = P.bufs("rz", 2)
            b_mg = P.bufs("mg", 2)
            b_mT = P.bufs("mT", 2)
            b_hres = P.bufs("hres", 2)
            b_mb = P.bufs("mb", 4)
            b_junk = P.buf("junk")
            b_st = P.bufs("st", 4)
            b_tp2 = P.bufs("tp", 2)
            b_ops4 = P.bufs("ops", 4)
            P.dma("pool", I_dma(Wo[:, :, :], w_o.rearrange("(c p) n -> p c n", p=128)), writes=[b_Wo])
            P.dma("sp", I_dma(gpost[:], g_post.partition_broadcast(128)), writes=[b_g])
            def m0(t):
                P.dma("sp", I_dma(a_in[t % 3][0][:, :, :].rearrange("p h e -> p (h e)"), acc[0, t * 128:(t + 1) * 128, :]),
                      writes=[b_ain[t % 3]])

            def m1(t):
                i = t % 2
                A0 = a_in[t % 3][0]
                P.op("dve", lambda e, o=rz[:, i, :], s=A0[:, :, 64]: e.reciprocal(out=o, in_=s), reads=[b_ain[t % 3]], writes=[b_rz[i]])
                P.op("dve", I_tt(mg[i][:], A0[:, :, 0:64], rz[:, i, :].unsqueeze(2).to_broadcast([128, 8, 64]), ALU.mult),
                     reads=[b_ain[t % 3], b_rz[i]], writes=[b_mg[i]])

            def m2(t):
                i = t % 2
                tp, b_tp = tp2[t % 2], b_tp2[t % 2]
                for c in range(4):
                    P.op("pe", I_tr(tp[:, c, :], mg[i][:, 2 * c:2 * c + 2, :].rearrange("p h e -> p (h e)"), C.ident[:]),
                         reads=[b_mg[i], C.b_ident], writes=[b_tp])
                P.op("act", I_acopy(mT[i][:, :, :], tp[:, :, :]), reads=[b_tp], writes=[b_mT[i]])

            def m3(t):
                i = t % 2
                ops_, b_ops = ops4[(t % 2) * 2:(t % 2) * 2 + 2], b_ops4[(t % 2) * 2:(t % 2) * 2 + 2]
                for dh in range(2):
                    for c in range(4):
                        P.op("pe", I_mm(ops_[dh][:], mT[i][:, c, :], Wo[:, c, dh * 512:(dh + 1) * 512], c == 0, c == 3),
                             reads=[b_mT[i], b_Wo], writes=[b_ops[dh]])

            def m4(t):
                i = t % 4
                ops_, b_ops = ops4[(t % 2) * 2:(t % 2) * 2 + 2], b_ops4[(t % 2) * 2:(t % 2) * 2 + 2]
                post_part1(P, ops_, b_ops, mb, b_mb, junk, b_junk, stat, b_st, i, copy_eng="act")

            def m5(t):
                i = t % 4
                post_part2(P, C, mb, b_mb, hres, b_hres, stat, b_st, gpost, b_g, h, t * 128, i, 1.0, add_eng="dma")

            sw_pipeline(NT, [m0, m1, m2, m3, m4, m5])
            P.barrier()


def bc3(ap2, n, axis):
    a = ap2.shape[1]
    if axis == 2:
        return ap2.unsqueeze(2).to_broadcast([ap2.shape[0], a, n])
    return ap2.unsqueeze(1).to_broadcast([ap2.shape[0], n, a])


def hyb_phase(P, C, h, w_in, conv_w, conv_b, ln_g, ln_b, sconv_w, sconv_b, dt_bias, a_log, d_skip, ng, w_out,
              g_pre, g_post, ysc, tag="hy", nblk=8):
    nc = P.nc
    TBK = 512
    with ExitStack() as st:
        Wdt = sb(st, nc, tag + "Wdt", [128, 8, 16], BF16)
        gpre = sb(st, nc, tag + "gpre", [128, D], F32)
        ngb = sb(st, nc, tag + "ngb", [128, D], F32)
        trif = sb(st, nc, tag + "trif", [128, 128], F32)
        onesf = sb(st, nc, tag + "onesf", [128, 128], F32)
        onesb = sb(st, nc, tag + "onesb", [128, 128], BF16)
        maskrep = sb(st, nc, tag + "maskrep", [128, 4, 128], BF16)
        w31T = sb(st, nc, tag + "w31T", [128, 8, 31], F32)
        w4T = sb(st, nc, tag + "w4T", [128, 12, 4], F32)
        cvec = sb(st, nc, tag + "cvec", [128, 48], F32)
        small = sb(st, nc, tag + "small", [128, 64], F32)
        brow = sb(st, nc, tag + "brow", [1, 1280], BF16)
        onesrow = sb(st, nc, tag + "onesrow", [1, 128], BF16)
        dg4 = sb(st, nc, tag + "dg4", [128, 12, 4, 128], BF16)
        dg31 = [sb(st, nc, tag + "dg31_%d" % i, [128, 31, 128], BF16) for i in range(2)]
        gluT = sb(st, nc, tag + "gluT", [128, 8, 30 + TBK], BF16)
        xbcT = sb(st, nc, tag + "xbcT", [128, 12, 3 + TBK], BF16)
        hidT = sb(st, nc, tag + "hidT", [128, 8, TBK], BF16)
        szT = sb(st, nc, tag + "szT", [128, 8, TBK], BF16)
        XT = sb(st, nc, tag + "XT", [128, 12, TBK], BF16)
        uT = [sb(st, nc, tag + "uT%d" % i, [128, 8, TBK], BF16) for i in range(2)]
        wsl = [sb(st, nc, tag + "wsl%d" % i, [128, 8, 128], BF16) for i in range(6)]
        S32 = sb(st, nc, tag + "S32", [128, D], F32)
        Sbf = sb(st, nc, tag + "Sbf", [128, D], BF16)
        dts = sb(st, nc, tag + "dts", [128, 4, 16], F32)
        lndt = sb(st, nc, tag + "lndt", [128, 4, 16], F32)
        hb = [sb(st, nc, tag + "hb%d" % i, [128, D], F32) for i in range(2)]
        ub = [sb(st, nc, tag + "ub%d" % i, [128, D], BF16) for i in range(2)]
        stat = sb(st, nc, tag + "stat", [128, 24], F32)
        f512 = [sb(st, nc, tag + "f512_%d" % i, [128, 512], F32) for i in range(8)]
        sqb = sb(st, nc, tag + "sqb", [128, 512], BF16)
        xs_t = [sb(st, nc, tag + "xs%d" % i, [128, D], BF16) for i in range(2)]
        yt_t = [sb(st, nc, tag + "yt%d" % i, [128, D], F32) for i in range(2)]
        Btm = [sb(st, nc, tag + "Btm%d" % i, [128, 256], BF16) for i in range(2)]
        sm_t = [sb(st, nc, tag + "sm%d" % i, [128, 160], F32) for i in range(2)]
        Rt_t = [sb(st, nc, tag + "Rt%d" % i, [128, 8, 128], F32) for i in range(2)]
        Lt_t = [sb(st, nc, tag + "Lt%d" % i, [128, 8, 128], F32) for i in range(2)]
        Mt_t = [sb(st, nc, tag + "Mt%d" % i, [128, 8, 128], BF16) for i in range(2)]
        CBs_t = [sb(st, nc, tag + "CBs%d" % i, [128, 256], F32) for i in range(2)]
        xdtd_t = [sb(st, nc, tag + "xdtd%d" % i, [128, D], BF16) for i in range(2)]
        xsD_t = [sb(st, nc, tag + "xsD%d" % i, [128, D], BF16) for i in range(2)]
        yb_t = [sb(st, nc, tag + "yb%d" % i, [128, D], BF16) for i in range(2)]
        ybT_t = [sb(st, nc, tag + "ybT%d" % i, [128, 8, 128], BF16) for i in range(2)]
        pA = ps(st, nc, tag + "pA", [128, 2, 512], F32)
        pB = ps(st, nc, tag + "pB", [128, 2, 512], F32)
        pC = ps(st, nc, tag + "pC", [128, 2, 512], F32)
        p6 = ps(st, nc, tag + "p6", [128, 512], F32)
        p7 = ps(st, nc, tag + "p7", [128, 8, 128], BF16)
        stg = hb[0]

        B = {}
        for n_ in ["Wdt", "g", "const", "dg4", "S32", "Sbf", "dts",
                   "stat", "sqb", "stg", "pA0", "pA1", "pB0", "pB1", "pC0", "pC1", "p6", "p7"]:
            B[n_] = P.buf(n_)
        for n_, k in (("gluT", 8), ("xbcT", 12), ("hidT", 8), ("szT", 8), ("XT", 12), ("uT", 2), ("hb", 2), ("ub", 2), ("dg31", 2), ("wsl", 6), ("f512", 8), ("xs", 2), ("yt", 2), ("Btm", 2),
                      ("sm", 2), ("Rt", 2), ("Lt", 2), ("Mt", 2), ("CBs", 2), ("xdtd", 2), ("xsD", 2), ("yb", 2),
                      ("ybT", 2), ("nst", 2)):
            B[n_] = P.bufs(n_, k)
        cnt = {"w": 0, "dg": 0, "prep": 0, "ch": 0, "g": 0}
        w_in_v = w_in.rearrange("(kc p) c -> p kc c", p=128)

        P.dma("pool", I_dma(Wdt[:, :, :], w_in_v[:, :, 4608:4624]), writes=[B["Wdt"]])
        P.dma("sp", I_dma(gpre[:], g_pre.partition_broadcast(128)), writes=[B["g"]])
        P.dma("sp", I_dma(ngb[:], ng.partition_broadcast(128)), writes=[B["g"]])
        P.dma("sp", I_dma(small[:, 0:16], a_log.partition_broadcast(128)), writes=[B["const"]])
        P.dma("sp", I_dma(small[:, 16:32], dt_bias.partition_broadcast(128)), writes=[B["const"]])
        P.dma("sp", I_dma(small[:, 32:48], d_skip.partition_broadcast(128)), writes=[B["const"]])
        P.op("act", I_act(small[:, 0:16], small[:, 0:16], AF.Exp), reads=[B["const"]], writes=[B["const"]])
        P.op("dve", I_ts(small[:, 0:16], small[:, 0:16], -1.0, None, ALU.mult), reads=[B["const"]], writes=[B["const"]])
        P.op("pool", I_memset(trif[:], 1.0), writes=[B["const"]])
        P.op("pool", lambda e: e.affine_select(out=trif[:], in_=trif[:], pattern=[[1, 128]], compare_op=ALU.is_ge,
                                               fill=0.0, base=0, channel_multiplier=-1), reads=[B["const"]], writes=[B["const"]])
        P.op("pool", I_memset(onesf[:], 1.0), writes=[B["const"]])
        P.op("pool", I_memset(onesb[:], 1.0 / D), writes=[B["const"]])
        P.op("pool", I_memset(onesrow[:], 1.0), writes=[B["const"]])
        P.op("pool", I_memset(maskrep[:], 0.0), writes=[B["const"]])
        P.op("pool", lambda e: e.affine_select(out=maskrep[:], in_=maskrep[:], pattern=[[0, 4], [1, 128]],
                                               compare_op=ALU.is_ge, fill=NEG, base=0, channel_multiplier=-1),
             reads=[B["const"]], writes=[B["const"]])
        P.op("pool", I_memset(gluT[:, :, 0:30], 0.0), writes=B["gluT"])
        P.op("pool", I_memset(xbcT[:, :, 0:3], 0.0), writes=B["xbcT"])
        P.op("pool", I_memset(S32[:], 0.0), writes=[B["S32"]])
        P.op("pool", I_memset(Sbf[:], 0.0), writes=[B["Sbf"]])

        stg_tiles = [hb[0], hb[1], yt_t[0], yt_t[1]]
        stg_slots = [(ti, base) for ti in range(4) for base in (0, 32, 64)]
        b_stg = P.bufs("stgs", len(stg_slots))
        ps_rot = [(pA[:, 0, :], B["pA0"]), (pA[:, 1, :], B["pA1"]), (pB[:, 0, :], B["pB0"]), (pB[:, 1, :], B["pB1"]),
                  (pC[:, 0, :], B["pC0"]), (pC[:, 1, :], B["pC1"]), (p6[:, :], B["p6"])]
        scnt = {"s": 0, "p": 0}

        def load_T(src2d, r, n, dst3, scale=1.0):
            for c0 in range(0, n, 8):
                nn = min(8, n - c0)
                si = scnt["s"] % len(stg_slots)
                scnt["s"] += 1
                ti, base = stg_slots[si]
                sv = stg_tiles[ti]
                P.dma("sp", I_dma(sv[base:base + r, 0:nn * 128], src2d[:, c0 * 128:(c0 + nn) * 128]), writes=[b_stg[si]])
                for c in range(nn):
                    pbank, pbuf = ps_rot[scnt["p"] % len(ps_rot)]
                    scnt["p"] += 1
                    P.op("pe", I_tr(pbank[:, 0:r], sv[base:base + r, c * 128:(c + 1) * 128],
                                    C.identf[base:base + r, base:base + r]),
                         reads=[b_stg[si], C.b_ident], writes=[pbuf])
                    P.op("dve", I_ts(dst3[:, c0 + c, :], pbank[:, 0:r], scale, None, ALU.mult),
                         reads=[pbuf], writes=[B["const"]])

        load_T(conv_w, 31, 8, w31T[:, :, :], 0.5)
        load_T(sconv_w, 4, 12, w4T[:, :, :], 0.5)
        cv3 = cvec[:, :].rearrange("p (a b) -> p a b", b=1)
        load_T(conv_b.rearrange("(o n) -> o n", o=1), 1, 8, cv3[:, 0:8, :], 1.0)
        load_T(ln_g.rearrange("(o n) -> o n", o=1), 1, 8, cv3[:, 8:16, :], 0.5)
        load_T(ln_b.rearrange("(o n) -> o n", o=1), 1, 8, cv3[:, 16:24, :], 0.5)
        load_T(sconv_b.rearrange("(o n) -> o n", o=1), 1, 12, cv3[:, 24:36, :], 0.5)
        for c in range(12):
            P.op("dve", I_tt(dg4[:, c, :, :], bc3(C.identf[:], 4, 1), bc3(w4T[:, c, :], 128, 2), ALU.mult),
                 reads=[B["const"], C.b_ident], writes=[B["dg4"]])
        P.barrier()

        def prep_l(blk, t):
            i = cnt["prep"] % 2
            cnt["prep"] += 1
            tok0 = blk * TBK + t * 128
            P.dma("sp", I_dma(hb[i][:], h[tok0:tok0 + 128, :]), writes=[B["hb"][i]])
            return i

        def prep_a(blk, t, i=None):
            if i is None:
                i = prep_l(blk, t)
            P.op("act", I_act(ub[i][:], hb[i][:], AF.Square, accum_out=stat[:, i:i + 1]),
                 reads=[B["hb"][i]], writes=[B["ub"][i], B["nst"][i]])
            rstd_pool(P, C, stat[:, 2 + i:3 + i], stat[:, i:i + 1], 1.0 / D, [B["nst"][i]], [B["nst"][i]])
            P.op("dve", I_stt(ub[i][:], hb[i][:], stat[:, 2 + i:3 + i], gpre[:], ALU.mult, ALU.mult),
                 reads=[B["hb"][i], B["nst"][i], B["g"]], writes=[B["ub"][i]])
            return i

        def prep_b(blk, t, i):
            s = blk % 2
            for kc in range(8):
                P.op("pe", I_tr(p7[:, kc, :], ub[i][:, kc * 128:(kc + 1) * 128], C.ident[:]),
                     reads=[B["ub"][i], C.b_ident], writes=[B["p7"]])
            P.op("act", I_acopy(uT[s][:, :, t * 128:(t + 1) * 128], p7[:, :, :]), reads=[B["p7"]], writes=[B["uT"][s]])

        def load_w(col0):
            i = cnt["w"] % 6
            cnt["w"] += 1
            P.dma("pool", I_dma(wsl[i][:, :, :], w_in_v[:, :, col0:col0 + 128]), writes=[B["wsl"][i]])
            return i

        def proj_cm(blk, col0, half):
            s = blk % 2
            wi = load_w(col0)
            for kc in range(8):
                P.op("pe", I_mm(pA[:, half, :], wsl[wi][:, kc, :], uT[s][:, kc, :], kc == 0, kc == 7),
                     reads=[B["wsl"][wi], B["uT"][s]], writes=[B["pA%d" % half]])

        for t in range(4):
            prep_b(0, t, prep_a(0, t))

        for blk in range(nblk):
            tokb = blk * TBK
            pend = []
            nxt = [(blk + 1, t) for t in range(4)] if blk + 1 < nblk else []

            def prefetch_load():
                if nxt:
                    b_, t_ = nxt.pop(0)
                    pend.append((b_, t_, prep_l(b_, t_)))

            def prefetch_compute():
                if pend:
                    b_, t_, i_ = pend.pop(0)
                    prep_a(b_, t_, i_)
                    prep_b(b_, t_, i_)

            def c0_(c):
                proj_cm(blk, c * 128, 0)
                proj_cm(blk, 1024 + c * 128, 1)

            def c1_(c):
                di = c % 2
                P.op("dve", I_tt(dg31[di][:, :, :], bc3(C.ident[:], 31, 1), bc3(w31T[:, c, :], 128, 2), ALU.mult),
                     reads=[B["const"], C.b_ident], writes=[B["dg31"][di]])
                P.op("act", I_act(f512[2][:], pA[:, 1, :], AF.Tanh, scale=0.5), reads=[B["pA1"]], writes=[B["f512"][2]])
                P.op("dve", I_stt(gluT[:, c, 30:30 + TBK], f512[2][:], 1.0, pA[:, 0, :], ALU.add, ALU.mult),
                     reads=[B["f512"][2], B["pA0"]], writes=[B["gluT"][c]])

            def c2_(c):
                di = c % 2
                cb = c % 2
                for k in range(31):
                    P.op("pe", I_mm(pC[:, cb, :], dg31[di][:, k, :], gluT[:, c, k:k + TBK], k == 0, k == 30),
                         reads=[B["dg31"][di], B["gluT"][c]], writes=[B["pC%d" % cb]])

            def c3_(c):
                cb = c % 2
                P.op("act", I_act(hidT[:, c, :], pC[:, cb, :], AF.Identity, bias=cvec[:, c:c + 1]),
                     reads=[B["pC%d" % cb], B["const"]], writes=[B["hidT"][c]])
                P.op("act", I_act(sqb[:], hidT[:, c, :], AF.Square), reads=[B["hidT"][c]], writes=[B["sqb"]])

            def c4_(c):
                P.op("pe", I_mm(pB[:, 0, :], onesb[:], hidT[:, c, :], c == 0, c == 7), reads=[B["hidT"][c], B["const"]], writes=[B["pB0"]])
                P.op("pe", I_mm(pB[:, 1, :], onesb[:], sqb[:], c == 0, c == 7), reads=[B["sqb"], B["const"]], writes=[B["pB1"]])

            sw_pipeline(8, [c0_, c1_, c2_, c3_, c4_])
            P.op("pool", I_copy(gluT[:, :, 0:30], gluT[:, :, TBK:TBK + 30]), reads=B["gluT"], writes=B["gluT"])
            P.op("act", I_acopy(f512[0][:], pB[:, 0, :]), reads=[B["pB0"]], writes=[B["f512"][0]])
            P.op("dve", I_copy(f512[1][:], pB[:, 1, :]), reads=[B["pB1"]], writes=[B["f512"][1]])
            P.op("dve", I_tt(f512[2][:], f512[0][:], f512[0][:], ALU.mult), reads=[B["f512"][0]], writes=[B["f512"][2]])
            P.op("dve", I_tt(f512[1][:], f512[1][:], f512[2][:], ALU.subtract), reads=[B["f512"][1], B["f512"][2]], writes=[B["f512"][1]])
            P.op("dve", I_ts(f512[1][:], f512[1][:], EPS, None, ALU.add), reads=[B["f512"][1]], writes=[B["f512"][1]])
            P.op("act", I_act(f512[1][:], f512[1][:], AF.Ln), reads=[B["f512"][1]], writes=[B["f512"][1]])
            P.op("act", I_act(f512[1][:], f512[1][:], AF.Exp, scale=-0.5), reads=[B["f512"][1]], writes=[B["f512"][1]])
            s = blk % 2
            for t in range(4):
                for kc in range(8):
                    P.op("pe", I_mm(p6[:, t * 16:(t + 1) * 16], uT[s][:, kc, t * 128:(t + 1) * 128], Wdt[:, kc, :], kc == 0, kc == 7),
                         reads=[B["uT"][s], B["Wdt"]], writes=[B["p6"]])
            P.op("dve", I_tt(dts[:, :, :], p6[:, 0:64].rearrange("p (t h) -> p t h", t=4), bc3(small[:, 16:32], 4, 1), ALU.add),
                 reads=[B["p6"], B["const"]], writes=[B["dts"]])
            P.op("act", I_act(dts[:, :, :], dts[:, :, :], AF.Exp), reads=[B["dts"]], writes=[B["dts"]])
            P.op("act", I_act(dts[:, :, :], dts[:, :, :], AF.Ln, bias=1.0), reads=[B["dts"]], writes=[B["dts"]])
            P.op("act", I_act(lndt[:, :, :], dts[:, :, :], AF.Ln), reads=[B["dts"]], writes=[B["dts"]])

            def ln_(c):
                a, b_, d_ = 3 + (c % 2), 5 + (c % 2), 7
                P.op("dve", I_tt(f512[a][:], hidT[:, c, :], f512[0][:], ALU.subtract), reads=[B["hidT"][c], B["f512"][0]], writes=[B["f512"][a]])
                P.op("dve", I_tt(f512[a][:], f512[a][:], f512[1][:], ALU.mult), reads=[B["f512"][a], B["f512"][1]], writes=[B["f512"][a]])
                P.op("act", I_act(f512[b_][:], f512[a][:], AF.Identity, scale=cvec[:, 8 + c:9 + c], bias=cvec[:, 16 + c:17 + c]),
                     reads=[B["f512"][a], B["const"]], writes=[B["f512"][b_]])
                P.op("act", I_act(f512[a][:], f512[b_][:], AF.Tanh), reads=[B["f512"][b_]], writes=[B["f512"][a]])
                P.op("dve", I_stt(hidT[:, c, :], f512[a][:], 1.0, f512[b_][:], ALU.add, ALU.mult),
                     reads=[B["f512"][a], B["f512"][b_]], writes=[B["hidT"][c]])

            def z_(c):
                hf = c % 2
                proj_cm(blk, 2048 + c * 128, hf)
                P.op("act", I_act(f512[2][:] if hf == 0 else f512[7][:], pA[:, hf, :], AF.Tanh, scale=0.5),
                     reads=[B["pA%d" % hf]], writes=[B["f512"][2 if hf == 0 else 7]])
                P.op("dve", I_stt(szT[:, c, :], f512[2][:] if hf == 0 else f512[7][:], 1.0, pA[:, hf, :], ALU.add, ALU.mult),
                     reads=[B["f512"][2 if hf == 0 else 7], B["pA%d" % hf]], writes=[B["szT"][c]])

            for c in range(8):
                z_(c)
                ln_(c)
            for t in range(4):
                P.dma("sp", I_dma(ysc[blk * 4 + t, :, 0:8, :], hidT[:, :, t * 128:(t + 1) * 128]), reads=B["hidT"])
            for c in range(12):
                hf = c % 2
                proj_cm(blk, 3072 + c * 128, hf)
                if hf == 0:
                    P.op("act", I_acopy(xbcT[:, c, 3:3 + TBK], pA[:, 0, :]), reads=[B["pA0"]], writes=[B["xbcT"][c]])
                else:
                    P.op("dve", I_copy(xbcT[:, c, 3:3 + TBK], pA[:, 1, :]), reads=[B["pA1"]], writes=[B["xbcT"][c]])
            for c in range(12):
                a = 3 + (c % 2)
                b_ = 5 + (c % 2)
                hf = c % 2
                for k in range(4):
                    P.op("pe", I_mm(pC[:, hf, :], dg4[:, c, k, :], xbcT[:, c, k:k + TBK], k == 0, k == 3),
                         reads=[B["dg4"], B["xbcT"][c]], writes=[B["pC%d" % hf]])
                P.op("act", I_act(f512[b_][:], pC[:, hf, :], AF.Identity, bias=cvec[:, 24 + c:25 + c]),
                     reads=[B["pC%d" % hf], B["const"]], writes=[B["f512"][b_]])
                P.op("act", I_act(f512[a][:], f512[b_][:], AF.Tanh), reads=[B["f512"][b_]], writes=[B["f512"][a]])
                P.op("dve", I_stt(XT[:, c, :], f512[a][:], 1.0, f512[b_][:], ALU.add, ALU.mult),
                     reads=[B["f512"][a], B["f512"][b_]], writes=[B["XT"][c]])

            def ssd_vars(t):
                q = (blk * 4 + t) % 2
                tsl = slice(t * 128, (t + 1) * 128)
                return q, tsl, xs_t[q], sm_t[q], None, xdtd_t[q], xsD_t[q], yt_t[q], B["sm"][q], dts[:, t, :]

            def ssd_A(t):
                q, tsl, xs, sm, xdt, xdtd, xsD, yt, bsm, dtc = ssd_vars(t)
                prefetch_load()
                for c in range(2):
                    P.op("pe", I_tr(p7[:, c, :], XT[:, 8 + c, tsl], C.ident[:]), reads=[B["XT"][8 + c], C.b_ident], writes=[B["p7"]])
                P.op("act", I_acopy(Btm[q][:, :].rearrange("p (c e) -> p c e", c=2), p7[:, 0:2, :]), reads=[B["p7"]], writes=[B["Btm"][q]])
                for c in range(8):
                    P.op("pe", I_tr(p7[:, c, :], XT[:, c, tsl], C.ident[:]), reads=[B["XT"][c], C.b_ident], writes=[B["p7"]])
                P.op("act", I_acopy(xs[:, :].rearrange("p (c e) -> p c e", c=8), p7[:, :, :]), reads=[B["p7"]], writes=[B["xs"][q]])
                P.op("dve", I_tt(sm[:, 0:16], dtc, small[:, 0:16], ALU.mult), reads=[B["dts"], B["const"]], writes=[bsm])
                P.op("pe", I_mm(p6[:, 256:272], trif[:], sm[:, 0:16], True, True), reads=[bsm, B["const"]], writes=[B["p6"]])
                P.op("pe", I_mm(p6[:, 272:288], onesf[:], sm[:, 0:16], True, True), reads=[bsm, B["const"]], writes=[B["p6"]])
                P.op("dve", I_copy(sm[:, 16:48], p6[:, 256:288]), reads=[B["p6"]], writes=[bsm])
                P.op("dve", I_tt(sm[:, 48:64], sm[:, 32:48], sm[:, 16:32], ALU.subtract), reads=[bsm], writes=[bsm])
                P.op("act", I_act(sm[:, 48:64], sm[:, 48:64], AF.Exp), reads=[bsm], writes=[bsm])
                P.op("dve", I_tt(sm[:, 64:80], sm[:, 48:64], dtc, ALU.mult), reads=[bsm, B["dts"]], writes=[bsm])
                P.op("act", I_act(sm[:, 80:112], sm[:, 16:48], AF.Exp), reads=[bsm], writes=[bsm])
                P.op("dve", I_stt(sm[:, 112:128], sm[:, 16:32], -1.0, lndt[:, t, :], ALU.mult, ALU.add),
                     reads=[bsm, B["dts"]], writes=[bsm])
                xs3 = xs[:, :].rearrange("p (h e) -> p h e", h=16)
                P.op("dve", I_tt(xdtd[:, :].rearrange("p (h e) -> p h e", h=16), xs3, bc3(sm[:, 64:80], 64, 2), ALU.mult),
                     reads=[B["xs"][q], bsm], writes=[B["xdtd"][q]])
                P.op("dve", I_tt(xsD[:, :].rearrange("p (h e) -> p h e", h=16), xs3, bc3(small[:, 32:48], 64, 2), ALU.mult),
                     reads=[B["xs"][q], B["const"]], writes=[B["xsD"][q]])
                for g in range(2):
                    P.op("pe", I_mm(pC[:, 1, g * 128:(g + 1) * 128], XT[:, 8 + g, tsl], XT[:, 10 + g, tsl], True, True),
                         reads=[B["XT"][8 + g], B["XT"][10 + g]], writes=[B["pC1"]])
                P.op("act", I_acopy(CBs_t[q][:], pC[:, 1, 0:256]), reads=[B["pC1"]], writes=[B["CBs"][q]])

            def ssd_B(t, part):
                q, tsl, xs, sm, xdt, xdtd, xsD, yt, bsm, dtc = ssd_vars(t)
                pB_flat = pB[:, :, :].rearrange("p a b -> p (a b)")
                if part == 0:
                    for g in range(2):
                        P.op("pe", I_mm(pA[:, g, :], XT[:, 10 + g, tsl], Sbf[:, g * 512:(g + 1) * 512], True, True),
                             reads=[B["XT"][10 + g], B["Sbf"]], writes=[B["pA%d" % g]])

                for g in ([0] if part == 0 else [1]):
                    gq = cnt["g"] % 2
                    cnt["g"] += 1
                    Rt, Lt, Mt = Rt_t[gq], Lt_t[gq], Mt_t[gq]
                    P.op("dve", I_tt(Rt[:, :, :], bc3(trif[:], 8, 1), bc3(sm[:, g * 8:(g + 1) * 8], 128, 2), ALU.mult),
                         reads=[bsm, B["const"]], writes=[B["Rt"][gq]])
                    for hf in range(2):
                        P.op("pe", I_mm(pC[:, hf, :], onesf[:], Rt[:, hf * 4:(hf + 1) * 4, :].rearrange("p h l -> p (h l)"), True, False),
                             reads=[B["Rt"][gq], B["const"]], writes=[B["pC%d" % hf]])
                        P.op("pe", I_mm(pC[:, hf, :], C.ident[:], maskrep[:, :, :].rearrange("p h l -> p (h l)"), False, True),
                             reads=[B["const"], C.b_ident], writes=[B["pC%d" % hf]])
                    for hf in range(2):
                        for hh in range(4):
                            hgl = g * 8 + hf * 4 + hh
                            P.op("act", I_act(Lt[:, hf * 4 + hh, :], pC[:, hf, hh * 128:(hh + 1) * 128], AF.Exp,
                                              bias=sm[:, 112 + hgl:113 + hgl]),
                                 reads=[B["pC%d" % hf], bsm], writes=[B["Lt"][gq]])
                    P.op("dve", I_tt(Mt[:, :, :], Lt[:, :, :], bc3(CBs_t[q][:, g * 128:(g + 1) * 128], 8, 1), ALU.mult),
                         reads=[B["Lt"][gq], B["CBs"][q]], writes=[B["Mt"][gq]])
                    for hh in range(8):
                        hgl = g * 8 + hh
                        P.op("pe", I_mm(pB[:, g, hh * 64:(hh + 1) * 64], Mt[:, hh, :], xs[:, hgl * 64:(hgl + 1) * 64], hh == 0, False),
                             reads=[B["Mt"][gq], B["xs"][q]], writes=[B["pB%d" % g]])
                    P.op("pe", I_mm(pB[:, g, :], C.ident[:], xsD[:, g * 512:(g + 1) * 512], False, True),
                         reads=[B["xsD"][q], C.b_ident], writes=[B["pB%d" % g]])
                if part == 0:
                    return
                yt3 = yt[:, :].rearrange("p (h e) -> p h e", h=16)
                P.op("dve", I_tt(yt3, pA[:, :, :].rearrange("p a (h e) -> p (a h) e", e=64), bc3(sm[:, 80:96], 64, 2), ALU.mult),
                     reads=[B["pA0"], B["pA1"], bsm], writes=[B["yt"][q]])
                P.op("dve", I_tt(yt[:], yt[:], pB_flat, ALU.add), reads=[B["yt"][q], B["pB0"], B["pB1"]], writes=[B["yt"][q]])
                for g in range(2):
                    P.op("pe", I_mm(pC[:, g, :], Btm[q][:, g * 128:(g + 1) * 128], xdtd[:, g * 512:(g + 1) * 512], True, True),
                         reads=[B["Btm"][q], B["xdtd"][q]], writes=[B["pC%d" % g]])
                S3 = S32[:, :].rearrange("p (h e) -> p h e", h=16)
                P.op("dve", I_tt(S3, S3, bc3(sm[:, 96:112], 64, 2), ALU.mult), reads=[B["S32"], bsm], writes=[B["S32"]])
                P.op("dve", I_tt(S32[:], S32[:], pC[:, :, :].rearrange("p a b -> p (a b)"), ALU.add),
                     reads=[B["S32"], B["pC0"], B["pC1"]], writes=[B["S32"]])
                P.op("act", I_acopy(Sbf[:], S32[:]), reads=[B["S32"]], writes=[B["Sbf"]])

            def ssd_C(t):
                q, tsl, xs, sm, xdt, xdtd, xsD, yt, bsm, dtc = ssd_vars(t)
                for c in range(8):
                    P.op("pe", I_tr(p7[:, c, :], szT[:, c, tsl], C.ident[:]), reads=[B["szT"][c], C.b_ident], writes=[B["p7"]])
                P.op("dve", I_tt(yt[:], yt[:], p7[:, :, :].rearrange("p c e -> p (c e)"), ALU.mult),
                     reads=[B["yt"][q], B["p7"]], writes=[B["yt"][q]])
                P.op("act", I_act(yb_t[q][:], yt[:], AF.Square, accum_out=stat[:, 4 + q:5 + q]),
                     reads=[B["yt"][q]], writes=[B["yb"][q], B["stat"]])
                rstd_pool(P, C, stat[:, 6 + q:7 + q], stat[:, 4 + q:5 + q], 0.25 / D, [B["stat"]], [B["stat"]], post=0.5)
                P.op("dve", I_stt(yb_t[q][:], yt[:], stat[:, 6 + q:7 + q], ngb[:], ALU.mult, ALU.mult),
                     reads=[B["yt"][q], B["stat"], B["g"]], writes=[B["yb"][q]])
                for c in range(8):
                    P.op("pe", I_tr(p7[:, c, :], yb_t[q][:, c * 128:(c + 1) * 128], C.ident[:]),
                         reads=[B["yb"][q], C.b_ident], writes=[B["p7"]])
                P.op("act", I_acopy(ybT_t[q][:, :, :], p7[:, :, :]), reads=[B["p7"]], writes=[B["ybT"][q]])
                P.dma("sp", I_dma(ysc[blk * 4 + t, :, 8:16, :], ybT_t[q][:, :, :]), reads=[B["ybT"][q]])
                prefetch_compute()


            ssd_A(0)
            for t in range(4):
                ssd_B(t, 0)
                if t + 1 < 4:
                    ssd_A(t + 1)
                ssd_B(t, 1)
                ssd_C(t)
            P.op("pool", I_copy(xbcT[:, :, 0:3], xbcT[:, :, TBK:TBK + 3]), reads=B["xbcT"], writes=B["xbcT"])
        P.barrier()
    with ExitStack() as st:
        Wout = sb(st, nc, tag + "Wout", [128, 16, D], BF16)
        gpost = sb(st, nc, tag + "gpost", [128, D], F32)
        yin = [sb(st, nc, tag + "yin%d" % i, [128, 16, 128], BF16) for i in range(3)]
        hres = [sb(st, nc, tag + "ohr%d" % i, [128, D], F32) for i in range(2)]
        mb = [sb(st, nc, tag + "omb%d" % i, [128, D], F32) for i in range(4)]
        junk = sb(st, nc, tag + "ojunk", [128, 512], BF16)
        stat = sb(st, nc, tag + "ostat", [128, 24], F32)
        ops_ = [ps(st, nc, tag + "oops%d" % i, [128, 512], F32) for i in range(4)]
        b_Wo, b_g, b_junk = P.buf("Wout"), P.buf("gpo"), P.buf("ojunk")
        b_yin = P.bufs("yin", 3)
        b_hres = P.bufs("ohr", 2)
        b_mb = P.bufs("omb", 4)
        b_st = P.bufs("ost", 4)
        b_ops = P.bufs("oops", 4)
        P.dma("pool", I_dma(Wout[:, :, :], w_out.rearrange("(c p) n -> p c n", p=128)), writes=[b_Wo])
        P.dma("sp", I_dma(gpost[:], g_post.partition_broadcast(128)), writes=[b_g])
        ntile = nblk * 4

        def o0(t):
            P.dma("sp", I_dma(yin[t % 3][:, :, :], ysc[t]), writes=[b_yin[t % 3]])

        def o1(t):
            pp = (t % 2) * 2
            for dh in range(2):
                for c in range(16):
                    P.op("pe", I_mm(ops_[pp + dh][:], yin[t % 3][:, c, :], Wout[:, c, dh * 512:(dh + 1) * 512], c == 0, c == 15),
                         reads=[b_yin[t % 3], b_Wo], writes=[b_ops[pp + dh]])

        def o2(t):
            i = t % 4
            pp = (t % 2) * 2
            post_part1(P, ops_[pp:pp + 2], b_ops[pp:pp + 2], mb, b_mb, junk, b_junk, stat, b_st, i)

        def o3(t):
            i = t % 4
            post_part2(P, C, mb, b_mb, hres, b_hres, stat, b_st, gpost, b_g, h, t * 128, i, 1.0, add_eng="dma")

        sw_pipeline(ntile, [o0, o1, o2, o3])
        P.barrier()


def copy_phase(P, C, x, out):
    nc = P.nc
    with ExitStack() as st:
        t = [sb(st, nc, "cp%d" % i, [128, D], F32) for i in range(2)]
        b = P.bufs("cp", 2)
        for k in range(NT):
            i = k % 2
            P.dma("sp", I_dma(t[i][:], x[k * 128:(k + 1) * 128, :]), writes=[b[i]])
            P.dma("sp", I_dma(out[k * 128:(k + 1) * 128, :], t[i][:]), reads=[b[i]])
        P.barrier()


def build(phases, debug=None):
    nc = bass.Bass("TRN2", target_bir_lowering=False)
    x = nc.dram_tensor("x", [L, D], F32, kind="ExternalInput").ap()
    norm_g = nc.dram_tensor("norm_g", [2, 6, D], F32, kind="ExternalInput").ap()
    ffn_w1 = nc.dram_tensor("ffn_w1", [2, 2, D, 2 * DFF], F32, kind="ExternalInput").ap()
    ffn_w2 = nc.dram_tensor("ffn_w2", [2, 2, DFF, D], F32, kind="ExternalInput").ap()
    hy = {}
    for nm, shp in (("hyb_w_in", [1, D, 4624]), ("conv_dw_w", [1, 31, D]), ("conv_dw_b", [1, D]), ("conv_ln_g", [1, D]),
                    ("conv_ln_b", [1, D]), ("ssm_conv_w", [1, 4, 1536]), ("ssm_conv_b", [1, 1536]), ("ssm_dt_bias", [1, 16]),
                    ("ssm_a_log", [1, 16]), ("ssm_d", [1, 16]), ("ssm_norm_g", [1, D]), ("hyb_w_out", [1, 2048, D])):
        hy[nm] = nc.dram_tensor(nm, shp, F32, kind="ExternalInput").ap()
    attn_w_qkv = nc.dram_tensor("attn_w_qkv", [1, D, 4608], F32, kind="ExternalInput").ap()
    attn_w_o = nc.dram_tensor("attn_w_o", [1, 512, D], F32, kind="ExternalInput").ap()
    out = nc.dram_tensor("out", [L, D], F32, kind="ExternalOutput").ap()
    acc = nc.dram_tensor("attn_acc", [3, L, 520], F32).ap()
    ysc = nc.dram_tensor("hyb_ysc", [NT, 128, 16, 128], BF16).ap()
    ubs = nc.dram_tensor("attn_ubf", [L, D], BF16).ap()
    P = Prog(nc)
    C = Ctx()
    with ExitStack() as stack:
        emit_consts(P, C, stack)
        P.barrier()
        for ph in phases:
            if ph[0] == "ffn":
                _, li, fi, src_is_x, nblk = ph
                ffn_phase(P, C, x if src_is_x else out, out, ffn_w1[li, fi], ffn_w2[li, fi],
                          norm_g[li, 0 if fi == 0 else 4], norm_g[li, 1 if fi == 0 else 5],
                          "f%d%d" % (li, fi), nblk=nblk, dbg=debug)
            elif ph[0] == "ffnc":
                jobs = []
                for (li, fi, src_is_x) in ph[1]:
                    jobs.append(dict(h_src=(x if src_is_x else out), h_dst=out, w1=ffn_w1[li, fi], w2=ffn_w2[li, fi],
                                     g_pre=norm_g[li, 0 if fi == 0 else 4], g_post=norm_g[li, 1 if fi == 0 else 5]))
                ffn_chain(P, C, jobs, "fc%d" % len(jobs) + "".join("%d%d" % (a, b_) for (a, b_, _) in ph[1]))
            elif ph[0] == "copy":
                copy_phase(P, C, x, out)
            elif ph[0] == "hyb":
                hyb_phase(P, C, out, hy["hyb_w_in"][0], hy["conv_dw_w"][0], hy["conv_dw_b"][0], hy["conv_ln_g"][0],
                          hy["conv_ln_b"][0], hy["ssm_conv_w"][0], hy["ssm_conv_b"][0], hy["ssm_dt_bias"][0],
                          hy["ssm_a_log"][0], hy["ssm_d"][0], hy["ssm_norm_g"][0], hy["hyb_w_out"][0],
                          norm_g[0, 2], norm_g[0, 3], ysc, **ph[1])
            elif ph[0] == "attn":
                attn_phase(P, C, out, attn_w_qkv[0], attn_w_o[0], norm_g[1, 2], norm_g[1, 3], acc, ubs, **ph[1])
        P.emit(stack)
    return nc, P


FULL_PHASES = [("ffnc", [(0, 0, True)]), ("hyb", {}), ("ffnc", [(0, 1, False), (1, 0, False)]),
               ("attn", {}), ("ffnc", [(1, 1, False)])]
_W_NAMES = ["norm_g", "ffn_w1", "ffn_w2", "hyb_w_in", "conv_dw_w", "conv_dw_b", "conv_ln_g", "conv_ln_b", "ssm_conv_w",
            "ssm_conv_b", "ssm_dt_bias", "ssm_a_log", "ssm_d", "ssm_norm_g", "hyb_w_out", "attn_w_qkv", "attn_w_o"]


def kernel(**inputs):
    x = np.ascontiguousarray(np.asarray(inputs["x"], dtype=np.float32))
    n = x.shape[0]
    nc, _ = build(FULL_PHASES)
    shared = {k: np.ascontiguousarray(np.asarray(inputs[k], dtype=np.float32)) for k in _W_NAMES}
    in_maps = []
    for i in range(n):
        m = dict(shared)
        m["x"] = x[i]
        in_maps.append(m)
    res = run_bass_kernel_spmd(nc, in_maps, core_ids=list(range(n)))
    return np.stack([np.asarray(r["out"]) for r in res.results], axis=0).astype(np.float32)
```
